# Optimizing a Trainium2 kernel written in Bass

```python
import math
import jax, jax.numpy as jnp
from jax import lax
import numpy as np

D_MODEL = 1024
BATCH = 16
SEQ = 2048
DEPTH = 1

ATTN_HEADS = 8
KV_HEADS = 2
Q_PER_KV = ATTN_HEADS // KV_HEADS
HEAD_DIM = 64
D_ATTN = ATTN_HEADS * HEAD_DIM
D_KV = KV_HEADS * HEAD_DIM
WINDOW = 128
BLOCK = 128
NUM_BUCKETS = 32
MAX_DISTANCE = 128
D_SSM = 512
SSM_GROUP = 16
N_SSM_GROUPS = D_SSM // SSM_GROUP
SSM_STATE = 64
N_DIRS = 2
N_BRANCHES = 2
EPS = 1e-6
NEG_INF = -1e30
D_IN = D_ATTN + 2 * D_KV + D_ATTN + 2 * D_SSM + N_BRANCHES * D_MODEL
SPLITS = (D_ATTN,
          D_ATTN + D_KV,
          D_ATTN + 2 * D_KV,
          2 * D_ATTN + 2 * D_KV,
          2 * D_ATTN + 2 * D_KV + D_SSM,
          2 * D_ATTN + 2 * D_KV + 2 * D_SSM)

kernel_name = "hybrid_gated_swa_s5_block"


def rms_norm(x, g):
    x32 = x.astype(jnp.float32)
    y = x32 * lax.rsqrt(jnp.mean(x32 * x32, axis=-1, keepdims=True) + EPS)
    return (y * g.astype(jnp.float32)).astype(x.dtype)


def t5_bucket(rel):
    half = NUM_BUCKETS // 2
    ret = (rel > 0).astype(jnp.int32) * half
    n = jnp.abs(rel)
    max_exact = half // 2
    nf = jnp.maximum(n, 1).astype(jnp.float32)
    large = max_exact + (jnp.log(nf / max_exact) / math.log(MAX_DISTANCE / max_exact)
                         * (half - max_exact)).astype(jnp.int32)
    large = jnp.minimum(large, half - 1)
    return ret + jnp.where(n < max_exact, n, large)


def band_windows(t, nb):
    b = t.shape[0]
    tp = jnp.pad(t, ((0, 0), (BLOCK, BLOCK), (0, 0), (0, 0)))
    tb = tp.reshape(b, nb + 2, BLOCK, KV_HEADS, HEAD_DIM)
    return jnp.concatenate([tb[:, :-2], tb[:, 1:-1], tb[:, 2:]], axis=2)


def windowed_gqa(q, k, v, sink, rel_table):
    b, s, _ = q.shape
    nb = s // BLOCK
    q = q.reshape(b, nb, BLOCK, KV_HEADS, Q_PER_KV, HEAD_DIM)
    kw = band_windows(k.reshape(b, s, KV_HEADS, HEAD_DIM), nb)
    vw = band_windows(v.reshape(b, s, KV_HEADS, HEAD_DIM), nb)
    scores = jnp.einsum('bnqhgd,bnkhd->bnhgqk', q, kw).astype(jnp.float32) * (HEAD_DIM ** -0.5)
    rel = (jnp.arange(3 * BLOCK)[None, :] - BLOCK) - jnp.arange(BLOCK)[:, None]
    bias = rel_table.astype(jnp.float32)[t5_bucket(rel)]
    bias = jnp.transpose(bias, (2, 0, 1)).reshape(KV_HEADS, Q_PER_KV, BLOCK, 3 * BLOCK)
    kpos = jnp.arange(nb)[:, None] * BLOCK - BLOCK + jnp.arange(3 * BLOCK)[None, :]
    valid = (jnp.abs(rel) <= WINDOW)[None] & ((kpos >= 0) & (kpos < s))[:, None, :]
    scores = jnp.where(valid[None, :, None, None], scores + bias, NEG_INF)
    sk = sink.astype(jnp.float32).reshape(KV_HEADS, Q_PER_KV)[None, None, :, :, None, None]
    m = jnp.maximum(scores.max(axis=-1, keepdims=True), sk)
    p = jnp.exp(scores - m)
    probs = p / (p.sum(axis=-1, keepdims=True) + jnp.exp(sk - m))
    o = jnp.einsum('bnhgqk,bnkhd->bnqhgd', probs.astype(v.dtype), vw)
    return o.reshape(b, s, D_ATTN)


def _scan_op(e1, e2):
    a1, b1 = e1
    a2, b2 = e2
    return a1 * a2, a2 * b1 + b2


def s5_bidirectional(u, a_re, a_im, log_dt, b_re, b_im, c_re, c_im, d_skip):
    bsz, s, _ = u.shape
    f32 = jnp.float32
    ut = jnp.swapaxes(u.astype(f32).reshape(bsz, s, N_SSM_GROUPS, SSM_GROUP), 0, 1)
    y = ut * d_skip.astype(f32).reshape(N_SSM_GROUPS, SSM_GROUP)
    for d in range(N_DIRS):
        lam = lax.complex(a_re[d].astype(f32), a_im[d].astype(f32))
        dt = jnp.exp(log_dt[d].astype(f32))[:, None]
        a_bar = jnp.exp(lam * dt)
        coef = (a_bar - 1.0) / lam
        b_bar = lax.complex(b_re[d].astype(f32), b_im[d].astype(f32)) * coef[..., None]
        bu = lax.complex(jnp.einsum('sbgc,gpc->sbgp', ut, jnp.real(b_bar)),
                         jnp.einsum('sbgc,gpc->sbgp', ut, jnp.imag(b_bar)))
        a_el = jnp.broadcast_to(a_bar[None, None], (s, 1, N_SSM_GROUPS, SSM_STATE))
        _, xs = lax.associative_scan(_scan_op, (a_el, bu), reverse=(d == 1), axis=0)
        y = y + jnp.einsum('sbgp,gcp->sbgc', jnp.real(xs), c_re[d].astype(f32)) \
              - jnp.einsum('sbgp,gcp->sbgc', jnp.imag(xs), c_im[d].astype(f32))
    return jnp.swapaxes(y, 0, 1).reshape(bsz, s, D_SSM)


def setup_inputs(seed: int = 0) -> dict:
    key = jax.random.key(seed)
    ks = jax.random.split(key, 24)
    nrm = lambda k, shape, scale: jax.random.normal(k, shape, jnp.float32) * scale
    L, G, P, C = DEPTH, N_SSM_GROUPS, SSM_STATE, SSM_GROUP
    a_im_init = math.pi * jnp.arange(P, dtype=jnp.float32)
    return {
        "x": nrm(ks[0], (BATCH, SEQ, D_MODEL), 1.0),
        "norm_gain": 1.0 + nrm(ks[1], (L, D_MODEL), 0.02),
        "w_in": nrm(ks[2], (L, D_MODEL, D_IN), D_MODEL ** -0.5),
        "b_gate": nrm(ks[3], (L, N_BRANCHES * D_MODEL), 0.02),
        "attn_sink": nrm(ks[4], (L, ATTN_HEADS), 0.5),
        "rel_bias_table": nrm(ks[5], (NUM_BUCKETS, ATTN_HEADS), 0.5),
        "ssm_a_re": -0.5 + nrm(ks[6], (L, N_DIRS, G, P), 0.01),
        "ssm_a_im": a_im_init + nrm(ks[7], (L, N_DIRS, G, P), 0.01),
        "ssm_log_dt": jax.random.uniform(ks[8], (L, N_DIRS, G), jnp.float32,
                                         math.log(1e-3), math.log(1e-1)),
        "ssm_b_re": nrm(ks[9], (L, N_DIRS, G, P, C), (2 * C) ** -0.5),
        "ssm_b_im": nrm(ks[10], (L, N_DIRS, G, P, C), (2 * C) ** -0.5),
        "ssm_c_re": nrm(ks[11], (L, N_DIRS, G, C, P), (2 * P) ** -0.5),
        "ssm_c_im": nrm(ks[12], (L, N_DIRS, G, C, P), (2 * P) ** -0.5),
        "ssm_d": nrm(ks[13], (L, D_SSM), 1.0),
        "w_glu": nrm(ks[14], (L, D_SSM, D_SSM), D_SSM ** -0.5),
        "b_glu": nrm(ks[15], (L, D_SSM), 0.02),
        "w_branch_attn": nrm(ks[16], (L, D_ATTN, D_MODEL), D_ATTN ** -0.5),
        "w_branch_ssm": nrm(ks[17], (L, D_SSM, D_MODEL), D_SSM ** -0.5),
        "w_out": nrm(ks[18], (L, D_MODEL, D_MODEL), D_MODEL ** -0.5),
        "final_norm_gain": 1.0 + nrm(ks[19], (D_MODEL,), 0.02),
    }


def reference(x, norm_gain, w_in, b_gate, attn_sink, rel_bias_table, ssm_a_re, ssm_a_im,
              ssm_log_dt, ssm_b_re, ssm_b_im, ssm_c_re, ssm_c_im, ssm_d, w_glu, b_glu,
              w_branch_attn, w_branch_ssm, w_out, final_norm_gain):
    for l in range(DEPTH):
        h = rms_norm(x, norm_gain[l])
        proj = jnp.einsum('bsd,de->bse', h, w_in[l])
        q, k, v, z_attn, u_ssm, z_ssm, g = jnp.split(proj, SPLITS, axis=-1)
        g_attn, g_ssm = jnp.split(g + b_gate[l], N_BRANCHES, axis=-1)
        attn = windowed_gqa(q, k, v, attn_sink[l], rel_bias_table) * jax.nn.silu(z_attn)
        y = s5_bidirectional(u_ssm, ssm_a_re[l], ssm_a_im[l], ssm_log_dt[l], ssm_b_re[l],
                             ssm_b_im[l], ssm_c_re[l], ssm_c_im[l], ssm_d[l]).astype(x.dtype)
        y = jax.nn.gelu(y)
        y = y * jax.nn.sigmoid(jnp.einsum('bsc,ce->bse', y, w_glu[l]) + b_glu[l])
        ssm = y * jax.nn.silu(z_ssm)
        merged = jax.nn.sigmoid(g_attn) * jnp.einsum('bsc,cd->bsd', attn, w_branch_attn[l]) \
               + jax.nn.sigmoid(g_ssm) * jnp.einsum('bsc,cd->bsd', ssm, w_branch_ssm[l])
        x = x + jnp.einsum('bsd,de->bse', merged, w_out[l])
    return rms_norm(x, final_norm_gain)
```

```python
import math
from contextlib import ExitStack
import numpy as np
import concourse.bass as bass
import concourse.mybir as mybir
from concourse.bass_utils import run_bass_kernel_spmd

F32 = mybir.dt.float32
BF16 = mybir.dt.bfloat16
I32 = mybir.dt.int32
ALU = mybir.AluOpType
AF = mybir.ActivationFunctionType
ENGS = ("pe", "act", "dve", "pool", "sp")
STRICT_SAME_ENGINE = True
EPS = 1e-6
TWO_PI = 2.0 * math.pi


def _prod(xs):
    r = 1
    for v in xs:
        r *= int(v)
    return r


class Sched:
    N_DMA_SEMS = 40

    def __init__(self, nc):
        self.nc = nc
        self.ops = []
        self.last_w = {}
        self.readers = {}
        self.guards = {}

    def guard(self, new_prefix, old_prefixes):
        s = set()
        for k, w in self.last_w.items():
            if k.split(":")[0] in old_prefixes and w is not None:
                s.add(w)
        for k, rs in self.readers.items():
            if k.split(":")[0] in old_prefixes:
                s.update(rs)
        best = {}
        out = set()
        for i in s:
            o = self.ops[i]
            if o["dma"]:
                out.add(i)
            else:
                best[o["eng"]] = max(best.get(o["eng"], -1), i)
        out.update(best.values())
        self.guards.setdefault(new_prefix, set()).update(out)

    def add(self, eng, fn, reads=(), writes=(), dma=False):
        idx = len(self.ops)
        psr = [k for k in reads if k.startswith("ps")]
        writes = list(writes) + [k for k in psr if k not in writes]
        deps = set()
        raw = set()
        for k in list(reads) + list(writes):
            g = self.guards.get(k.split(":")[0])
            if g:
                deps |= g
                raw |= g
        for k in reads:
            w = self.last_w.get(k)
            if w is not None:
                deps.add(w)
                raw.add(w)
        for k in writes:
            w = self.last_w.get(k)
            if w is not None:
                deps.add(w)
            for r in self.readers.get(k, ()):
                deps.add(r)
        deps.discard(idx)
        for k in reads:
            self.readers.setdefault(k, []).append(idx)
        for k in writes:
            self.last_w[k] = idx
            self.readers[k] = []
        fdeps = []
        for d in deps:
            p = self.ops[d]
            if (not p["dma"]) and (not dma) and p["eng"] == eng:
                if eng == "pe" or (d not in raw and not STRICT_SAME_ENGINE):
                    continue
            fdeps.append(d)
        self.ops.append(dict(eng=eng, fn=fn, deps=sorted(fdeps), dma=dma, idx=idx))
        return idx

    def emit(self):
        nc = self.nc
        ops = self.ops
        needed = set()
        for o in ops:
            needed.update(o["deps"])
        cnt = {e: 0 for e in ENGS}
        for o in ops:
            if (not o["dma"]) and o["idx"] in needed:
                cnt[o["eng"]] += 1
                o["ticket"] = cnt[o["eng"]]
        dma_ops = [o for o in ops if o["dma"]]
        pools = {"sp": (0, 28), "pool": (28, 12), "act": (40, 0)}
        nd = 40
        sem_val = [0] * nd
        qcnt = {"sp": 0, "pool": 0}
        for o in dma_ops:
            base, n = pools[o["eng"]]
            s = base + qcnt[o["eng"]] % n
            qcnt[o["eng"]] += 1
            o["dsem"] = s
            o["dprev"] = sem_val[s]
            sem_val[s] += 16
            o["dval"] = sem_val[s]
        with ExitStack() as st:
            esem = {e: st.enter_context(nc.semaphore("sem_" + e)) for e in ENGS}
            dsem = [st.enter_context(nc.semaphore("dsem%d" % i)) for i in range(nd)]
            block = st.enter_context(nc.Block())
            streams = {e: [o for o in ops if o["eng"] == e] for e in ENGS}

            def make_body(e):
                def body(eng):
                    waited = {f: 0 for f in ENGS}
                    dwaited = [0] * nd
                    for o in streams[e]:
                        for d in o["deps"]:
                            p = ops[d]
                            if p["dma"]:
                                s = p["dsem"]
                                if dwaited[s] < p["dval"]:
                                    eng.wait_ge(dsem[s], p["dval"])
                                    dwaited[s] = p["dval"]
                            else:
                                f = p["eng"]
                                if waited[f] < p["ticket"]:
                                    eng.wait_ge(esem[f], p["ticket"])
                                    waited[f] = p["ticket"]
                        if o["dma"]:
                            s = o["dsem"]
                            if o["dprev"] > 0 and dwaited[s] < o["dprev"]:
                                eng.wait_ge(dsem[s], o["dprev"])
                                dwaited[s] = o["dprev"]
                            ins = o["fn"](eng)
                            ins.then_inc(dsem[s], 16)
                        else:
                            ins = o["fn"](eng)
                            if "ticket" in o:
                                ins.then_inc(esem[e], 1)
                return body

            block.tensor(make_body("pe"))
            block.scalar(make_body("act"))
            block.vector(make_body("dve"))
            block.gpsimd(make_body("pool"))
            block.sync(make_body("sp"))


class Arena:
    def __init__(self, nc, S, nbytes):
        self.t = nc.alloc_sbuf_tensor("arena", [128, nbytes // 2], BF16)
        self.S = S
        self.nbytes = nbytes
        self.allocs = []

    def view(self, prefix, off, shape, dt):
        esz = 2 if dt == BF16 else 4
        nb = _prod(shape[1:]) * esz
        assert off % 4 == 0 and off + nb <= self.nbytes, (prefix, off, nb)
        olds = set(p for (p, s, e) in self.allocs if p != prefix and s < off + nb and off < e)
        if olds:
            self.S.guard(prefix, olds)
        self.allocs.append((prefix, off, off + nb))
        ap = self.t[:, off // 2: off // 2 + nb // 2]
        if dt != BF16:
            ap = ap.bitcast(dt)
        if len(shape) == 3:
            ap = ap.rearrange("p (a b) -> p a b", a=shape[1])
        elif len(shape) == 4:
            ap = ap.rearrange("p (a b c) -> p a b c", a=shape[1], b=shape[2])
        elif len(shape) == 5:
            ap = ap.rearrange("p (a b c d) -> p a b c d", a=shape[1], b=shape[2], c=shape[3])
        if shape[0] < 128:
            ap = ap[0:shape[0]]
        return ap


def build_program(dbg=(), phase_limit=99):
    nc = bass.Bass("TRN2", target_bir_lowering=False)
    S = Sched(nc)
    _phase = [0]
    _orig_add = S.add

    _pending = []
    _defer = [False]

    def _add(eng, fn, reads=(), writes=(), dma=False):
        if _phase[0] > phase_limit:
            return None
        if _defer[0]:
            _pending.append((eng, fn, list(reads), list(writes), dma))
            return None
        return _orig_add(eng, fn, reads, writes, dma)
    S.add = _add

    def flush(n=None):
        k = len(_pending) if n is None else min(n, len(_pending))
        for _ in range(k):
            _orig_add(*_pending.pop(0))

    def PHASE(n):
        _phase[0] = n

    def din(name, shape):
        return nc.dram_tensor(name, shape, F32, kind="ExternalInput").ap()

    x = din("x", [4096, 1024])
    norm_gain = din("norm_gain", [1, 1024])
    w_in = din("w_in", [1024, 4352])
    b_gate = din("b_gate", [1, 2048])
    attn_sink = din("attn_sink", [1, 8])
    rel_tab = din("rel_bias_table", [32, 8])
    a_re = din("ssm_a_re", [2, 32, 64])
    a_im = din("ssm_a_im", [2, 32, 64])
    log_dt = din("ssm_log_dt", [2, 32])
    b_re = din("ssm_b_re", [2, 32, 64, 16])
    b_im = din("ssm_b_im", [2, 32, 64, 16])
    c_re = din("ssm_c_re", [2, 32, 16, 64])
    c_im = din("ssm_c_im", [2, 32, 16, 64])
    ssm_d = din("ssm_d", [1, 512])
    w_glu = din("w_glu", [512, 512])
    b_glu = din("b_glu", [1, 512])
    w_ba = din("w_branch_attn", [512, 1024])
    w_bs = din("w_branch_ssm", [512, 1024])
    w_out = din("w_out", [1024, 1024])
    fgain = din("final_norm_gain", [1, 1024])
    c_ident = din("c_ident", [128, 128])
    c_anti = din("c_anti", [128, 128])
    c_sval = din("c_sval", [128, 32])
    c_sig = din("c_sig", [128, 1])
    c_mf = din("c_mf", [128, 128])
    c_mb = din("c_mb", [128, 128])
    c_oh = din("c_oh", [33, 512])
    out = nc.dram_tensor("out", [4096, 1024], F32, kind="ExternalOutput").ap()
    fd_t = nc.dram_tensor("fd_scratch", [8, 512], F32, kind="Internal")
    fd = fd_t.ap()
    lsc = nc.dram_tensor("l_scratch", [32, 128, 1024], BF16, kind="Internal").ap()
    dbg_out = {}

    AR = Arena(nc, S, 212736)
    K = 1024
    ps = [nc.alloc_psum_tensor("ps%d" % i, [128, 512], F32) for i in range(8)]

    _bank = [0]
    _bank_mod = [8]

    def nb():
        i = _bank[0] % _bank_mod[0]
        _bank[0] += 1
        return i

    def psf(i):
        return ps[i][:]

    def psb(i):
        return ps[i][:].bitcast(BF16)

    def DMA(q, o, i, reads, writes, slow=False):
        if slow:
            S.add(q, lambda e: e.dma_start(out=o, in_=i, allow_slow_non_contiguous=True), reads, writes, dma=True)
        else:
            S.add(q, lambda e: e.dma_start(out=o, in_=i), reads, writes, dma=True)

    def ACT(o, i, func, reads, writes, **kw):
        S.add("act", lambda e: e.activation(out=o, in_=i, func=func, **kw), reads, writes)

    def TT(eng, o, a, b, op, reads, writes):
        S.add(eng, lambda e: e.tensor_tensor(out=o, in0=a, in1=b, op=op), reads, writes)

    def TS(eng, o, a, s1, s2, op0, op1, reads, writes):
        if op1 is None:
            S.add(eng, lambda e: e.tensor_scalar(out=o, in0=a, scalar1=s1, scalar2=None, op0=op0), reads, writes)
        else:
            S.add(eng, lambda e: e.tensor_scalar(out=o, in0=a, scalar1=s1, scalar2=s2, op0=op0, op1=op1), reads, writes)

    def STT(o, a, sc, b, op0, op1, reads, writes):
        S.add("dve", lambda e: e.scalar_tensor_tensor(out=o, in0=a, scalar=sc, in1=b, op0=op0, op1=op1), reads, writes)

    def CP(eng, o, i, reads, writes):
        if eng == "act":
            ACT(o, i, AF.Copy, reads, writes)
        else:
            S.add(eng, lambda e: e.tensor_copy(out=o, in_=i), reads, writes)

    def MM(lst, reads, writes):
        def fn(e):
            ins = None
            for (o, l, r, st, sp) in lst:
                ins = e.matmul(o, lhsT=l, rhs=r, start=st, stop=sp)
            return ins
        S.add("pe", fn, reads, writes)

    def TRS(lst, reads, writes):
        def fn(e):
            ins = None
            for (o, i, idn) in lst:
                ins = e.transpose(o, in_=i, identity=idn)
            return ins
        S.add("pe", fn, reads, writes)

    def dump(name, ap, shape, dt, reads):
        if name not in dbg:
            return
        t = nc.dram_tensor("dbg_" + name, shape, dt, kind="ExternalOutput").ap()
        dbg_out[name] = t
        DMA("sp", t, ap, reads, ["dbgout:" + name])

    ident_f = AR.view("cst", 0, [128, 128], F32)
    ident_b = AR.view("cst", 512, [128, 128], BF16)
    mf = AR.view("cst", 768, [128, 128], F32)
    mb = AR.view("cst", 1280, [128, 128], F32)
    sval = AR.view("cst", 1792, [128, 32], F32)
    sig = AR.view("cst", 1920, [128, 1], F32)
    epst = AR.view("cst", 1924, [128, 1], F32)
    gainT = AR.view("cst", 1928, [128, 8], F32)
    esink = AR.view("cst", 1960, [128, 8], F32)
    bgT = AR.view("cst", 1992, [128, 16], F32)
    bgluT = AR.view("cst", 2056, [128, 4], F32)
    ss_t = AR.view("cst", 2072, [128, 8], F32)
    anti = AR.view("cst", 2176, [128, 128], F32)
    tab33 = AR.view("cst", 2688, [33, 8], F32)
    fsb = AR.view("cst", 2720, [8, 512], F32)
    CST_END = 5 * K

    DMA("sp", ident_f, c_ident, [], ["cst:ident_f"])
    DMA("sp", anti, c_anti, [], ["cst:anti"])
    DMA("sp", mf, c_mf, [], ["cst:mf"])
    DMA("sp", mb, c_mb, [], ["cst:mb"])
    DMA("sp", sval, c_sval, [], ["cst:sval"])
    DMA("sp", sig, c_sig, [], ["cst:sig"])
    CP("dve", ident_b, ident_f, ["cst:ident_f"], ["cst:ident_b"])
    S.add("dve", lambda e: e.memset(epst, EPS), [], ["cst:eps"])
    vecraw = AR.view("raw", 5 * K + 8 * K + 8704 + 6 * K + 32 * K + 14 * K, [32, 128], F32)
    S.add("dve", lambda e: e.memset(vecraw, 0.0), [], ["raw:vr"])
    DMA("sp", vecraw[0:8, :], norm_gain[0].rearrange("(k p) -> k p", p=128), [], ["raw:vr"])
    DMA("sp", vecraw[8:24, :], b_gate[0].rearrange("(k p) -> k p", p=128), [], ["raw:vr"])
    DMA("sp", vecraw[24:28, :], b_glu[0].rearrange("(k p) -> k p", p=128), [], ["raw:vr"])
    TRS([(ps[7][:, 0:32], vecraw, ident_f[0:32, 0:32])], ["raw:vr", "cst:ident_f"], ["ps7"])
    CP("dve", gainT, ps[7][:, 0:8], ["ps7"], ["cst:gainT"])
    CP("dve", bgT, ps[7][:, 8:24], ["ps7"], ["cst:bgT"])
    CP("dve", bgluT, ps[7][:, 24:28], ["ps7"], ["cst:bgluT"])
    DMA("sp", esink, attn_sink[0:1, :].partition_broadcast(128), [], ["cst:esink"])
    ACT(esink, esink, AF.Exp, ["cst:esink"], ["cst:esink"])

    KT_OFF = CST_END
    VALL_OFF = KT_OFF + 8 * K
    BIAS_OFF = VALL_OFF + 8704
    RA_OFF = BIAS_OFF + 6 * K
    RB_OFF = RA_OFF + 32 * K
    RC_OFF = RB_OFF + 32 * K
    RD_OFF = RC_OFF + 32 * K
    assert RD_OFF % 4 == 0
    kT_all = AR.view("kT", KT_OFF, [128, 4096], BF16)
    v_all = AR.view("vall", VALL_OFF, [128, 32, 2, 65], BF16)
    biasT = AR.view("bias", BIAS_OFF, [128, 2, 3, 4, 128], BF16)

    w_in_v = w_in.rearrange("(k p) e -> p k e", p=128)

    o = RA_OFF
    Wu = AR.view("p1w", o, [128, 8, 512], BF16); o += 8 * K
    Wk = AR.view("p1w", o, [128, 8, 128], BF16); o += 2 * K
    Wv = AR.view("p1w", o, [128, 8, 128], BF16); o += 2 * K
    xs = [AR.view("p1x", o + 4 * K * i, [128, 1024], F32) for i in range(2)]; o += 8 * K
    hn = [AR.view("p1h", o + 2 * K * i, [128, 1024], BF16) for i in range(2)]; o += 4 * K
    hTs = [AR.view("p1t", o + 2 * K * i, [128, 8, 128], BF16) for i in range(2)]; o += 4 * K
    assert o <= RA_OFF + 32 * K
    U_tm = AR.view("utm", RC_OFF, [128, 32, 32, 16], BF16)
    VT_OFF = RB_OFF + 16 * K
    vT_all = AR.view("vT", VT_OFF, [128, 4096], BF16)

    DMA("pool", Wu, w_in_v[:, :, 1280:1792], [], ["p1w:u"])
    DMA("pool", Wk, w_in_v[:, :, 512:640], [], ["p1w:k"])
    DMA("pool", Wv, w_in_v[:, :, 640:768], [], ["p1w:v"])

    x_s = x.rearrange("(n s) d -> s n d", s=32)

    def rms_front(xt, xkey, hnt, hkey, col, reads_x):
        ssc = ss_t[:, col:col + 1]
        ACT(hnt, xt, AF.Square, reads_x, [hkey, "cst:ss%d" % col], accum_out=ssc)
        ACT(ssc, ssc, AF.Sqrt, ["cst:ss%d" % col, "cst:eps"], ["cst:ss%d" % col], scale=1.0 / 1024.0, bias=epst)
        S.add("dve", lambda e: e.reciprocal(out=ssc, in_=ssc), ["cst:ss%d" % col], ["cst:ss%d" % col])
        TS("dve", hnt, xt, ssc, None, ALU.mult, None, reads_x + ["cst:ss%d" % col], [hkey])

    _defer[0] = True
    PHASE(4)
    o = RD_OFF
    def rd(prefix, shape, dt):
        nonlocal o
        esz = 2 if dt == BF16 else 4
        v = AR.view(prefix, o, shape, dt)
        o += (_prod(shape[1:]) * esz + 3) // 4 * 4
        return v

    Pr = rd("tab", [128, 32, 32], F32)
    Pi = rd("tab", [128, 32, 32], F32)
    Qr = rd("tab", [128, 32, 32], F32)
    NQi = rd("tab", [128, 32, 32], F32)
    Bbr = rd("par", [128, 32, 16], F32)
    Bbi = rd("par", [128, 32, 16], F32)
    Cr = rd("par", [128, 32, 16], F32)
    Ci = rd("par", [128, 32, 16], F32)
    dB = rd("par", [128, 512], F32)
    sm = {}
    for nm in ("are", "aim", "dt", "rho", "th", "er", "cs", "sn", "abr", "abi", "den", "cfr", "cfi", "t1", "t2",
               "rhop", "thp", "a32r", "a32i", "e32", "kf"):
        sm[nm] = rd("par", [128, 32], F32)
    ki32 = rd("par", [128, 1024], I32)
    AAf = rd("par", [128, 2, 2, 64], F32)
    Wsc = rd("scn", [128, 2, 2, 64], F32)
    Ssc = rd("scn", [128, 2, 64], F32)
    Ssc2 = rd("scn", [128, 2, 64], F32)
    Lb = [rd("lt", [128, 2, 512], BF16) for _ in range(2)]
    Rb = [rd("rt", [128, 2, 512], BF16) for _ in range(2)]
    tmpD = [rd("tmpd", [128, 512], F32) for _ in range(2)]
    tmpP = [rd("tmpp", [128, 512], F32) for _ in range(2)]
    LTb = [rd("ltt", [128, 4, 2, 128], BF16) for _ in range(2)]
    Tg = [rd("tg", [128, 4, 512], BF16) for _ in range(2)]
    Dg = [rd("dg", [128, 128], BF16) for _ in range(2)]
    bt1 = [rd("bt", [128, 128], F32) for _ in range(2)]
    bt2 = [rd("bt", [128, 128], F32) for _ in range(2)]
    g_arg = rd("gen", [128, 32, 32], F32)
    g_phi = rd("gen", [128, 32, 32], F32)
    g_sn = rd("gen", [128, 32, 32], F32)
    g_cs = rd("gen", [128, 32, 32], F32)
    assert o <= AR.nbytes, o
    GEN_OFF = o - 16 * K
    Braw_r = AR.view("raw", RB_OFF, [128, 32, 16], F32)
    Braw_i = AR.view("raw", RB_OFF + 2 * K, [128, 32, 16], F32)
    Craw_r = AR.view("raw", RB_OFF + 4 * K, [128, 4, 128], F32)
    Craw_i = AR.view("raw", RB_OFF + 6 * K, [128, 4, 128], F32)
    Araw_r = AR.view("raw", RB_OFF + 8 * K, [32, 128], F32)
    Araw_i = AR.view("raw", RB_OFF + 8 * K + 512, [32, 128], F32)

    for d in range(2):
        DMA("sp", Braw_r[64 * d:64 * d + 64], b_re[d].rearrange("g p c -> p g c"), [], ["raw:br%d" % d])
        DMA("sp", Braw_i[64 * d:64 * d + 64], b_im[d].rearrange("g p c -> p g c"), [], ["raw:bi%d" % d])
        DMA("sp", Araw_r[:, 64 * d:64 * d + 64], a_re[d], [], ["raw:ar%d" % d])
        DMA("sp", Araw_i[:, 64 * d:64 * d + 64], a_im[d], [], ["raw:ai%d" % d])
        DMA("sp", sm["dt"][64 * d:64 * d + 64, :], log_dt[d:d + 1, :].partition_broadcast(64), [], ["par:dt%d" % d])
        for t in range(4):
            DMA("sp", Craw_r[:, t, 64 * d:64 * d + 64],
                c_re[d].rearrange("g c p -> (g c) p")[128 * t:128 * t + 128, :], [], ["raw:cr%d%d" % (d, t)])
            DMA("sp", Craw_i[:, t, 64 * d:64 * d + 64],
                c_im[d].rearrange("g c p -> (g c) p")[128 * t:128 * t + 128, :], [], ["raw:ci%d%d" % (d, t)])
    DMA("sp", dB, ssm_d[0:1, :].partition_broadcast(128), [], ["par:dB"])
    p6 = psf(6)
    TRS([(p6[:, 0:32], Araw_r, ident_f[0:32, 0:32]), (p6[:, 32:64], Araw_i, ident_f[0:32, 0:32])],
        ["raw:ar0", "raw:ar1", "raw:ai0", "raw:ai1", "cst:ident_f"], ["ps6"])
    CP("dve", sm["are"], p6[:, 0:32], ["ps6"], ["par:are"])
    CP("dve", sm["aim"], p6[:, 32:64], ["ps6"], ["par:aim"])
    p7 = psf(7)
    TRS([(p7[:, 128 * t:128 * t + 128], Craw_r[:, t, :], ident_f) for t in range(4)],
        ["raw:cr%d%d" % (d, t) for d in range(2) for t in range(4)] + ["cst:ident_f"], ["ps7"])
    CP("dve", Cr.rearrange("p g c -> p (g c)"), p7, ["ps7"], ["par:Cr"])
    TRS([(p6[:, 128 * t:128 * t + 128], Craw_i[:, t, :], ident_f) for t in range(4)],
        ["raw:ci%d%d" % (d, t) for d in range(2) for t in range(4)] + ["cst:ident_f"], ["ps6"])
    CP("dve", Ci.rearrange("p g c -> p (g c)"), p6, ["ps6"], ["par:Ci"])

    PK = ["par:small"]

    def sincos(phi, sn_o, cs_o, kf, n, keyr, keyw):
        ki = ki32[:, 0:n]
        for (dst, shift) in ((sn_o, 0.0), (cs_o, math.pi / 2)):
            TS("dve", ki, phi, 1.0 / TWO_PI, shift / TWO_PI, ALU.mult, ALU.add, keyr, keyw)
            CP("dve", kf, ki, keyw, keyw)
            STT(dst, kf, -TWO_PI, phi, ALU.mult, ALU.add, keyr + keyw, keyw)
            TS("dve", dst, dst, shift, None, ALU.add, None, keyw, keyw)
            TS("dve", dst, dst, 3.14159, -3.14159, ALU.min, ALU.max, keyw, keyw)
            ACT(dst, dst, AF.Sin, keyw, keyw)

    rk = ["par:are", "par:aim", "par:dt0", "par:dt1"]
    ACT(sm["dt"], sm["dt"], AF.Exp, ["par:dt0", "par:dt1"], PK)
    TT("dve", sm["rho"], sm["are"], sm["dt"], ALU.mult, rk + PK, PK)
    TT("dve", sm["th"], sm["aim"], sm["dt"], ALU.mult, rk + PK, PK)
    sincos(sm["th"], sm["sn"], sm["cs"], sm["kf"], 32, PK, PK)
    ACT(sm["er"], sm["rho"], AF.Exp, PK, PK)
    TT("dve", sm["abr"], sm["er"], sm["cs"], ALU.mult, PK, PK)
    TT("dve", sm["abi"], sm["er"], sm["sn"], ALU.mult, PK, PK)
    TS("dve", sm["abr"], sm["abr"], -1.0, None, ALU.add, None, PK, PK)
    TT("dve", sm["t1"], sm["are"], sm["are"], ALU.mult, rk + PK, PK)
    TT("dve", sm["t2"], sm["aim"], sm["aim"], ALU.mult, rk + PK, PK)
    TT("dve", sm["den"], sm["t1"], sm["t2"], ALU.add, PK, PK)
    S.add("dve", lambda e: e.reciprocal(out=sm["den"], in_=sm["den"]), PK, PK)
    TT("dve", sm["t1"], sm["abr"], sm["are"], ALU.mult, rk + PK, PK)
    TT("dve", sm["t2"], sm["abi"], sm["aim"], ALU.mult, rk + PK, PK)
    TT("dve", sm["t1"], sm["t1"], sm["t2"], ALU.add, PK, PK)
    TT("dve", sm["cfr"], sm["t1"], sm["den"], ALU.mult, PK, PK)
    TT("dve", sm["t1"], sm["abi"], sm["are"], ALU.mult, rk + PK, PK)
    TT("dve", sm["t2"], sm["abr"], sm["aim"], ALU.mult, rk + PK, PK)
    TT("dve", sm["t1"], sm["t1"], sm["t2"], ALU.subtract, PK, PK)
    TT("dve", sm["cfi"], sm["t1"], sm["den"], ALU.mult, PK, PK)
    rawb = ["raw:br0", "raw:br1", "raw:bi0", "raw:bi1"]
    cfr_b = sm["cfr"].unsqueeze(2).broadcast_to([128, 32, 16])
    cfi_b = sm["cfi"].unsqueeze(2).broadcast_to([128, 32, 16])
    t512a = tmpD[0].rearrange("p (g c) -> p g c", g=32)
    t512b = tmpD[1].rearrange("p (g c) -> p g c", g=32)
    TT("dve", t512a, Braw_r, cfr_b, ALU.mult, rawb + PK, ["tmpd:0"])
    TT("dve", t512b, Braw_i, cfi_b, ALU.mult, rawb + PK, ["tmpd:1"])
    TT("dve", Bbr, t512a, t512b, ALU.subtract, ["tmpd:0", "tmpd:1"], ["par:Bb"])
    TT("dve", t512a, Braw_r, cfi_b, ALU.mult, rawb + PK, ["tmpd:0"])
    TT("dve", t512b, Braw_i, cfr_b, ALU.mult, rawb + PK, ["tmpd:1"])
    TT("dve", Bbi, t512a, t512b, ALU.add, ["tmpd:0", "tmpd:1"], ["par:Bb"])
    TS("dve", sm["t1"], sm["th"], 32.0, None, ALU.mult, None, PK, PK)
    sincos(sm["t1"], sm["a32i"], sm["a32r"], sm["kf"], 32, PK, PK)
    ACT(sm["e32"], sm["rho"], AF.Exp, PK, PK, scale=32.0)
    TT("dve", sm["a32r"], sm["a32r"], sm["e32"], ALU.mult, PK, PK)
    TT("dve", sm["a32i"], sm["a32i"], sm["e32"], ALU.mult, PK, PK)
    TS("dve", sm["t2"], sm["a32i"], -1.0, None, ALU.mult, None, PK, PK)
    AAv = AAf.rearrange("p o t (g q) -> p o t g q", q=2)
    for (oo, tt_, src) in ((0, 0, "a32r"), (0, 1, "t2"), (1, 0, "a32i"), (1, 1, "a32r")):
        CP("dve", AAv[:, oo, tt_, :, :], sm[src].unsqueeze(2).broadcast_to([128, 32, 2]), PK, ["par:AAf"])
    TS("dve", sm["rhop"], sm["rho"], sig, None, ALU.mult, None, PK + ["cst:sig"], PK)
    TS("dve", sm["thp"], sm["th"], sig, None, ALU.mult, None, PK + ["cst:sig"], PK)
    sval_b = sval.unsqueeze(1).broadcast_to([128, 32, 32])
    GK = ["gen:all"]
    TT("dve", g_arg, sm["rhop"].unsqueeze(2).broadcast_to([128, 32, 32]), sval_b, ALU.mult, PK + ["cst:sval"], GK)
    TT("dve", g_phi, sm["thp"].unsqueeze(2).broadcast_to([128, 32, 32]), sval_b, ALU.mult, PK + ["cst:sval"], GK)
    fl = lambda a: a.rearrange("p g s -> p (g s)")
    sincos(fl(g_phi), fl(g_sn), fl(g_cs), fl(Pr), 1024, GK, GK + ["tab:all"])
    ACT(fl(g_phi), fl(g_arg), AF.Exp, GK, GK)
    ACT(fl(g_arg), fl(g_arg), AF.Exp, GK, GK, scale=-1.0)
    TT("dve", Pr, g_phi, g_cs, ALU.mult, GK, ["tab:all"])
    TT("dve", Pi, g_phi, g_sn, ALU.mult, GK, ["tab:all"])
    TT("dve", Qr, g_arg, g_cs, ALU.mult, GK, ["tab:all"])
    TT("dve", NQi, g_arg, g_sn, ALU.mult, GK, ["tab:all"])
    dump("Pr", Pr, [128, 32, 32], F32, ["tab:all"])
    dump("Bbr", Bbr, [128, 32, 16], F32, ["par:Bb"])
    for nm in ("are", "aim", "dt", "rho", "th", "er", "cs", "sn", "abr", "abi", "den", "cfr", "cfi"):
        dump("sm_" + nm, sm[nm], [128, 32], F32, PK + ["par:are", "par:aim"])
    dump("Braw", Braw_r, [128, 32, 16], F32, rawb)
    dump("Cr", Cr, [128, 32, 16], F32, ["par:Cr"])

    PHASE(5)
    _bias_split = len(_pending)
    DMA("sp", tab33[0:32, :], rel_tab, [], ["cst:tab33a"])
    S.add("dve", lambda e: e.memset(tab33[32:33, :], -30000.0), [], ["cst:tab33b"])
    ohs = AR.view("raw", RB_OFF + 10 * K, [33, 512], F32)
    DMA("sp", ohs, c_oh, [], ["raw:oh"])
    MM([(p7[0:8, :], tab33, ohs, True, True)], ["cst:tab33a", "cst:tab33b", "raw:oh", "par:Cr"], ["ps7"])
    CP("dve", fsb, p7[0:8, :], ["ps7"], ["cst:fsb"])
    DMA("sp", fd, fsb, ["cst:fsb"], ["fd:all"])
    hk = [AR.view("raw", RB_OFF + 12 * K + 512 * i, [128, 128], F32) for i in range(4)]
    for h in range(8):
        for jk in range(3):
            i = (h * 3 + jk) % 4
            src = bass.AP(fd_t, h * 512 + 128 * jk, [[1, 128], [1, 128]])
            DMA("sp", hk[i], src, ["fd:all"], ["raw:hk%d" % i])
            pb_i = 6 + (h * 3 + jk) % 2
            MM([(psf(pb_i)[:, 0:128], hk[i], anti, True, True)], ["raw:hk%d" % i, "cst:anti"], ["ps%d" % pb_i])
            CP("act", biasT[:, h // 4, jk, h % 4, :], psf(pb_i)[:, 0:128], ["ps%d" % pb_i], ["bias:all"])
    dump("bias", biasT, [128, 2, 3, 4, 128], BF16, ["bias:all"])

    _defer[0] = False
    PHASE(1)
    gain_b = gainT.unsqueeze(2).broadcast_to([128, 8, 128])

    p1_bank = {}

    def p1_front1(s):
        b = s % 2
        DMA("sp", xs[b], x_s[s], [], ["p1x:%d" % b])
        rms_front(xs[b], "p1x:%d" % b, hn[b], "p1h:%d" % b, b, ["p1x:%d" % b])
        bi = nb()
        p1_bank[("T", s)] = bi
        pT = psb(bi)
        TRS([(pT[:, 128 * k:128 * k + 128], hn[b][:, 128 * k:128 * k + 128], ident_b) for k in range(8)],
            ["p1h:%d" % b, "cst:ident_b"], ["ps%d" % bi])

    def p1_front2(s):
        b = s % 2
        bi = p1_bank[("T", s)]
        TT("dve", hTs[b], psb(bi).rearrange("p (k n) -> p k n", k=8), gain_b, ALU.mult,
           ["ps%d" % bi, "cst:gainT"], ["p1t:%d" % b])

    def p1_mm(s):
        b = s % 2
        bu = nb()
        MM([(psf(bu), hTs[b][:, k, :], Wu[:, k, :], k == 0, k == 7) for k in range(8)],
           ["p1t:%d" % b, "p1w:u"], ["ps%d" % bu])
        bk = nb()
        pkv = psf(bk)
        MM([(pkv[:, 0:128], Wk[:, k, :], hTs[b][:, k, :], k == 0, k == 7) for k in range(8)] +
           [(pkv[:, 128:256], Wv[:, k, :], hTs[b][:, k, :], k == 0, k == 7) for k in range(8)],
           ["p1t:%d" % b, "p1w:k", "p1w:v"], ["ps%d" % bk])
        p1_bank[("U", s)] = bu
        p1_bank[("KV", s)] = bk

    def p1_copy(s):
        bu, bk = p1_bank[("U", s)], p1_bank[("KV", s)]
        pkv = psf(bk)
        CP("act", U_tm[:, :, s, :], psf(bu).rearrange("p (g c) -> p g c", g=32), ["ps%d" % bu], ["utm:%d" % s])
        CP("act", kT_all[:, s::32], pkv[:, 0:128], ["ps%d" % bk], ["kT:all"])
        CP("act", vT_all[:, s::32], pkv[:, 128:256], ["ps%d" % bk], ["vT:all"])

    _bank_mod[0] = 8
    PREP_FIRST = True
    if PREP_FIRST:
        flush(_bias_split)
    p1_front1(0)
    p1_front1(1)
    p1_front2(0)
    for s in range(32):
        if s + 2 < 32:
            p1_front1(s + 2)
        if s + 1 < 32:
            p1_front2(s + 1)
        p1_mm(s)
        if s >= 1:
            p1_copy(s - 1)
        flush(0)
    p1_copy(31)
    flush()
    _bank_mod[0] = 8
    dump("utm", U_tm, [128, 32, 32, 16], BF16, ["utm:%d" % s for s in range(32)])
    dump("kT", kT_all, [128, 4096], BF16, ["kT:all"])

    PHASE(2)
    S.add("pool", lambda e: e.memset(v_all[:, :, :, 64:65], 1.0), [], ["vall:ones"])
    for j in range(4):
        bi = nb()
        pb_ = psb(bi)
        TRS([(pb_[:, 128 * i:128 * i + 128], vT_all[:, 128 * (8 * j + i):128 * (8 * j + i) + 128], ident_b)
             for i in range(8)], ["vT:all", "cst:ident_b"], ["ps%d" % bi])
        CP("act", v_all[:, 8 * j:8 * j + 8, :, 0:64],
           pb_.rearrange("p (i h d) -> p i h d", i=8, h=2), ["ps%d" % bi], ["vall:%d" % j])
    dump("vall", v_all, [128, 32, 2, 65], BF16, ["vall:%d" % j for j in range(4)] + ["vall:ones"])

    PHASE(3)
    U_g = AR.view("ug", RA_OFF, [128, 32, 4, 128], BF16)
    for g2 in range(16):
        bi = nb()
        pb_ = psb(bi)
        lst = []
        for gi in range(2):
            g = 2 * g2 + gi
            for k in range(4):
                lst.append((pb_[:, (gi * 4 + k) * 128:(gi * 4 + k) * 128 + 128],
                            U_tm[:, g, 8 * k:8 * k + 8, :].rearrange("p s c -> p (s c)"), ident_b))
        TRS(lst, ["utm:%d" % s for s in range(32)] + ["cst:ident_b"], ["ps%d" % bi])
        CP("act" if g2 % 2 == 0 else "dve", U_g[:, 2 * g2:2 * g2 + 2, :, :],
           pb_.rearrange("p (g k n) -> p g k n", g=2, k=4), ["ps%d" % bi], ["ug:%d" % g2])
    UG_ALL = ["ug:%d" % i for i in range(16)]

    def gen_L(g, b):
        Prg = Pr[:, g, :].unsqueeze(2).broadcast_to([128, 32, 16])
        Pig = Pi[:, g, :].unsqueeze(2).broadcast_to([128, 32, 16])
        Brg = Bbr[:, g, :].unsqueeze(1).broadcast_to([128, 32, 16])
        Big = Bbi[:, g, :].unsqueeze(1).broadcast_to([128, 32, 16])
        v = lambda t: t.rearrange("p (s c) -> p s c", s=32)
        rkeys = ["tab:all", "par:Bb"]
        TT("dve", v(tmpD[0]), Prg, Brg, ALU.mult, rkeys, ["tmpd:0"])
        TT("dve", v(tmpD[1]), Pig, Big, ALU.mult, rkeys, ["tmpd:1"])
        TT("pool", v(tmpP[1]), Pig, Brg, ALU.mult, rkeys, ["tmpp:1"])
        TT("dve", Lb[b][:, 0, :], tmpD[0], tmpD[1], ALU.subtract, ["tmpd:0", "tmpd:1"], ["lt:%dr" % b])
        TT("dve", v(tmpP[0]), Prg, Big, ALU.mult, rkeys, ["tmpp:0"])
        TT("pool", Lb[b][:, 1, :], tmpP[0], tmpP[1], ALU.add, ["tmpp:0", "tmpp:1"], ["lt:%di" % b])

    def gen_R(g, b):
        Qrg = Qr[:, g, :].unsqueeze(2).broadcast_to([128, 32, 16])
        NQg = NQi[:, g, :].unsqueeze(2).broadcast_to([128, 32, 16])
        Crg = Cr[:, g, :].unsqueeze(1).broadcast_to([128, 32, 16])
        Cig = Ci[:, g, :].unsqueeze(1).broadcast_to([128, 32, 16])
        v = lambda t: t.rearrange("p (s c) -> p s c", s=32)
        rkeys = ["tab:all", "par:Cr", "par:Ci"]
        TT("dve", v(tmpD[0]), Crg, Qrg, ALU.mult, rkeys, ["tmpd:0"])
        TT("dve", v(tmpD[1]), Cig, NQg, ALU.mult, rkeys, ["tmpd:1"])
        TT("pool", v(tmpP[0]), Crg, NQg, ALU.mult, rkeys, ["tmpp:0"])
        TT("dve", Rb[b][:, 0, :], tmpD[0], tmpD[1], ALU.add, ["tmpd:0", "tmpd:1"], ["rt:%dr" % b])
        TT("pool", v(tmpP[1]), Cig, Qrg, ALU.mult, rkeys, ["tmpp:1"])
        TT("pool", Rb[b][:, 1, :], tmpP[0], tmpP[1], ALU.subtract, ["tmpp:0", "tmpp:1"], ["rt:%di" % b])

    PHASE(6)
    Z = AR.view("z", RB_OFF, [128, 2, 32, 128], F32)
    for g in range(32):
        b = g % 2
        gen_L(g, b)
        bl = nb()
        pl_ = psb(bl)
        plv = pl_.rearrange("p (k r m) -> p k r m", k=4, r=2)
        TRS([(plv[:, k, ri, :], Lb[b][:, ri, 128 * k:128 * k + 128], ident_b) for k in range(4) for ri in range(2)],
            ["lt:%dr" % b, "lt:%di" % b, "cst:ident_b"], ["ps%d" % bl])
        CP("act", LTb[b].rearrange("p k r m -> p (k r m)"), pl_, ["ps%d" % bl], ["ltt:%d" % b])
        DMA("sp", lsc[g], Lb[b].rearrange("p r m -> p (r m)"), ["lt:%dr" % b, "lt:%di" % b], ["lsc:%d" % g])
        bz = nb()
        pz = psf(bz)
        MM([(pz[:, 128 * ri:128 * ri + 128], LTb[b][:, k, ri, :], U_g[:, g, k, :], k == 0, k == 3)
            for ri in range(2) for k in range(4)], ["ltt:%d" % b] + UG_ALL, ["ps%d" % bz])
        CP("act", Z[:, :, g, :], pz[:, 0:256].rearrange("p (r n) -> p r n", r=2), ["ps%d" % bz], ["z:%d" % g])
    Z_ALL = ["z:%d" % g for g in range(32)]
    dump("Z", Z, [128, 2, 32, 128], F32, Z_ALL)

    PHASE(7)
    Xd = AR.view("xd", GEN_OFF, [128, 2, 32, 128], BF16)
    Zv = Z.rearrange("p r g (q j) -> p r (g q) j", q=2)
    Xv = Xd.rearrange("p r g (q j) -> p r (g q) j", q=2)
    S.add("dve", lambda e: e.memset(Xv[0:64, :, :, 0:1], 0.0), [], ["xd:f"])
    S.add("pool", lambda e: e.memset(Xv[64:128, :, :, 63:64], 0.0), [], ["xd:b"])
    for step in range(1, 64):
        for (eng, lo, hi, cur, prev, tag) in (("dve", 0, 64, step, step - 1, "f"), ("pool", 64, 128, 63 - step, 64 - step, "b")):
            W_ = Wsc[lo:hi]
            sp_ = step % 2
            S_ = (Ssc if sp_ == 0 else Ssc2)[lo:hi]
            tag2 = tag + str(sp_)
            Xp = Zv[lo:hi, :, :, prev].unsqueeze(1).broadcast_to([64, 2, 2, 64])
            TT(eng, W_, AAf[lo:hi], Xp, ALU.mult, Z_ALL + ["par:AAf", "z:scan" + tag], ["scn:w" + tag])
            TT(eng, S_, W_[:, :, 0, :], W_[:, :, 1, :], ALU.add, ["scn:w" + tag], ["scn:s" + tag2])
            TT(eng, Zv[lo:hi, :, :, cur], Zv[lo:hi, :, :, cur], S_, ALU.add, ["scn:s" + tag2] + Z_ALL, ["z:scan" + tag])
            CP("act", Xv[lo:hi, :, :, cur], S_, ["scn:s" + tag2], ["xd:" + tag])
    dump("Xd", Xd, [128, 2, 32, 128], BF16, ["xd:f", "xd:b"])

    PHASE(8)
    yg = AR.view("yg", RC_OFF, [128, 32, 512], BF16)
    ident3 = ident_f.rearrange("p (t c) -> p t c", t=8)
    for g in range(32):
        b = g % 2
        DMA("sp", Lb[b].rearrange("p r m -> p (r m)"), lsc[g], ["lsc:%d" % g], ["lt:%dr" % b, "lt:%di" % b])
        gen_R(g, b)
        TT("dve", Dg[b].rearrange("p (t c) -> p t c", t=8), ident3,
           dB[:, 16 * g:16 * g + 16].unsqueeze(1).broadcast_to([128, 8, 16]), ALU.mult,
           ["cst:ident_f", "par:dB"], ["dg:%d" % b])
        lk = ["lt:%dr" % b, "lt:%di" % b, "rt:%dr" % b, "rt:%di" % b]
        pf_i, pb_i = nb(), nb()
        pf, pbk = psf(pf_i), psf(pb_i)
        MM([(pf, Lb[b][0:64, ri, 0:128], Rb[b][0:64, ri, :], ri == 0, ri == 1) for ri in range(2)] +
           [(pbk, Lb[b][64:128, ri, 384:512], Rb[b][64:128, ri, :], ri == 0, ri == 1) for ri in range(2)],
           lk, ["ps%d" % pf_i, "ps%d" % pb_i])
        TB = Tg[b].rearrange("p k m -> p (k m)")[:, 0:896].rearrange("p (m q) -> p m q", m=7)
        CP("act", TB[:, 4:7, :], pf[:, 128:512].rearrange("p (m q) -> p m q", m=3), ["ps%d" % pf_i], ["tg:%d_f" % b])
        CP("act", TB[:, 0:3, :], pbk[:, 0:384].rearrange("p (m q) -> p m q", m=3), ["ps%d" % pb_i], ["tg:%d_b" % b])
        TT("dve", bt1[0], pf[:, 0:128], mf, ALU.mult, ["ps%d" % pf_i, "cst:mf"], ["bt:1_0"])
        TT("dve", bt2[0], pbk[:, 384:512], mb, ALU.mult, ["ps%d" % pb_i, "cst:mb"], ["bt:2_0"])
        TT("dve", TB[:, 3, :], bt1[0], bt2[0], ALU.add, ["bt:1_0", "bt:2_0"], ["tg:%d_d" % b])
        py_i = nb()
        py = psf(py_i)
        MM([(py, Xd[:, ri, g, :], Rb[b][:, ri, :], ri == 0, False) for ri in range(2)] +
           [(py, U_g[:, g, k, :], TB[:, 3 - k:7 - k, :].rearrange("p m q -> p (m q)"), False, False)
            for k in range(4)] +
           [(py[:, 128 * k:128 * k + 128], U_g[:, g, k, :], Dg[b], False, k == 3) for k in range(4)],
           UG_ALL + ["tg:%d_f" % b, "tg:%d_b" % b, "tg:%d_d" % b, "dg:%d" % b, "xd:f", "xd:b", "rt:%dr" % b, "rt:%di" % b],
           ["ps%d" % py_i])
        ACT(yg[:, :, 16 * g:16 * g + 16], py.rearrange("p (t c) -> p t c", t=32), AF.Gelu_apprx_tanh,
            ["ps%d" % py_i], ["yg:%d" % g])
    YG_ALL = ["yg:%d" % g for g in range(32)]
    dump("tg", Tg[1], [128, 4, 512], BF16, ["tg:1_f", "tg:1_b", "tg:1_d"])
    dump("yg", yg, [128, 32, 512], BF16, YG_ALL)

    PHASE(9)
    ygT = AR.view("ygT", RB_OFF, [128, 4, 4096], BF16)
    ygT_v = ygT.rearrange("p c (n t) -> p c n t", t=32)
    cnt = 0
    for c in range(4):
        for tb in range(4):
            pi_ = nb()
            pb_ = psb(pi_)
            TRS([(pb_[:, 128 * i:128 * i + 128], yg[:, 8 * tb + i, 128 * c:128 * c + 128], ident_b) for i in range(8)],
                YG_ALL + ["cst:ident_b"], ["ps%d" % pi_])
            CP("act" if cnt % 2 == 0 else "dve", ygT_v[:, c, :, 8 * tb:8 * tb + 8],
               pb_.rearrange("p (i n) -> p n i", i=8), ["ps%d" % pi_], ["ygT:%d" % cnt])
            cnt += 1
    YGT_ALL = ["ygT:%d" % i for i in range(16)]
    dump("ygT", ygT, [128, 4, 4096], BF16, YGT_ALL)

    PHASE(10)
    attnT = AR.view("attnT", RA_OFF, [128, 4, 4096], BF16)
    o = RC_OFF
    def rc(prefix, shape, dt):
        nonlocal o
        esz = 2 if dt == BF16 else 4
        v = AR.view(prefix, o, shape, dt)
        o += (_prod(shape[1:]) * esz + 3) // 4 * 4
        return v
    Wq = rc("w2a", [128, 8, 512], BF16)
    Wza = rc("w2a", [128, 8, 512], BF16)
    xt2 = [rc("x2", [128, 1024], F32) for _ in range(2)]
    hn2 = [AR.view("h2", RD_OFF + 76 * K + 2 * K * i_, [128, 1024], BF16) for i_ in range(4)]
    hTb = [rc("hT2", [128, 8, 512], BF16) for _ in range(2)]
    qT = rc("qT", [128, 8, 512], BF16)
    zas = rc("zas", [128, 4, 512], BF16)
    PT = [rc("pt", [128, 3, 512], BF16) for _ in range(2)]
    den = [rc("den", [128, 8], F32) for _ in range(2)]
    at = [rc("at", [128, 256], F32) for _ in range(2)]
    ag = [rc("ag", [128, 512], BF16) for _ in range(2)]
    assert o <= RD_OFF + 32 * K, o
    A2_END = o
    Wq_p = Wq.rearrange("p k (g h d) -> p k g h d", g=4, h=2)
    for h_ in range(2):
        for g_ in range(4):
            c0_ = 256 * h_ + 64 * g_
            DMA("pool", Wq_p[:, :, g_, h_, :], w_in_v[:, :, c0_:c0_ + 64], [], ["w2a:q"])
    DMA("pool", Wza, w_in_v[:, :, 768:1280], [], ["w2a:za"])
    S.add("dve", lambda e: e.memset(qT, 0.0), [], ["qT:zero"] + ["qT:%d" % h_ for h_ in range(8)])

    def front(tok0, ntile, hT_dst, hkey, xt_l, hn_l, xpre, hpre, psbase, cnt0):
        for i in range(ntile):
            bb = (cnt0 + i) % 2
            DMA("sp", xt_l[bb], x[tok0 + 128 * i: tok0 + 128 * i + 128, :], [], ["%s:%d" % (xpre, bb)])
            rms_front(xt_l[bb], None, hn_l[bb], "%s:%d" % (hpre, bb), 2 + bb, ["%s:%d" % (xpre, bb)])
            pi_ = nb()
            pT_ = psb(pi_)
            TRS([(pT_[:, 128 * k:128 * k + 128], hn_l[bb][:, 128 * k:128 * k + 128], ident_b) for k in range(8)],
                ["%s:%d" % (hpre, bb), "cst:ident_b"], ["ps%d" % pi_])
            TT("dve", hT_dst[:, :, 128 * i:128 * i + 128], pT_.rearrange("p (k n) -> p k n", k=8), gain_b, ALU.mult,
               ["ps%d" % pi_, "cst:gainT"], [hkey + "_%d" % i])

    ucount = [0]
    pend_tr = []
    unit_hooks = {}

    def attention_block(jb):
        bb = jb % 2
        hT_ = hTb[bb]
        hkeys = ["hT2:%d_%d" % (bb, i) for i in range(4)]
        for hq in range(4):
            bi = nb()
            MM([(psf(bi), Wq[:, k, 128 * hq:128 * hq + 128], hT_[:, k, :], k == 0, k == 7) for k in range(8)],
               hkeys + ["w2a:q"], ["ps%d" % bi])
            TS("dve", qT[0:64, hq, :], psf(bi)[0:64, :], 0.125, None, ALU.mult, None, ["ps%d" % bi, "qT:zero"], ["qT:%d" % hq])
            TS("dve", qT[64:128, 4 + hq, :], psf(bi)[64:128, :], 0.125, None, ALU.mult, None, ["ps%d" % bi, "qT:zero"],
               ["qT:%d" % (4 + hq)])
        for tt in range(4):
            bi = nb()
            MM([(psf(bi), hT_[:, k, 128 * tt:128 * tt + 128], Wza[:, k, :], k == 0, k == 7) for k in range(8)],
               hkeys + ["w2a:za"], ["ps%d" % bi])
            ACT(zas[:, tt, :], psf(bi), AF.Silu, ["ps%d" % bi], ["zas:%d" % tt])

        def scores(tt, kvh):
            qi = 4 * jb + tt
            bi_ = qi % 16
            jks = [jk for jk in range(3) if 0 <= bi_ + jk - 1 <= 15]
            u = ucount[0]
            ucount[0] += 1
            pb2 = u % 2
            for jk in jks:
                kt = qi + jk - 1
                sb = nb()
                lst = [(psf(sb), ident_b, biasT[:, kvh, jk, :, :].rearrange("p g q -> p (g q)"), True, False)]
                for g in range(4):
                    h = 4 * kvh + g
                    lst.append((psf(sb)[:, 128 * g:128 * g + 128], kT_all[:, 128 * kt:128 * kt + 128],
                                qT[:, h, 128 * tt:128 * tt + 128], False, g == 3))
                MM(lst, ["kT:all", "bias:all", "cst:ident_b"] + ["qT:%d" % (4 * kvh + g) for g in range(4)], ["ps%d" % sb])
                ACT(PT[pb2][:, jk, :], psf(sb), AF.Exp, ["ps%d" % sb], ["pt:%d_%d" % (pb2, jk)])
            return (tt, kvh, qi, jks, pb2)

        def bias_mults(st):
            (tt, kvh, qi, jks, pb2) = st
            for jk in jks:
                ptv = PT[pb2][:, jk, :].rearrange("p (g q) -> p g q", g=4)
                TT("dve", ptv, ptv, biasT[:, 4 * kvh:4 * kvh + 4, jk, :], ALU.mult,
                   ["pt:%d_%d" % (pb2, jk), "bias:all"], ["pt:%d_%d" % (pb2, jk)])

        def pv_post(st):
            (tt, kvh, qi, jks, pb2) = st
            ab = qi % 2
            pvb = nb()
            pv = psf(pvb)[:, 0:260].rearrange("p (g d) -> p g d", g=4)
            lst = []
            for g in range(4):
                for jk in jks:
                    kt = qi + jk - 1
                    lst.append((pv[:, g, :], PT[pb2][:, jk, 128 * g:128 * g + 128], v_all[:, kt, kvh, :],
                                jk == jks[0], jk == jks[-1]))
            MM(lst, ["pt:%d_%d" % (pb2, jk) for jk in jks] + ["vall:ones"] + ["vall:%d" % j for j in range(4)], ["ps%d" % pvb])
            dn = den[pb2]
            TT("dve", dn[:, 0:4], pv[:, :, 64], esink[:, 4 * kvh:4 * kvh + 4], ALU.add, ["ps%d" % pvb, "cst:esink"], ["den:%d" % pb2])
            S.add("dve", lambda e, dn=dn: e.reciprocal(out=dn[:, 4:8], in_=dn[:, 0:4]), ["den:%d" % pb2], ["den:%d" % pb2])
            atv = at[pb2].rearrange("p (g d) -> p g d", g=4)
            TT("dve", atv, pv[:, :, 0:64], dn[:, 4:8].unsqueeze(2).broadcast_to([128, 4, 64]), ALU.mult,
               ["ps%d" % pvb, "den:%d" % pb2], ["at:%d" % pb2])
            TT("dve", ag[ab][:, 256 * kvh:256 * kvh + 256], at[pb2], zas[:, tt, 256 * kvh:256 * kvh + 256], ALU.mult,
               ["at:%d" % pb2, "zas:%d" % tt], ["ag:%d_%d" % (ab, kvh)])
            if kvh == 1:
                pend_tr.append((ab, qi))

        def do_tr(n):
            for _ in range(n):
                (ab, qi) = pend_tr.pop(0)
                pt_i = nb()
                TRS([(psb(pt_i)[:, 128 * c:128 * c + 128], ag[ab][:, 128 * c:128 * c + 128], ident_b) for c in range(4)],
                    ["ag:%d_0" % ab, "ag:%d_1" % ab, "cst:ident_b"], ["ps%d" % pt_i])
                CP("dve", attnT[:, :, 128 * qi:128 * qi + 128], psb(pt_i)[:, 0:512].rearrange("p (c n) -> p c n", c=4),
                   ["ps%d" % pt_i], ["attnT:%d" % qi])

        units = [(tt, kvh) for tt in range(4) for kvh in range(2)]
        pend = [scores(*units[0])]
        for i in range(len(units)):
            if i + 1 < len(units):
                pend.append(scores(*units[i + 1]))
            had = list(pend_tr)
            pv_post(pend.pop(0))
            if had:
                do_tr(len(had))
            for fn_ in unit_hooks.get(i, ()):
                fn_()

    def f2_rms(jb):
        for i in range(4):
            DMA("sp", xt2[i % 2], x[512 * jb + 128 * i: 512 * jb + 128 * i + 128, :], [], ["x2:%d" % (i % 2)])
            rms_front(xt2[i % 2], None, hn2[i], "h2:%d" % i, 2 + i % 2, ["x2:%d" % (i % 2)])

    def f2_T(jb):
        for i in range(4):
            pi_ = nb()
            pT_ = psb(pi_)
            TRS([(pT_[:, 128 * k:128 * k + 128], hn2[i][:, 128 * k:128 * k + 128], ident_b) for k in range(8)],
                ["h2:%d" % i, "cst:ident_b"], ["ps%d" % pi_])
            TT("dve", hTb[jb % 2][:, :, 128 * i:128 * i + 128], pT_.rearrange("p (k n) -> p k n", k=8), gain_b, ALU.mult,
               ["ps%d" % pi_, "cst:gainT"], ["hT2:%d_%d" % (jb % 2, i)])

    f2_rms(0)
    f2_T(0)
    f2_rms(1)
    for jb in range(8):
        unit_hooks.clear()
        if jb + 1 < 8:
            f2_T(jb + 1)
        if jb + 2 < 8:
            unit_hooks[3] = [lambda nj=jb + 2: f2_rms(nj)]
        attention_block(jb)
    if pend_tr:
        _jb_last = 7
        (ab, qi) = pend_tr.pop(0)
        pt_i = nb()
        TRS([(psb(pt_i)[:, 128 * c:128 * c + 128], ag[ab][:, 128 * c:128 * c + 128], ident_b) for c in range(4)],
            ["ag:%d_0" % ab, "ag:%d_1" % ab, "cst:ident_b"], ["ps%d" % pt_i])
        CP("dve", attnT[:, :, 128 * qi:128 * qi + 128], psb(pt_i)[:, 0:512].rearrange("p (c n) -> p c n", c=4),
           ["ps%d" % pt_i], ["attnT:%d" % qi])
    ATT_ALL = ["attnT:%d" % i for i in range(32)]
    dump("attnT", attnT, [128, 4, 4096], BF16, ATT_ALL)

    PHASE(11)
    o = RD_OFF + 32 * K
    Wg = rc("w2b", [128, 8, 2048], BF16)
    Wzs = rc("w2b", [128, 8, 512], BF16)
    Wglu = rc("w2b", [128, 4, 512], BF16)
    assert o <= AR.nbytes, o
    o = CST_END
    Wo = rc("w2c", [128, 8, 1024], BF16)
    fgB = rc("w2c", [128, 1024], F32)
    assert o <= RA_OFF
    o = RC_OFF
    Wba = rc("w2d", [128, 4, 1024], BF16)
    Wbs = rc("w2d", [128, 4, 1024], BF16)
    xt3 = [rc("x3", [128, 1024], F32) for _ in range(2)]
    hn3 = [rc("h3", [128, 1024], BF16) for _ in range(2)]
    hTc = [rc("hT3", [128, 8, 256], BF16) for _ in range(2)]
    zss = rc("zss", [128, 4, 256], F32)
    sg = rc("sg", [128, 4, 256], F32)
    ssmT = rc("ssmT", [128, 4, 256], BF16)
    sga = [rc("sga", [128, 256], F32) for _ in range(2)]
    sgs = [rc("sgs", [128, 256], F32) for _ in range(2)]
    m1 = [rc("m1", [128, 256], F32) for _ in range(2)]
    m2 = [rc("m2", [128, 256], F32) for _ in range(2)]
    mT = [rc("mT", [128, 8, 256], BF16) for _ in range(2)]
    junk3 = rc("junk3", [128, 1024], BF16)
    assert o <= RD_OFF + 32 * K, o
    o = RD_OFF + 32 * K + 44 * K
    xr = [rc("xr", [128, 1024], F32) for _ in range(2)]
    assert o <= AR.nbytes, o
    for c4 in range(4):
        DMA("pool", Wg[:, :, 512 * c4:512 * c4 + 512], w_in_v[:, :, 2304 + 512 * c4:2304 + 512 * c4 + 512], [], ["w2b:g%d" % c4])
    DMA("pool", Wzs, w_in_v[:, :, 1792:2304], [], ["w2b:zs"])
    DMA("pool", Wglu, w_glu.rearrange("(k p) e -> p k e", p=128), [], ["w2b:glu"])
    DMA("pool", Wo, w_out.rearrange("(k p) e -> p k e", p=128), [], ["w2c:o"])
    DMA("sp", fgB, fgain[0:1, :].partition_broadcast(128), [], ["w2c:fg"])
    DMA("pool", Wba, w_ba.rearrange("(k p) e -> p k e", p=128), [], ["w2d:ba"])
    DMA("pool", Wbs, w_bs.rearrange("(k p) e -> p k e", p=128), [], ["w2d:bs"])
    WG_ALL = ["w2b:g%d" % c for c in range(4)]
    ocnt = [0]

    def merge_p1(jb):
        bb = jb % 2
        hT_ = hTc[bb]
        hkeys = ["hT3:%d_%d" % (bb, i) for i in range(2)]
        tok0 = 256 * jb
        for c in range(4):
            bi = nb()
            MM([(psf(bi)[:, 0:256], Wzs[:, k, 128 * c:128 * c + 128], hT_[:, k, :], k == 0, k == 7) for k in range(8)],
               hkeys + ["w2b:zs"], ["ps%d" % bi])
            ACT(zss[:, c, :], psf(bi)[:, 0:256], AF.Silu, ["ps%d" % bi], ["zss:%d" % c])
        for c in range(4):
            bi = nb()
            MM([(psf(bi)[:, 0:256], Wglu[:, kc, 128 * c:128 * c + 128], ygT[:, kc, tok0:tok0 + 256], kc == 0, kc == 3)
                for kc in range(4)], YGT_ALL + ["w2b:glu"], ["ps%d" % bi])
            ACT(sg[:, c, :], psf(bi)[:, 0:256], AF.Sigmoid, ["ps%d" % bi, "cst:bgluT"], ["sg:%d" % c], bias=bgluT[:, c:c + 1])
        TT("pool", sg, sg, ygT[:, :, tok0:tok0 + 256], ALU.mult, ["sg:%d" % c for c in range(4)] + YGT_ALL, ["sg:all"])
        TT("pool", ssmT, sg, zss, ALU.mult, ["sg:all"] + ["zss:%d" % c for c in range(4)], ["ssmT:all"])

    def merge_p2(jb, hooks=None):
        bb = jb % 2
        hT_ = hTc[bb]
        hkeys = ["hT3:%d_%d" % (bb, i) for i in range(2)]
        tok0 = 256 * jb
        for e8 in range(8):
            for fn_ in (hooks or {}).get(e8, ()):
                fn_()
            eb = e8 % 2
            b_ga, b_gs, b_ba, b_bs = nb(), nb(), nb(), nb()
            MM([(psf(b_ga)[:, 0:256], Wg[:, k, 128 * e8:128 * e8 + 128], hT_[:, k, :], k == 0, k == 7) for k in range(8)],
               hkeys + WG_ALL, ["ps%d" % b_ga])
            ACT(sga[eb], psf(b_ga)[:, 0:256], AF.Sigmoid, ["ps%d" % b_ga, "cst:bgT"], ["sga:%d" % eb], bias=bgT[:, e8:e8 + 1])
            MM([(psf(b_gs)[:, 0:256], Wg[:, k, 1024 + 128 * e8:1024 + 128 * e8 + 128], hT_[:, k, :], k == 0, k == 7) for k in range(8)],
               hkeys + WG_ALL, ["ps%d" % b_gs])
            ACT(sgs[eb], psf(b_gs)[:, 0:256], AF.Sigmoid, ["ps%d" % b_gs, "cst:bgT"], ["sgs:%d" % eb], bias=bgT[:, 8 + e8:9 + e8])
            MM([(psf(b_ba)[:, 0:256], Wba[:, kc, 128 * e8:128 * e8 + 128], attnT[:, kc, tok0:tok0 + 256], kc == 0, kc == 3)
                for kc in range(4)], ATT_ALL + ["w2d:ba"], ["ps%d" % b_ba])
            MM([(psf(b_bs)[:, 0:256], Wbs[:, kc, 128 * e8:128 * e8 + 128], ssmT[:, kc, :], kc == 0, kc == 3)
                for kc in range(4)], ["ssmT:all", "w2d:bs"], ["ps%d" % b_bs])
            TT("dve", m1[eb], psf(b_ba)[:, 0:256], sga[eb], ALU.mult, ["ps%d" % b_ba, "sga:%d" % eb], ["m1:%d" % eb])
            TT("dve", m2[eb], psf(b_bs)[:, 0:256], sgs[eb], ALU.mult, ["ps%d" % b_bs, "sgs:%d" % eb], ["m2:%d" % eb])
            TT("dve", mT[bb][:, e8, :], m1[eb], m2[eb], ALU.add, ["m1:%d" % eb, "m2:%d" % eb], ["mT:%d_%d" % (bb, e8)])

    out_pend = []

    def merge_out(jb):
        bb = jb % 2
        tok0 = 256 * jb
        mkeys = ["mT:%d_%d" % (bb, e8) for e8 in range(8)]
        for tt in range(2):
            ob = ocnt[0] % 2
            ocnt[0] += 1
            t0 = tok0 + 128 * tt
            out_pend.append((jb, tt, ob, t0))
            DMA("sp", xr[ob], x[t0:t0 + 128, :], [], ["xr:%d" % ob])
            for half in range(2):
                bo = nb()
                MM([(psf(bo), mT[bb][:, e8, 128 * tt:128 * tt + 128], Wo[:, e8, 512 * half:512 * half + 512], e8 == 0, e8 == 7)
                    for e8 in range(8)], mkeys + ["w2c:o"], ["ps%d" % bo])
                TT("dve", xr[ob][:, 512 * half:512 * half + 512], psf(bo), xr[ob][:, 512 * half:512 * half + 512], ALU.add,
                   ["ps%d" % bo, "xr:%d" % ob], ["xr:%d" % ob])

    def merge_fin():
        while out_pend:
            (jb, tt, ob, t0) = out_pend.pop(0)
            col = 4 + ob
            ssc = ss_t[:, col:col + 1]
            ACT(junk3, xr[ob], AF.Square, ["xr:%d" % ob], ["junk3:0", "cst:ss%d" % col], accum_out=ssc)
            ACT(ssc, ssc, AF.Sqrt, ["cst:ss%d" % col, "cst:eps"], ["cst:ss%d" % col], scale=1.0 / 1024.0, bias=epst)
            S.add("dve", lambda e, ssc=ssc: e.reciprocal(out=ssc, in_=ssc), ["cst:ss%d" % col], ["cst:ss%d" % col])
            STT(xr[ob], xr[ob], ssc, fgB, ALU.mult, ALU.mult, ["xr:%d" % ob, "cst:ss%d" % col, "w2c:fg"], ["xr:%d" % ob])
            DMA("pool", out[t0:t0 + 128, :], xr[ob], ["xr:%d" % ob], ["out:%d" % (2 * jb + tt)])

    def front3_rms(jb):
        for i in range(2):
            DMA("sp", xt3[i], x[256 * jb + 128 * i: 256 * jb + 128 * i + 128, :], [], ["x3:%d" % i])
            rms_front(xt3[i], None, hn3[i], "h3:%d" % i, 2 + i, ["x3:%d" % i])

    def front3_T(jb):
        bb = jb % 2
        for i in range(2):
            pi_ = nb()
            pT_ = psb(pi_)
            TRS([(pT_[:, 128 * k:128 * k + 128], hn3[i][:, 128 * k:128 * k + 128], ident_b) for k in range(8)],
                ["h3:%d" % i, "cst:ident_b"], ["ps%d" % pi_])
            TT("dve", hTc[bb][:, :, 128 * i:128 * i + 128], pT_.rearrange("p (k n) -> p k n", k=8), gain_b, ALU.mult,
               ["ps%d" % pi_, "cst:gainT"], ["hT3:%d_%d" % (bb, i)])

    front3_rms(0)
    front3_T(0)
    front3_rms(1)
    for jb in range(16):
        if jb + 1 < 16:
            front3_T(jb + 1)
        merge_p1(jb)
        if jb >= 1:
            merge_out(jb - 1)
        hk_ = {2: [merge_fin]}
        if jb + 2 < 16:
            hk_[5] = [lambda nj=jb + 2: front3_rms(nj)]
        merge_p2(jb, hk_)
    merge_out(15)
    merge_fin()
    PHASE(0)
    S.add("sp", lambda e: e.nop(), ["out:%d" % i for i in range(32)] + ["dbgout:" + n for n in dbg_out], [])
    S.emit()
    return nc, dbg_out


def _t5_bucket_np(rel):
    half = 16
    ret = (rel > 0).astype(np.int64) * half
    n = np.abs(rel)
    max_exact = half // 2
    nf = np.maximum(n, 1).astype(np.float32)
    large = max_exact + (np.log(nf / np.float32(max_exact)) / np.float32(math.log(128 / max_exact))
                         * np.float32(half - max_exact)).astype(np.int32)
    large = np.minimum(large, half - 1)
    return ret + np.where(n < max_exact, n, large)


def _t5_bucket_table(rel):
    try:
        import jax
        import jax.numpy as jnp
        cpu = jax.devices("cpu")[0]
        with jax.default_device(cpu):
            r = jnp.asarray(rel, dtype=jnp.int32)
            half = 16
            ret = (r > 0).astype(jnp.int32) * half
            n = jnp.abs(r)
            max_exact = half // 2
            nf = jnp.maximum(n, 1).astype(jnp.float32)
            large = max_exact + (jnp.log(nf / max_exact) / math.log(128 / max_exact)
                                 * (half - max_exact)).astype(jnp.int32)
            large = jnp.minimum(large, half - 1)
            return np.asarray(ret + jnp.where(n < max_exact, n, large)).astype(np.int64)
    except Exception:
        return _t5_bucket_np(rel)


def _host_constants():
    c = {}
    c["c_ident"] = np.eye(128, dtype=np.float32)
    c["c_anti"] = np.ascontiguousarray(np.eye(128, dtype=np.float32)[::-1])
    c["c_sval"] = np.broadcast_to(np.arange(32, dtype=np.float32), (128, 32)).copy()
    sg = np.ones((128, 1), np.float32)
    sg[:64] = -1.0
    c["c_sig"] = sg
    s_idx = np.arange(128)[:, None] // 16
    t_idx = np.arange(128)[None, :] // 16
    c["c_mf"] = (t_idx >= s_idx).astype(np.float32)
    c["c_mb"] = (s_idx >= t_idx).astype(np.float32)
    oh = np.zeros((33, 512), np.float32)
    r = np.arange(511)
    rel = r - 255
    bk = _t5_bucket_table(rel)
    oh[bk, r] = 1.0
    oh[32, r] = (np.abs(rel) > 128).astype(np.float32)
    oh[32, 511] = 1.0
    c["c_oh"] = oh
    return c


_CACHE = {}


def _get_program():
    if "nc" not in _CACHE:
        _CACHE["nc"] = build_program()[0]
    return _CACHE["nc"]


def make_in_maps(inputs):
    f = lambda a: np.ascontiguousarray(np.asarray(a, dtype=np.float32))
    shared = {
        "norm_gain": f(inputs["norm_gain"]).reshape(1, 1024),
        "w_in": f(inputs["w_in"]).reshape(1024, 4352),
        "b_gate": f(inputs["b_gate"]).reshape(1, 2048),
        "attn_sink": f(inputs["attn_sink"]).reshape(1, 8),
        "rel_bias_table": f(inputs["rel_bias_table"]).reshape(32, 8),
        "ssm_a_re": f(inputs["ssm_a_re"]).reshape(2, 32, 64),
        "ssm_a_im": f(inputs["ssm_a_im"]).reshape(2, 32, 64),
        "ssm_log_dt": f(inputs["ssm_log_dt"]).reshape(2, 32),
        "ssm_b_re": f(inputs["ssm_b_re"]).reshape(2, 32, 64, 16),
        "ssm_b_im": f(inputs["ssm_b_im"]).reshape(2, 32, 64, 16),
        "ssm_c_re": f(inputs["ssm_c_re"]).reshape(2, 32, 16, 64),
        "ssm_c_im": f(inputs["ssm_c_im"]).reshape(2, 32, 16, 64),
        "ssm_d": f(inputs["ssm_d"]).reshape(1, 512),
        "w_glu": f(inputs["w_glu"]).reshape(512, 512),
        "b_glu": f(inputs["b_glu"]).reshape(1, 512),
        "w_branch_attn": f(inputs["w_branch_attn"]).reshape(512, 1024),
        "w_branch_ssm": f(inputs["w_branch_ssm"]).reshape(512, 1024),
        "w_out": f(inputs["w_out"]).reshape(1024, 1024),
        "final_norm_gain": f(inputs["final_norm_gain"]).reshape(1, 1024),
    }
    shared.update(_host_constants())
    xs = f(inputs["x"])
    maps = []
    for c in range(8):
        m = dict(shared)
        m["x"] = np.ascontiguousarray(xs[2 * c:2 * c + 2].reshape(4096, 1024))
        maps.append(m)
    return maps


def kernel(**inputs):
    nc = _get_program()
    in_maps = make_in_maps(inputs)
    res = run_bass_kernel_spmd(nc, in_maps, core_ids=list(range(8)))
    outs = [np.asarray(r["out"]).reshape(2, 2048, 1024) for r in res.results]
    return np.concatenate(outs, axis=0).astype(np.float32)
```

```python
import math
from contextlib import ExitStack
import numpy as np
import concourse.bass as bass
import concourse.mybir as mybir
from concourse.bass_utils import run_bass_kernel_spmd

F32 = mybir.dt.float32
BF16 = mybir.dt.bfloat16
I32 = mybir.dt.int32
ALU = mybir.AluOpType
AF = mybir.ActivationFunctionType
ENGS = ("pe", "act", "dve", "pool", "sp")
STRICT_SAME_ENGINE = True
EPS = 1e-6
TWO_PI = 2.0 * math.pi


def _prod(xs):
    r = 1
    for v in xs:
        r *= int(v)
    return r


class Sched:
    N_DMA_SEMS = 40

    def __init__(self, nc):
        self.nc = nc
        self.ops = []
        self.last_w = {}
        self.readers = {}
        self.guards = {}

    def guard(self, new_prefix, old_prefixes):
        s = set()
        for k, w in self.last_w.items():
            if k.split(":")[0] in old_prefixes and w is not None:
                s.add(w)
        for k, rs in self.readers.items():
            if k.split(":")[0] in old_prefixes:
                s.update(rs)
        best = {}
        out = set()
        for i in s:
            o = self.ops[i]
            if o["dma"]:
                out.add(i)
            else:
                best[o["eng"]] = max(best.get(o["eng"], -1), i)
        out.update(best.values())
        self.guards.setdefault(new_prefix, set()).update(out)

    def add(self, eng, fn, reads=(), writes=(), dma=False):
        idx = len(self.ops)
        psr = [k for k in reads if k.startswith("ps")]
        writes = list(writes) + [k for k in psr if k not in writes]
        deps = set()
        raw = set()
        for k in list(reads) + list(writes):
            g = self.guards.get(k.split(":")[0])
            if g:
                deps |= g
                raw |= g
        for k in reads:
            w = self.last_w.get(k)
            if w is not None:
                deps.add(w)
                raw.add(w)
        for k in writes:
            w = self.last_w.get(k)
            if w is not None:
                deps.add(w)
            for r in self.readers.get(k, ()):
                deps.add(r)
        deps.discard(idx)
        for k in reads:
            self.readers.setdefault(k, []).append(idx)
        for k in writes:
            self.last_w[k] = idx
            self.readers[k] = []
        fdeps = []
        for d in deps:
            p = self.ops[d]
            if (not p["dma"]) and (not dma) and p["eng"] == eng:
                if eng == "pe" or (d not in raw and not STRICT_SAME_ENGINE):
                    continue
            fdeps.append(d)
        self.ops.append(dict(eng=eng, fn=fn, deps=sorted(fdeps), dma=dma, idx=idx))
        return idx

    def emit(self):
        nc = self.nc
        ops = self.ops
        needed = set()
        for o in ops:
            needed.update(o["deps"])
        cnt = {e: 0 for e in ENGS}
        for o in ops:
            if (not o["dma"]) and o["idx"] in needed:
                cnt[o["eng"]] += 1
                o["ticket"] = cnt[o["eng"]]
        dma_ops = [o for o in ops if o["dma"]]
        pools = {"sp": (0, 28), "pool": (28, 12), "act": (40, 0)}
        nd = 40
        sem_val = [0] * nd
        qcnt = {"sp": 0, "pool": 0}
        for o in dma_ops:
            base, n = pools[o["eng"]]
            s = base + qcnt[o["eng"]] % n
            qcnt[o["eng"]] += 1
            o["dsem"] = s
            o["dprev"] = sem_val[s]
            sem_val[s] += 16
            o["dval"] = sem_val[s]
        with ExitStack() as st:
            esem = {e: st.enter_context(nc.semaphore("sem_" + e)) for e in ENGS}
            dsem = [st.enter_context(nc.semaphore("dsem%d" % i)) for i in range(nd)]
            block = st.enter_context(nc.Block())
            streams = {e: [o for o in ops if o["eng"] == e] for e in ENGS}

            def make_body(e):
                def body(eng):
                    waited = {f: 0 for f in ENGS}
                    dwaited = [0] * nd
                    for o in streams[e]:
                        for d in o["deps"]:
                            p = ops[d]
                            if p["dma"]:
                                s = p["dsem"]
                                if dwaited[s] < p["dval"]:
                                    eng.wait_ge(dsem[s], p["dval"])
                                    dwaited[s] = p["dval"]
                            else:
                                f = p["eng"]
                                if waited[f] < p["ticket"]:
                                    eng.wait_ge(esem[f], p["ticket"])
                                    waited[f] = p["ticket"]
                        if o["dma"]:
                            s = o["dsem"]
                            if o["dprev"] > 0 and dwaited[s] < o["dprev"]:
                                eng.wait_ge(dsem[s], o["dprev"])
                                dwaited[s] = o["dprev"]
                            ins = o["fn"](eng)
                            ins.then_inc(dsem[s], 16)
                        else:
                            ins = o["fn"](eng)
                            if "ticket" in o:
                                ins.then_inc(esem[e], 1)
                return body

            block.tensor(make_body("pe"))
            block.scalar(make_body("act"))
            block.vector(make_body("dve"))
            block.gpsimd(make_body("pool"))
            block.sync(make_body("sp"))


class Arena:
    def __init__(self, nc, S, nbytes):
        self.t = nc.alloc_sbuf_tensor("arena", [128, nbytes // 2], BF16)
        self.S = S
        self.nbytes = nbytes
        self.allocs = []

    def view(self, prefix, off, shape, dt):
        esz = 2 if dt == BF16 else 4
        nb = _prod(shape[1:]) * esz
        assert off % 4 == 0 and off + nb <= self.nbytes, (prefix, off, nb)
        olds = set(p for (p, s, e) in self.allocs if p != prefix and s < off + nb and off < e)
        if olds:
            self.S.guard(prefix, olds)
        self.allocs.append((prefix, off, off + nb))
        ap = self.t[:, off // 2: off // 2 + nb // 2]
        if dt != BF16:
            ap = ap.bitcast(dt)
        if len(shape) == 3:
            ap = ap.rearrange("p (a b) -> p a b", a=shape[1])
        elif len(shape) == 4:
            ap = ap.rearrange("p (a b c) -> p a b c", a=shape[1], b=shape[2])
        elif len(shape) == 5:
            ap = ap.rearrange("p (a b c d) -> p a b c d", a=shape[1], b=shape[2], c=shape[3])
        if shape[0] < 128:
            ap = ap[0:shape[0]]
        return ap


def build_program(dbg=(), phase_limit=99):
    nc = bass.Bass("TRN2", target_bir_lowering=False)
    S = Sched(nc)
    _phase = [0]
    _orig_add = S.add

    _pending = []
    _defer = [False]

    def _add(eng, fn, reads=(), writes=(), dma=False):
        if _phase[0] > phase_limit:
            return None
        if _defer[0]:
            _pending.append((eng, fn, list(reads), list(writes), dma))
            return None
        return _orig_add(eng, fn, reads, writes, dma)
    S.add = _add

    def flush(n=None):
        k = len(_pending) if n is None else min(n, len(_pending))
        for _ in range(k):
            _orig_add(*_pending.pop(0))

    def PHASE(n):
        _phase[0] = n

    def din(name, shape):
        return nc.dram_tensor(name, shape, F32, kind="ExternalInput").ap()

    x = din("x", [4096, 1024])
    norm_gain = din("norm_gain", [1, 1024])
    w_in = din("w_in", [1024, 4352])
    b_gate = din("b_gate", [1, 2048])
    attn_sink = din("attn_sink", [1, 8])
    rel_tab = din("rel_bias_table", [32, 8])
    a_re = din("ssm_a_re", [2, 32, 64])
    a_im = din("ssm_a_im", [2, 32, 64])
    log_dt = din("ssm_log_dt", [2, 32])
    b_re = din("ssm_b_re", [2, 32, 64, 16])
    b_im = din("ssm_b_im", [2, 32, 64, 16])
    c_re = din("ssm_c_re", [2, 32, 16, 64])
    c_im = din("ssm_c_im", [2, 32, 16, 64])
    ssm_d = din("ssm_d", [1, 512])
    w_glu = din("w_glu", [512, 512])
    b_glu = din("b_glu", [1, 512])
    w_ba = din("w_branch_attn", [512, 1024])
    w_bs = din("w_branch_ssm", [512, 1024])
    w_out = din("w_out", [1024, 1024])
    fgain = din("final_norm_gain", [1, 1024])
    c_ident = din("c_ident", [128, 128])
    c_anti = din("c_anti", [128, 128])
    c_sval = din("c_sval", [128, 32])
    c_sig = din("c_sig", [128, 1])
    c_mf = din("c_mf", [128, 128])
    c_mb = din("c_mb", [128, 128])
    c_oh = din("c_oh", [33, 512])
    out = nc.dram_tensor("out", [4096, 1024], F32, kind="ExternalOutput").ap()
    fd_t = nc.dram_tensor("fd_scratch", [8, 512], F32, kind="Internal")
    fd = fd_t.ap()
    lsc = nc.dram_tensor("l_scratch", [32, 128, 1024], BF16, kind="Internal").ap()
    dbg_out = {}

    AR = Arena(nc, S, 212736)
    K = 1024
    ps = [nc.alloc_psum_tensor("ps%d" % i, [128, 512], F32) for i in range(8)]

    _bank = [0]
    _bank_mod = [8]

    def nb():
        i = _bank[0] % _bank_mod[0]
        _bank[0] += 1
        return i

    def psf(i):
        return ps[i][:]

    def psb(i):
        return ps[i][:].bitcast(BF16)

    def DMA(q, o, i, reads, writes, slow=False):
        if slow:
            S.add(q, lambda e: e.dma_start(out=o, in_=i, allow_slow_non_contiguous=True), reads, writes, dma=True)
        else:
            S.add(q, lambda e: e.dma_start(out=o, in_=i), reads, writes, dma=True)

    def ACT(o, i, func, reads, writes, **kw):
        S.add("act", lambda e: e.activation(out=o, in_=i, func=func, **kw), reads, writes)

    def TT(eng, o, a, b, op, reads, writes):
        S.add(eng, lambda e: e.tensor_tensor(out=o, in0=a, in1=b, op=op), reads, writes)

    def TS(eng, o, a, s1, s2, op0, op1, reads, writes):
        if op1 is None:
            S.add(eng, lambda e: e.tensor_scalar(out=o, in0=a, scalar1=s1, scalar2=None, op0=op0), reads, writes)
        else:
            S.add(eng, lambda e: e.tensor_scalar(out=o, in0=a, scalar1=s1, scalar2=s2, op0=op0, op1=op1), reads, writes)

    def STT(o, a, sc, b, op0, op1, reads, writes):
        S.add("dve", lambda e: e.scalar_tensor_tensor(out=o, in0=a, scalar=sc, in1=b, op0=op0, op1=op1), reads, writes)

    def CP(eng, o, i, reads, writes):
        if eng == "act":
            ACT(o, i, AF.Copy, reads, writes)
        else:
            S.add(eng, lambda e: e.tensor_copy(out=o, in_=i), reads, writes)

    def MM(lst, reads, writes):
        def fn(e):
            ins = None
            for (o, l, r, st, sp) in lst:
                ins = e.matmul(o, lhsT=l, rhs=r, start=st, stop=sp)
            return ins
        S.add("pe", fn, reads, writes)

    def TRS(lst, reads, writes):
        def fn(e):
            ins = None
            for (o, i, idn) in lst:
                ins = e.transpose(o, in_=i, identity=idn)
            return ins
        S.add("pe", fn, reads, writes)

    def dump(name, ap, shape, dt, reads):
        if name not in dbg:
            return
        t = nc.dram_tensor("dbg_" + name, shape, dt, kind="ExternalOutput").ap()
        dbg_out[name] = t
        DMA("sp", t, ap, reads, ["dbgout:" + name])

    ident_f = AR.view("cst", 0, [128, 128], F32)
    ident_b = AR.view("cst", 512, [128, 128], BF16)
    mf = AR.view("cst", 768, [128, 128], F32)
    mb = AR.view("cst", 1280, [128, 128], F32)
    sval = AR.view("cst", 1792, [128, 32], F32)
    sig = AR.view("cst", 1920, [128, 1], F32)
    epst = AR.view("cst", 1924, [128, 1], F32)
    gainT = AR.view("cst", 1928, [128, 8], F32)
    esink = AR.view("cst", 1960, [128, 8], F32)
    bgT = AR.view("cst", 1992, [128, 16], F32)
    bgluT = AR.view("cst", 2056, [128, 4], F32)
    ss_t = AR.view("cst", 2072, [128, 8], F32)
    anti = AR.view("cst", 2176, [128, 128], F32)
    tab33 = AR.view("cst", 2688, [33, 8], F32)
    fsb = AR.view("cst", 2720, [8, 512], F32)
    CST_END = 5 * K

    DMA("sp", ident_f, c_ident, [], ["cst:ident_f"])
    DMA("sp", anti, c_anti, [], ["cst:anti"])
    DMA("sp", mf, c_mf, [], ["cst:mf"])
    DMA("sp", mb, c_mb, [], ["cst:mb"])
    DMA("sp", sval, c_sval, [], ["cst:sval"])
    DMA("sp", sig, c_sig, [], ["cst:sig"])
    CP("dve", ident_b, ident_f, ["cst:ident_f"], ["cst:ident_b"])
    S.add("dve", lambda e: e.memset(epst, EPS), [], ["cst:eps"])
    vecraw = AR.view("raw", 5 * K + 8 * K + 8704 + 6 * K + 32 * K + 14 * K, [32, 128], F32)
    S.add("dve", lambda e: e.memset(vecraw, 0.0), [], ["raw:vr"])
    DMA("sp", vecraw[0:8, :], norm_gain[0].rearrange("(k p) -> k p", p=128), [], ["raw:vr"])
    DMA("sp", vecraw[8:24, :], b_gate[0].rearrange("(k p) -> k p", p=128), [], ["raw:vr"])
    DMA("sp", vecraw[24:28, :], b_glu[0].rearrange("(k p) -> k p", p=128), [], ["raw:vr"])
    TRS([(ps[7][:, 0:32], vecraw, ident_f[0:32, 0:32])], ["raw:vr", "cst:ident_f"], ["ps7"])
    CP("dve", gainT, ps[7][:, 0:8], ["ps7"], ["cst:gainT"])
    CP("dve", bgT, ps[7][:, 8:24], ["ps7"], ["cst:bgT"])
    CP("dve", bgluT, ps[7][:, 24:28], ["ps7"], ["cst:bgluT"])
    DMA("sp", esink, attn_sink[0:1, :].partition_broadcast(128), [], ["cst:esink"])
    ACT(esink, esink, AF.Exp, ["cst:esink"], ["cst:esink"])

    KT_OFF = CST_END
    VALL_OFF = KT_OFF + 8 * K
    BIAS_OFF = VALL_OFF + 8704
    RA_OFF = BIAS_OFF + 6 * K
    RB_OFF = RA_OFF + 32 * K
    RC_OFF = RB_OFF + 32 * K
    RD_OFF = RC_OFF + 32 * K
    assert RD_OFF % 4 == 0
    kT_all = AR.view("kT", KT_OFF, [128, 4096], BF16)
    v_all = AR.view("vall", VALL_OFF, [128, 32, 2, 65], BF16)
    biasT = AR.view("bias", BIAS_OFF, [128, 2, 3, 4, 128], BF16)

    w_in_v = w_in.rearrange("(k p) e -> p k e", p=128)

    o = RA_OFF
    Wu = AR.view("p1w", o, [128, 8, 512], BF16); o += 8 * K
    Wk = AR.view("p1w", o, [128, 8, 128], BF16); o += 2 * K
    Wv = AR.view("p1w", o, [128, 8, 128], BF16); o += 2 * K
    xs = [AR.view("p1x", o + 4 * K * i, [128, 1024], F32) for i in range(2)]; o += 8 * K
    hn = [AR.view("p1h", o + 2 * K * i, [128, 1024], BF16) for i in range(2)]; o += 4 * K
    hTs = [AR.view("p1t", o + 2 * K * i, [128, 8, 128], BF16) for i in range(2)]; o += 4 * K
    assert o <= RA_OFF + 32 * K
    U_tm = AR.view("utm", RC_OFF, [128, 32, 32, 16], BF16)
    VT_OFF = RB_OFF + 16 * K
    vT_all = AR.view("vT", VT_OFF, [128, 4096], BF16)

    DMA("pool", Wu, w_in_v[:, :, 1280:1792], [], ["p1w:u"])
    DMA("pool", Wk, w_in_v[:, :, 512:640], [], ["p1w:k"])
    DMA("pool", Wv, w_in_v[:, :, 640:768], [], ["p1w:v"])

    x_s = x.rearrange("(n s) d -> s n d", s=32)

    def rms_front(xt, xkey, hnt, hkey, col, reads_x):
        ssc = ss_t[:, col:col + 1]
        ACT(hnt, xt, AF.Square, reads_x, [hkey, "cst:ss%d" % col], accum_out=ssc)
        ACT(ssc, ssc, AF.Sqrt, ["cst:ss%d" % col, "cst:eps"], ["cst:ss%d" % col], scale=1.0 / 1024.0, bias=epst)
        S.add("dve", lambda e: e.reciprocal(out=ssc, in_=ssc), ["cst:ss%d" % col], ["cst:ss%d" % col])
        TS("dve", hnt, xt, ssc, None, ALU.mult, None, reads_x + ["cst:ss%d" % col], [hkey])

    _defer[0] = True
    PHASE(4)
    o = RD_OFF
    def rd(prefix, shape, dt):
        nonlocal o
        esz = 2 if dt == BF16 else 4
        v = AR.view(prefix, o, shape, dt)
        o += (_prod(shape[1:]) * esz + 3) // 4 * 4
        return v

    Pr = rd("tab", [128, 32, 32], F32)
    Pi = rd("tab", [128, 32, 32], F32)
    Qr = rd("tab", [128, 32, 32], F32)
    NQi = rd("tab", [128, 32, 32], F32)
    Bbr = rd("par", [128, 32, 16], F32)
    Bbi = rd("par", [128, 32, 16], F32)
    Cr = rd("par", [128, 32, 16], F32)
    Ci = rd("par", [128, 32, 16], F32)
    dB = rd("par", [128, 512], F32)
    sm = {}
    for nm in ("are", "aim", "dt", "rho", "th", "er", "cs", "sn", "abr", "abi", "den", "cfr", "cfi", "t1", "t2",
               "rhop", "thp", "a32r", "a32i", "e32", "kf"):
        sm[nm] = rd("par", [128, 32], F32)
    ki32 = rd("par", [128, 1024], I32)
    AAf = rd("par", [128, 2, 2, 64], F32)
    Wsc = rd("scn", [128, 2, 2, 64], F32)
    Ssc = rd("scn", [128, 2, 64], F32)
    Ssc2 = rd("scn", [128, 2, 64], F32)
    Lb = [rd("lt", [128, 2, 512], BF16) for _ in range(2)]
    Rb = [rd("rt", [128, 2, 512], BF16) for _ in range(2)]
    tmpD = [rd("tmpd", [128, 512], F32) for _ in range(2)]
    tmpP = [rd("tmpp", [128, 512], F32) for _ in range(2)]
    LTb = [rd("ltt", [128, 4, 2, 128], BF16) for _ in range(2)]
    Tg = [rd("tg", [128, 4, 512], BF16) for _ in range(2)]
    Dg = [rd("dg", [128, 128], BF16) for _ in range(2)]
    bt1 = [rd("bt", [128, 128], F32) for _ in range(2)]
    bt2 = [rd("bt", [128, 128], F32) for _ in range(2)]
    g_arg = rd("gen", [128, 32, 32], F32)
    g_phi = rd("gen", [128, 32, 32], F32)
    g_sn = rd("gen", [128, 32, 32], F32)
    g_cs = rd("gen", [128, 32, 32], F32)
    assert o <= AR.nbytes, o
    GEN_OFF = o - 16 * K
    Braw_r = AR.view("raw", RB_OFF, [128, 32, 16], F32)
    Braw_i = AR.view("raw", RB_OFF + 2 * K, [128, 32, 16], F32)
    Craw_r = AR.view("raw", RB_OFF + 4 * K, [128, 4, 128], F32)
    Craw_i = AR.view("raw", RB_OFF + 6 * K, [128, 4, 128], F32)
    Araw_r = AR.view("raw", RB_OFF + 8 * K, [32, 128], F32)
    Araw_i = AR.view("raw", RB_OFF + 8 * K + 512, [32, 128], F32)

    for d in range(2):
        DMA("sp", Braw_r[64 * d:64 * d + 64], b_re[d].rearrange("g p c -> p g c"), [], ["raw:br%d" % d])
        DMA("sp", Braw_i[64 * d:64 * d + 64], b_im[d].rearrange("g p c -> p g c"), [], ["raw:bi%d" % d])
        DMA("sp", Araw_r[:, 64 * d:64 * d + 64], a_re[d], [], ["raw:ar%d" % d])
        DMA("sp", Araw_i[:, 64 * d:64 * d + 64], a_im[d], [], ["raw:ai%d" % d])
        DMA("sp", sm["dt"][64 * d:64 * d + 64, :], log_dt[d:d + 1, :].partition_broadcast(64), [], ["par:dt%d" % d])
        for t in range(4):
            DMA("sp", Craw_r[:, t, 64 * d:64 * d + 64],
                c_re[d].rearrange("g c p -> (g c) p")[128 * t:128 * t + 128, :], [], ["raw:cr%d%d" % (d, t)])
            DMA("sp", Craw_i[:, t, 64 * d:64 * d + 64],
                c_im[d].rearrange("g c p -> (g c) p")[128 * t:128 * t + 128, :], [], ["raw:ci%d%d" % (d, t)])
    DMA("sp", dB, ssm_d[0:1, :].partition_broadcast(128), [], ["par:dB"])
    p6 = psf(6)
    TRS([(p6[:, 0:32], Araw_r, ident_f[0:32, 0:32]), (p6[:, 32:64], Araw_i, ident_f[0:32, 0:32])],
        ["raw:ar0", "raw:ar1", "raw:ai0", "raw:ai1", "cst:ident_f"], ["ps6"])
    CP("dve", sm["are"], p6[:, 0:32], ["ps6"], ["par:are"])
    CP("dve", sm["aim"], p6[:, 32:64], ["ps6"], ["par:aim"])
    p7 = psf(7)
    TRS([(p7[:, 128 * t:128 * t + 128], Craw_r[:, t, :], ident_f) for t in range(4)],
        ["raw:cr%d%d" % (d, t) for d in range(2) for t in range(4)] + ["cst:ident_f"], ["ps7"])
    CP("dve", Cr.rearrange("p g c -> p (g c)"), p7, ["ps7"], ["par:Cr"])
    TRS([(p6[:, 128 * t:128 * t + 128], Craw_i[:, t, :], ident_f) for t in range(4)],
        ["raw:ci%d%d" % (d, t) for d in range(2) for t in range(4)] + ["cst:ident_f"], ["ps6"])
    CP("dve", Ci.rearrange("p g c -> p (g c)"), p6, ["ps6"], ["par:Ci"])

    PK = ["par:small"]

    def sincos(phi, sn_o, cs_o, kf, n, keyr, keyw):
        ki = ki32[:, 0:n]
        for (dst, shift) in ((sn_o, 0.0), (cs_o, math.pi / 2)):
            TS("dve", ki, phi, 1.0 / TWO_PI, shift / TWO_PI, ALU.mult, ALU.add, keyr, keyw)
            CP("dve", kf, ki, keyw, keyw)
            STT(dst, kf, -TWO_PI, phi, ALU.mult, ALU.add, keyr + keyw, keyw)
            TS("dve", dst, dst, shift, None, ALU.add, None, keyw, keyw)
            TS("dve", dst, dst, 3.14159, -3.14159, ALU.min, ALU.max, keyw, keyw)
            ACT(dst, dst, AF.Sin, keyw, keyw)

    rk = ["par:are", "par:aim", "par:dt0", "par:dt1"]
    ACT(sm["dt"], sm["dt"], AF.Exp, ["par:dt0", "par:dt1"], PK)
    TT("dve", sm["rho"], sm["are"], sm["dt"], ALU.mult, rk + PK, PK)
    TT("dve", sm["th"], sm["aim"], sm["dt"], ALU.mult, rk + PK, PK)
    sincos(sm["th"], sm["sn"], sm["cs"], sm["kf"], 32, PK, PK)
    ACT(sm["er"], sm["rho"], AF.Exp, PK, PK)
    TT("dve", sm["abr"], sm["er"], sm["cs"], ALU.mult, PK, PK)
    TT("dve", sm["abi"], sm["er"], sm["sn"], ALU.mult, PK, PK)
    TS("dve", sm["abr"], sm["abr"], -1.0, None, ALU.add, None, PK, PK)
    TT("dve", sm["t1"], sm["are"], sm["are"], ALU.mult, rk + PK, PK)
    TT("dve", sm["t2"], sm["aim"], sm["aim"], ALU.mult, rk + PK, PK)
    TT("dve", sm["den"], sm["t1"], sm["t2"], ALU.add, PK, PK)
    S.add("dve", lambda e: e.reciprocal(out=sm["den"], in_=sm["den"]), PK, PK)
    TT("dve", sm["t1"], sm["abr"], sm["are"], ALU.mult, rk + PK, PK)
    TT("dve", sm["t2"], sm["abi"], sm["aim"], ALU.mult, rk + PK, PK)
    TT("dve", sm["t1"], sm["t1"], sm["t2"], ALU.add, PK, PK)
    TT("dve", sm["cfr"], sm["t1"], sm["den"], ALU.mult, PK, PK)
    TT("dve", sm["t1"], sm["abi"], sm["are"], ALU.mult, rk + PK, PK)
    TT("dve", sm["t2"], sm["abr"], sm["aim"], ALU.mult, rk + PK, PK)
    TT("dve", sm["t1"], sm["t1"], sm["t2"], ALU.subtract, PK, PK)
    TT("dve", sm["cfi"], sm["t1"], sm["den"], ALU.mult, PK, PK)
    rawb = ["raw:br0", "raw:br1", "raw:bi0", "raw:bi1"]
    cfr_b = sm["cfr"].unsqueeze(2).broadcast_to([128, 32, 16])
    cfi_b = sm["cfi"].unsqueeze(2).broadcast_to([128, 32, 16])
    t512a = tmpD[0].rearrange("p (g c) -> p g c", g=32)
    t512b = tmpD[1].rearrange("p (g c) -> p g c", g=32)
    TT("dve", t512a, Braw_r, cfr_b, ALU.mult, rawb + PK, ["tmpd:0"])
    TT("dve", t512b, Braw_i, cfi_b, ALU.mult, rawb + PK, ["tmpd:1"])
    TT("dve", Bbr, t512a, t512b, ALU.subtract, ["tmpd:0", "tmpd:1"], ["par:Bb"])
    TT("dve", t512a, Braw_r, cfi_b, ALU.mult, rawb + PK, ["tmpd:0"])
    TT("dve", t512b, Braw_i, cfr_b, ALU.mult, rawb + PK, ["tmpd:1"])
    TT("dve", Bbi, t512a, t512b, ALU.add, ["tmpd:0", "tmpd:1"], ["par:Bb"])
    TS("dve", sm["t1"], sm["th"], 32.0, None, ALU.mult, None, PK, PK)
    sincos(sm["t1"], sm["a32i"], sm["a32r"], sm["kf"], 32, PK, PK)
    ACT(sm["e32"], sm["rho"], AF.Exp, PK, PK, scale=32.0)
    TT("dve", sm["a32r"], sm["a32r"], sm["e32"], ALU.mult, PK, PK)
    TT("dve", sm["a32i"], sm["a32i"], sm["e32"], ALU.mult, PK, PK)
    TS("dve", sm["t2"], sm["a32i"], -1.0, None, ALU.mult, None, PK, PK)
    AAv = AAf.rearrange("p o t (g q) -> p o t g q", q=2)
    for (oo, tt_, src) in ((0, 0, "a32r"), (0, 1, "t2"), (1, 0, "a32i"), (1, 1, "a32r")):
        CP("dve", AAv[:, oo, tt_, :, :], sm[src].unsqueeze(2).broadcast_to([128, 32, 2]), PK, ["par:AAf"])
    TS("dve", sm["rhop"], sm["rho"], sig, None, ALU.mult, None, PK + ["cst:sig"], PK)
    TS("dve", sm["thp"], sm["th"], sig, None, ALU.mult, None, PK + ["cst:sig"], PK)
    sval_b = sval.unsqueeze(1).broadcast_to([128, 32, 32])
    GK = ["gen:all"]
    TT("dve", g_arg, sm["rhop"].unsqueeze(2).broadcast_to([128, 32, 32]), sval_b, ALU.mult, PK + ["cst:sval"], GK)
    TT("dve", g_phi, sm["thp"].unsqueeze(2).broadcast_to([128, 32, 32]), sval_b, ALU.mult, PK + ["cst:sval"], GK)
    fl = lambda a: a.rearrange("p g s -> p (g s)")
    sincos(fl(g_phi), fl(g_sn), fl(g_cs), fl(Pr), 1024, GK, GK + ["tab:all"])
    ACT(fl(g_phi), fl(g_arg), AF.Exp, GK, GK)
    ACT(fl(g_arg), fl(g_arg), AF.Exp, GK, GK, scale=-1.0)
    TT("dve", Pr, g_phi, g_cs, ALU.mult, GK, ["tab:all"])
    TT("dve", Pi, g_phi, g_sn, ALU.mult, GK, ["tab:all"])
    TT("dve", Qr, g_arg, g_cs, ALU.mult, GK, ["tab:all"])
    TT("dve", NQi, g_arg, g_sn, ALU.mult, GK, ["tab:all"])
    dump("Pr", Pr, [128, 32, 32], F32, ["tab:all"])
    dump("Bbr", Bbr, [128, 32, 16], F32, ["par:Bb"])
    for nm in ("are", "aim", "dt", "rho", "th", "er", "cs", "sn", "abr", "abi", "den", "cfr", "cfi"):
        dump("sm_" + nm, sm[nm], [128, 32], F32, PK + ["par:are", "par:aim"])
    dump("Braw", Braw_r, [128, 32, 16], F32, rawb)
    dump("Cr", Cr, [128, 32, 16], F32, ["par:Cr"])

    PHASE(5)
    _bias_split = len(_pending)
    DMA("sp", tab33[0:32, :], rel_tab, [], ["cst:tab33a"])
    S.add("dve", lambda e: e.memset(tab33[32:33, :], -30000.0), [], ["cst:tab33b"])
    ohs = AR.view("raw", RB_OFF + 10 * K, [33, 512], F32)
    DMA("sp", ohs, c_oh, [], ["raw:oh"])
    MM([(p7[0:8, :], tab33, ohs, True, True)], ["cst:tab33a", "cst:tab33b", "raw:oh", "par:Cr"], ["ps7"])
    CP("dve", fsb, p7[0:8, :], ["ps7"], ["cst:fsb"])
    DMA("sp", fd, fsb, ["cst:fsb"], ["fd:all"])
    hk = [AR.view("raw", RB_OFF + 12 * K + 512 * i, [128, 128], F32) for i in range(4)]
    for h in range(8):
        for jk in range(3):
            i = (h * 3 + jk) % 4
            src = bass.AP(fd_t, h * 512 + 128 * jk, [[1, 128], [1, 128]])
            DMA("sp", hk[i], src, ["fd:all"], ["raw:hk%d" % i])
            pb_i = 6 + (h * 3 + jk) % 2
            MM([(psf(pb_i)[:, 0:128], hk[i], anti, True, True)], ["raw:hk%d" % i, "cst:anti"], ["ps%d" % pb_i])
            CP("act", biasT[:, h // 4, jk, h % 4, :], psf(pb_i)[:, 0:128], ["ps%d" % pb_i], ["bias:all"])
    dump("bias", biasT, [128, 2, 3, 4, 128], BF16, ["bias:all"])

    _defer[0] = False
    PHASE(1)
    gain_b = gainT.unsqueeze(2).broadcast_to([128, 8, 128])

    p1_bank = {}

    def p1_front1(s):
        b = s % 2
        DMA("sp", xs[b], x_s[s], [], ["p1x:%d" % b])
        rms_front(xs[b], "p1x:%d" % b, hn[b], "p1h:%d" % b, b, ["p1x:%d" % b])
        bi = nb()
        p1_bank[("T", s)] = bi
        pT = psb(bi)
        TRS([(pT[:, 128 * k:128 * k + 128], hn[b][:, 128 * k:128 * k + 128], ident_b) for k in range(8)],
            ["p1h:%d" % b, "cst:ident_b"], ["ps%d" % bi])

    def p1_front2(s):
        b = s % 2
        bi = p1_bank[("T", s)]
        TT("dve", hTs[b], psb(bi).rearrange("p (k n) -> p k n", k=8), gain_b, ALU.mult,
           ["ps%d" % bi, "cst:gainT"], ["p1t:%d" % b])

    def p1_mm(s):
        b = s % 2
        bu = nb()
        MM([(psf(bu), hTs[b][:, k, :], Wu[:, k, :], k == 0, k == 7) for k in range(8)],
           ["p1t:%d" % b, "p1w:u"], ["ps%d" % bu])
        bk = nb()
        pkv = psf(bk)
        MM([(pkv[:, 0:128], Wk[:, k, :], hTs[b][:, k, :], k == 0, k == 7) for k in range(8)] +
           [(pkv[:, 128:256], Wv[:, k, :], hTs[b][:, k, :], k == 0, k == 7) for k in range(8)],
           ["p1t:%d" % b, "p1w:k", "p1w:v"], ["ps%d" % bk])
        p1_bank[("U", s)] = bu
        p1_bank[("KV", s)] = bk

    def p1_copy(s):
        bu, bk = p1_bank[("U", s)], p1_bank[("KV", s)]
        pkv = psf(bk)
        CP("act", U_tm[:, :, s, :], psf(bu).rearrange("p (g c) -> p g c", g=32), ["ps%d" % bu], ["utm:%d" % s])
        CP("act", kT_all[:, s::32], pkv[:, 0:128], ["ps%d" % bk], ["kT:all"])
        CP("act", vT_all[:, s::32], pkv[:, 128:256], ["ps%d" % bk], ["vT:all"])

    _bank_mod[0] = 6
    PREP_FIRST = True
    if PREP_FIRST:
        flush(_bias_split)
    p1_front1(0)
    p1_front1(1)
    p1_front2(0)
    for s in range(32):
        if s + 2 < 32:
            p1_front1(s + 2)
        if s + 1 < 32:
            p1_front2(s + 1)
        p1_mm(s)
        if s >= 1:
            p1_copy(s - 1)
        flush(0)
    p1_copy(31)
    flush()
    _bank_mod[0] = 8
    dump("utm", U_tm, [128, 32, 32, 16], BF16, ["utm:%d" % s for s in range(32)])
    dump("kT", kT_all, [128, 4096], BF16, ["kT:all"])

    PHASE(2)
    S.add("pool", lambda e: e.memset(v_all[:, :, :, 64:65], 1.0), [], ["vall:ones"])
    for j in range(4):
        bi = nb()
        pb_ = psb(bi)
        TRS([(pb_[:, 128 * i:128 * i + 128], vT_all[:, 128 * (8 * j + i):128 * (8 * j + i) + 128], ident_b)
             for i in range(8)], ["vT:all", "cst:ident_b"], ["ps%d" % bi])
        CP("act", v_all[:, 8 * j:8 * j + 8, :, 0:64],
           pb_.rearrange("p (i h d) -> p i h d", i=8, h=2), ["ps%d" % bi], ["vall:%d" % j])
    dump("vall", v_all, [128, 32, 2, 65], BF16, ["vall:%d" % j for j in range(4)] + ["vall:ones"])

    PHASE(3)
    U_g = AR.view("ug", RA_OFF, [128, 32, 4, 128], BF16)
    for g2 in range(16):
        bi = nb()
        pb_ = psb(bi)
        lst = []
        for gi in range(2):
            g = 2 * g2 + gi
            for k in range(4):
                lst.append((pb_[:, (gi * 4 + k) * 128:(gi * 4 + k) * 128 + 128],
                            U_tm[:, g, 8 * k:8 * k + 8, :].rearrange("p s c -> p (s c)"), ident_b))
        TRS(lst, ["utm:%d" % s for s in range(32)] + ["cst:ident_b"], ["ps%d" % bi])
        CP("act" if g2 % 2 == 0 else "dve", U_g[:, 2 * g2:2 * g2 + 2, :, :],
           pb_.rearrange("p (g k n) -> p g k n", g=2, k=4), ["ps%d" % bi], ["ug:%d" % g2])
    UG_ALL = ["ug:%d" % i for i in range(16)]

    def gen_L(g, b):
        Prg = Pr[:, g, :].unsqueeze(2).broadcast_to([128, 32, 16])
        Pig = Pi[:, g, :].unsqueeze(2).broadcast_to([128, 32, 16])
        Brg = Bbr[:, g, :].unsqueeze(1).broadcast_to([128, 32, 16])
        Big = Bbi[:, g, :].unsqueeze(1).broadcast_to([128, 32, 16])
        v = lambda t: t.rearrange("p (s c) -> p s c", s=32)
        rkeys = ["tab:all", "par:Bb"]
        TT("dve", v(tmpP[0]), Prg, Big, ALU.mult, rkeys, ["tmpp:0"])
        TT("pool", v(tmpP[1]), Pig, Brg, ALU.mult, rkeys, ["tmpp:1"])
        TT("dve", v(tmpD[0]), Prg, Brg, ALU.mult, rkeys, ["tmpd:0"])
        TT("dve", v(tmpD[1]), Pig, Big, ALU.mult, rkeys, ["tmpd:1"])
        TT("pool", Lb[b][:, 1, :], tmpP[0], tmpP[1], ALU.add, ["tmpp:0", "tmpp:1"], ["lt:%di" % b])
        TT("dve", Lb[b][:, 0, :], tmpD[0], tmpD[1], ALU.subtract, ["tmpd:0", "tmpd:1"], ["lt:%dr" % b])

    def gen_R(g, b):
        Qrg = Qr[:, g, :].unsqueeze(2).broadcast_to([128, 32, 16])
        NQg = NQi[:, g, :].unsqueeze(2).broadcast_to([128, 32, 16])
        Crg = Cr[:, g, :].unsqueeze(1).broadcast_to([128, 32, 16])
        Cig = Ci[:, g, :].unsqueeze(1).broadcast_to([128, 32, 16])
        v = lambda t: t.rearrange("p (s c) -> p s c", s=32)
        rkeys = ["tab:all", "par:Cr", "par:Ci"]
        TT("dve", v(tmpD[0]), Crg, Qrg, ALU.mult, rkeys, ["tmpd:0"])
        TT("dve", v(tmpD[1]), Cig, NQg, ALU.mult, rkeys, ["tmpd:1"])
        TT("pool", v(tmpP[0]), Crg, NQg, ALU.mult, rkeys, ["tmpp:0"])
        TT("dve", Rb[b][:, 0, :], tmpD[0], tmpD[1], ALU.add, ["tmpd:0", "tmpd:1"], ["rt:%dr" % b])
        TT("pool", v(tmpP[1]), Cig, Qrg, ALU.mult, rkeys, ["tmpp:1"])
        TT("pool", Rb[b][:, 1, :], tmpP[0], tmpP[1], ALU.subtract, ["tmpp:0", "tmpp:1"], ["rt:%di" % b])

    PHASE(6)
    Z = AR.view("z", RB_OFF, [128, 2, 32, 128], F32)
    for g in range(32):
        b = g % 2
        gen_L(g, b)
        bl = nb()
        pl_ = psb(bl)
        plv = pl_.rearrange("p (k r m) -> p k r m", k=4, r=2)
        TRS([(plv[:, k, ri, :], Lb[b][:, ri, 128 * k:128 * k + 128], ident_b) for k in range(4) for ri in range(2)],
            ["lt:%dr" % b, "lt:%di" % b, "cst:ident_b"], ["ps%d" % bl])
        CP("act", LTb[b].rearrange("p k r m -> p (k r m)"), pl_, ["ps%d" % bl], ["ltt:%d" % b])
        DMA("sp", lsc[g], Lb[b].rearrange("p r m -> p (r m)"), ["lt:%dr" % b, "lt:%di" % b], ["lsc:%d" % g])
        bz = nb()
        pz = psf(bz)
        MM([(pz[:, 128 * ri:128 * ri + 128], LTb[b][:, k, ri, :], U_g[:, g, k, :], k == 0, k == 3)
            for ri in range(2) for k in range(4)], ["ltt:%d" % b] + UG_ALL, ["ps%d" % bz])
        CP("act", Z[:, :, g, :], pz[:, 0:256].rearrange("p (r n) -> p r n", r=2), ["ps%d" % bz], ["z:%d" % g])
    Z_ALL = ["z:%d" % g for g in range(32)]
    dump("Z", Z, [128, 2, 32, 128], F32, Z_ALL)

    PHASE(7)
    Xd = AR.view("xd", GEN_OFF, [128, 2, 32, 128], BF16)
    Zv = Z.rearrange("p r g (q j) -> p r (g q) j", q=2)
    Xv = Xd.rearrange("p r g (q j) -> p r (g q) j", q=2)
    S.add("dve", lambda e: e.memset(Xv[0:64, :, :, 0:1], 0.0), [], ["xd:f"])
    S.add("pool", lambda e: e.memset(Xv[64:128, :, :, 63:64], 0.0), [], ["xd:b"])
    for step in range(1, 64):
        for (eng, lo, hi, cur, prev, tag) in (("dve", 0, 64, step, step - 1, "f"), ("pool", 64, 128, 63 - step, 64 - step, "b")):
            W_ = Wsc[lo:hi]
            sp_ = step % 2
            S_ = (Ssc if sp_ == 0 else Ssc2)[lo:hi]
            tag2 = tag + str(sp_)
            Xp = Zv[lo:hi, :, :, prev].unsqueeze(1).broadcast_to([64, 2, 2, 64])
            TT(eng, W_, AAf[lo:hi], Xp, ALU.mult, Z_ALL + ["par:AAf", "z:scan" + tag], ["scn:w" + tag])
            TT(eng, S_, W_[:, :, 0, :], W_[:, :, 1, :], ALU.add, ["scn:w" + tag], ["scn:s" + tag2])
            TT(eng, Zv[lo:hi, :, :, cur], Zv[lo:hi, :, :, cur], S_, ALU.add, ["scn:s" + tag2] + Z_ALL, ["z:scan" + tag])
            CP("act", Xv[lo:hi, :, :, cur], S_, ["scn:s" + tag2], ["xd:" + tag])
    dump("Xd", Xd, [128, 2, 32, 128], BF16, ["xd:f", "xd:b"])

    PHASE(8)
    yg = AR.view("yg", RC_OFF, [128, 32, 512], BF16)
    ident3 = ident_f.rearrange("p (t c) -> p t c", t=8)
    for g in range(32):
        b = g % 2
        DMA("sp", Lb[b].rearrange("p r m -> p (r m)"), lsc[g], ["lsc:%d" % g], ["lt:%dr" % b, "lt:%di" % b])
        gen_R(g, b)
        TT("dve", Dg[b].rearrange("p (t c) -> p t c", t=8), ident3,
           dB[:, 16 * g:16 * g + 16].unsqueeze(1).broadcast_to([128, 8, 16]), ALU.mult,
           ["cst:ident_f", "par:dB"], ["dg:%d" % b])
        lk = ["lt:%dr" % b, "lt:%di" % b, "rt:%dr" % b, "rt:%di" % b]
        pf_i, pb_i = nb(), nb()
        pf, pbk = psf(pf_i), psf(pb_i)
        MM([(pf, Lb[b][0:64, ri, 0:128], Rb[b][0:64, ri, :], ri == 0, ri == 1) for ri in range(2)] +
           [(pbk, Lb[b][64:128, ri, 384:512], Rb[b][64:128, ri, :], ri == 0, ri == 1) for ri in range(2)],
           lk, ["ps%d" % pf_i, "ps%d" % pb_i])
        TB = Tg[b].rearrange("p k m -> p (k m)")[:, 0:896].rearrange("p (m q) -> p m q", m=7)
        CP("act", TB[:, 4:7, :], pf[:, 128:512].rearrange("p (m q) -> p m q", m=3), ["ps%d" % pf_i], ["tg:%d_f" % b])
        CP("act", TB[:, 0:3, :], pbk[:, 0:384].rearrange("p (m q) -> p m q", m=3), ["ps%d" % pb_i], ["tg:%d_b" % b])
        TT("dve", bt1[0], pf[:, 0:128], mf, ALU.mult, ["ps%d" % pf_i, "cst:mf"], ["bt:1_0"])
        TT("dve", bt2[0], pbk[:, 384:512], mb, ALU.mult, ["ps%d" % pb_i, "cst:mb"], ["bt:2_0"])
        TT("dve", TB[:, 3, :], bt1[0], bt2[0], ALU.add, ["bt:1_0", "bt:2_0"], ["tg:%d_d" % b])
        py_i = nb()
        py = psf(py_i)
        MM([(py, Xd[:, ri, g, :], Rb[b][:, ri, :], ri == 0, False) for ri in range(2)] +
           [(py[:, 128 * j:128 * j + 128], U_g[:, g, k, :], TB[:, j - k + 3, :], False, False)
            for k in range(4) for j in range(4)] +
           [(py[:, 128 * k:128 * k + 128], U_g[:, g, k, :], Dg[b], False, k == 3) for k in range(4)],
           UG_ALL + ["tg:%d_f" % b, "tg:%d_b" % b, "tg:%d_d" % b, "dg:%d" % b, "xd:f", "xd:b", "rt:%dr" % b, "rt:%di" % b],
           ["ps%d" % py_i])
        ACT(yg[:, :, 16 * g:16 * g + 16], py.rearrange("p (t c) -> p t c", t=32), AF.Gelu_apprx_tanh,
            ["ps%d" % py_i], ["yg:%d" % g])
    YG_ALL = ["yg:%d" % g for g in range(32)]
    dump("tg", Tg[1], [128, 4, 512], BF16, ["tg:1_f", "tg:1_b", "tg:1_d"])
    dump("yg", yg, [128, 32, 512], BF16, YG_ALL)

    PHASE(9)
    ygT = AR.view("ygT", RB_OFF, [128, 4, 4096], BF16)
    ygT_v = ygT.rearrange("p c (n t) -> p c n t", t=32)
    cnt = 0
    for c in range(4):
        for tb in range(4):
            pi_ = nb()
            pb_ = psb(pi_)
            TRS([(pb_[:, 128 * i:128 * i + 128], yg[:, 8 * tb + i, 128 * c:128 * c + 128], ident_b) for i in range(8)],
                YG_ALL + ["cst:ident_b"], ["ps%d" % pi_])
            CP("act" if cnt % 2 == 0 else "dve", ygT_v[:, c, :, 8 * tb:8 * tb + 8],
               pb_.rearrange("p (i n) -> p n i", i=8), ["ps%d" % pi_], ["ygT:%d" % cnt])
            cnt += 1
    YGT_ALL = ["ygT:%d" % i for i in range(16)]
    dump("ygT", ygT, [128, 4, 4096], BF16, YGT_ALL)

    PHASE(10)
    attnT = AR.view("attnT", RA_OFF, [128, 4, 4096], BF16)
    o = RC_OFF
    def rc(prefix, shape, dt):
        nonlocal o
        esz = 2 if dt == BF16 else 4
        v = AR.view(prefix, o, shape, dt)
        o += (_prod(shape[1:]) * esz + 3) // 4 * 4
        return v
    Wq = rc("w2a", [128, 8, 512], BF16)
    Wza = rc("w2a", [128, 8, 512], BF16)
    xt2 = [rc("x2", [128, 1024], F32) for _ in range(2)]
    hn2 = [AR.view("h2", RD_OFF + 76 * K + 2 * K * i_, [128, 1024], BF16) for i_ in range(4)]
    hTb = [rc("hT2", [128, 8, 512], BF16) for _ in range(2)]
    qT = rc("qT", [128, 8, 512], BF16)
    zas = rc("zas", [128, 4, 512], BF16)
    PT = [rc("pt", [128, 3, 512], BF16) for _ in range(2)]
    den = [rc("den", [128, 8], F32) for _ in range(2)]
    at = [rc("at", [128, 256], F32) for _ in range(2)]
    ag = [rc("ag", [128, 512], BF16) for _ in range(2)]
    assert o <= RD_OFF + 32 * K, o
    A2_END = o
    Wq_p = Wq.rearrange("p k (g h d) -> p k g h d", g=4, h=2)
    for h_ in range(2):
        for g_ in range(4):
            c0_ = 256 * h_ + 64 * g_
            DMA("pool", Wq_p[:, :, g_, h_, :], w_in_v[:, :, c0_:c0_ + 64], [], ["w2a:q"])
    DMA("pool", Wza, w_in_v[:, :, 768:1280], [], ["w2a:za"])
    S.add("dve", lambda e: e.memset(qT, 0.0), [], ["qT:zero"] + ["qT:%d" % h_ for h_ in range(8)])

    def front(tok0, ntile, hT_dst, hkey, xt_l, hn_l, xpre, hpre, psbase, cnt0):
        for i in range(ntile):
            bb = (cnt0 + i) % 2
            DMA("sp", xt_l[bb], x[tok0 + 128 * i: tok0 + 128 * i + 128, :], [], ["%s:%d" % (xpre, bb)])
            rms_front(xt_l[bb], None, hn_l[bb], "%s:%d" % (hpre, bb), 2 + bb, ["%s:%d" % (xpre, bb)])
            pi_ = nb()
            pT_ = psb(pi_)
            TRS([(pT_[:, 128 * k:128 * k + 128], hn_l[bb][:, 128 * k:128 * k + 128], ident_b) for k in range(8)],
                ["%s:%d" % (hpre, bb), "cst:ident_b"], ["ps%d" % pi_])
            TT("dve", hT_dst[:, :, 128 * i:128 * i + 128], pT_.rearrange("p (k n) -> p k n", k=8), gain_b, ALU.mult,
               ["ps%d" % pi_, "cst:gainT"], [hkey + "_%d" % i])

    ucount = [0]
    pend_tr = []
    unit_hooks = {}

    def attention_block(jb):
        bb = jb % 2
        hT_ = hTb[bb]
        hkeys = ["hT2:%d_%d" % (bb, i) for i in range(4)]
        for hq in range(4):
            bi = nb()
            MM([(psf(bi), Wq[:, k, 128 * hq:128 * hq + 128], hT_[:, k, :], k == 0, k == 7) for k in range(8)],
               hkeys + ["w2a:q"], ["ps%d" % bi])
            TS("dve", qT[0:64, hq, :], psf(bi)[0:64, :], 0.125, None, ALU.mult, None, ["ps%d" % bi, "qT:zero"], ["qT:%d" % hq])
            TS("dve", qT[64:128, 4 + hq, :], psf(bi)[64:128, :], 0.125, None, ALU.mult, None, ["ps%d" % bi, "qT:zero"],
               ["qT:%d" % (4 + hq)])
        for tt in range(4):
            bi = nb()
            MM([(psf(bi), hT_[:, k, 128 * tt:128 * tt + 128], Wza[:, k, :], k == 0, k == 7) for k in range(8)],
               hkeys + ["w2a:za"], ["ps%d" % bi])
            ACT(zas[:, tt, :], psf(bi), AF.Silu, ["ps%d" % bi], ["zas:%d" % tt])

        def scores(tt, kvh):
            qi = 4 * jb + tt
            bi_ = qi % 16
            jks = [jk for jk in range(3) if 0 <= bi_ + jk - 1 <= 15]
            u = ucount[0]
            ucount[0] += 1
            pb2 = u % 2
            for jk in jks:
                kt = qi + jk - 1
                sb = nb()
                lst = [(psf(sb), ident_b, biasT[:, kvh, jk, :, :].rearrange("p g q -> p (g q)"), True, False)]
                for g in range(4):
                    h = 4 * kvh + g
                    lst.append((psf(sb)[:, 128 * g:128 * g + 128], kT_all[:, 128 * kt:128 * kt + 128],
                                qT[:, h, 128 * tt:128 * tt + 128], False, g == 3))
                MM(lst, ["kT:all", "bias:all", "cst:ident_b"] + ["qT:%d" % (4 * kvh + g) for g in range(4)], ["ps%d" % sb])
                ACT(PT[pb2][:, jk, :], psf(sb), AF.Exp, ["ps%d" % sb], ["pt:%d_%d" % (pb2, jk)])
            return (tt, kvh, qi, jks, pb2)

        def bias_mults(st):
            (tt, kvh, qi, jks, pb2) = st
            for jk in jks:
                ptv = PT[pb2][:, jk, :].rearrange("p (g q) -> p g q", g=4)
                TT("dve", ptv, ptv, biasT[:, 4 * kvh:4 * kvh + 4, jk, :], ALU.mult,
                   ["pt:%d_%d" % (pb2, jk), "bias:all"], ["pt:%d_%d" % (pb2, jk)])

        def pv_post(st):
            (tt, kvh, qi, jks, pb2) = st
            ab = qi % 2
            pvb = nb()
            pv = psf(pvb)[:, 0:260].rearrange("p (g d) -> p g d", g=4)
            lst = []
            for g in range(4):
                for jk in jks:
                    kt = qi + jk - 1
                    lst.append((pv[:, g, :], PT[pb2][:, jk, 128 * g:128 * g + 128], v_all[:, kt, kvh, :],
                                jk == jks[0], jk == jks[-1]))
            MM(lst, ["pt:%d_%d" % (pb2, jk) for jk in jks] + ["vall:ones"] + ["vall:%d" % j for j in range(4)], ["ps%d" % pvb])
            dn = den[pb2]
            TT("dve", dn[:, 0:4], pv[:, :, 64], esink[:, 4 * kvh:4 * kvh + 4], ALU.add, ["ps%d" % pvb, "cst:esink"], ["den:%d" % pb2])
            S.add("dve", lambda e, dn=dn: e.reciprocal(out=dn[:, 4:8], in_=dn[:, 0:4]), ["den:%d" % pb2], ["den:%d" % pb2])
            atv = at[pb2].rearrange("p (g d) -> p g d", g=4)
            TT("dve", atv, pv[:, :, 0:64], dn[:, 4:8].unsqueeze(2).broadcast_to([128, 4, 64]), ALU.mult,
               ["ps%d" % pvb, "den:%d" % pb2], ["at:%d" % pb2])
            TT("dve", ag[ab][:, 256 * kvh:256 * kvh + 256], at[pb2], zas[:, tt, 256 * kvh:256 * kvh + 256], ALU.mult,
               ["at:%d" % pb2, "zas:%d" % tt], ["ag:%d_%d" % (ab, kvh)])
            if kvh == 1:
                pend_tr.append((ab, qi))

        def do_tr(n):
            for _ in range(n):
                (ab, qi) = pend_tr.pop(0)
                pt_i = nb()
                TRS([(psb(pt_i)[:, 128 * c:128 * c + 128], ag[ab][:, 128 * c:128 * c + 128], ident_b) for c in range(4)],
                    ["ag:%d_0" % ab, "ag:%d_1" % ab, "cst:ident_b"], ["ps%d" % pt_i])
                CP("dve", attnT[:, :, 128 * qi:128 * qi + 128], psb(pt_i)[:, 0:512].rearrange("p (c n) -> p c n", c=4),
                   ["ps%d" % pt_i], ["attnT:%d" % qi])

        units = [(tt, kvh) for tt in range(4) for kvh in range(2)]
        pend = [scores(*units[0])]
        for i in range(len(units)):
            if i + 1 < len(units):
                pend.append(scores(*units[i + 1]))
            had = list(pend_tr)
            pv_post(pend.pop(0))
            if had:
                do_tr(len(had))
            for fn_ in unit_hooks.get(i, ()):
                fn_()

    def f2_rms(jb):
        for i in range(4):
            DMA("sp", xt2[i % 2], x[512 * jb + 128 * i: 512 * jb + 128 * i + 128, :], [], ["x2:%d" % (i % 2)])
            rms_front(xt2[i % 2], None, hn2[i], "h2:%d" % i, 2 + i % 2, ["x2:%d" % (i % 2)])

    def f2_T(jb):
        for i in range(4):
            pi_ = nb()
            pT_ = psb(pi_)
            TRS([(pT_[:, 128 * k:128 * k + 128], hn2[i][:, 128 * k:128 * k + 128], ident_b) for k in range(8)],
                ["h2:%d" % i, "cst:ident_b"], ["ps%d" % pi_])
            TT("dve", hTb[jb % 2][:, :, 128 * i:128 * i + 128], pT_.rearrange("p (k n) -> p k n", k=8), gain_b, ALU.mult,
               ["ps%d" % pi_, "cst:gainT"], ["hT2:%d_%d" % (jb % 2, i)])

    f2_rms(0)
    f2_T(0)
    f2_rms(1)
    for jb in range(8):
        unit_hooks.clear()
        if jb + 1 < 8:
            f2_T(jb + 1)
        if jb + 2 < 8:
            unit_hooks[3] = [lambda nj=jb + 2: f2_rms(nj)]
        attention_block(jb)
    if pend_tr:
        _jb_last = 7
        (ab, qi) = pend_tr.pop(0)
        pt_i = nb()
        TRS([(psb(pt_i)[:, 128 * c:128 * c + 128], ag[ab][:, 128 * c:128 * c + 128], ident_b) for c in range(4)],
            ["ag:%d_0" % ab, "ag:%d_1" % ab, "cst:ident_b"], ["ps%d" % pt_i])
        CP("dve", attnT[:, :, 128 * qi:128 * qi + 128], psb(pt_i)[:, 0:512].rearrange("p (c n) -> p c n", c=4),
           ["ps%d" % pt_i], ["attnT:%d" % qi])
    ATT_ALL = ["attnT:%d" % i for i in range(32)]
    dump("attnT", attnT, [128, 4, 4096], BF16, ATT_ALL)

    PHASE(11)
    o = RD_OFF + 32 * K
    Wg = rc("w2b", [128, 8, 2048], BF16)
    Wzs = rc("w2b", [128, 8, 512], BF16)
    Wglu = rc("w2b", [128, 4, 512], BF16)
    assert o <= AR.nbytes, o
    o = CST_END
    Wo = rc("w2c", [128, 8, 1024], BF16)
    fgB = rc("w2c", [128, 1024], F32)
    assert o <= RA_OFF
    o = RC_OFF
    Wba = rc("w2d", [128, 4, 1024], BF16)
    Wbs = rc("w2d", [128, 4, 1024], BF16)
    xt3 = [rc("x3", [128, 1024], F32) for _ in range(2)]
    hn3 = [rc("h3", [128, 1024], BF16) for _ in range(2)]
    hTc = [rc("hT3", [128, 8, 256], BF16) for _ in range(2)]
    zss = rc("zss", [128, 4, 256], F32)
    sg = rc("sg", [128, 4, 256], F32)
    ssmT = rc("ssmT", [128, 4, 256], BF16)
    sga = [rc("sga", [128, 256], F32) for _ in range(2)]
    sgs = [rc("sgs", [128, 256], F32) for _ in range(2)]
    m1 = [rc("m1", [128, 256], F32) for _ in range(2)]
    m2 = [rc("m2", [128, 256], F32) for _ in range(2)]
    mT = [rc("mT", [128, 8, 256], BF16) for _ in range(2)]
    junk3 = rc("junk3", [128, 1024], BF16)
    assert o <= RD_OFF + 32 * K, o
    o = RD_OFF + 32 * K + 44 * K
    xr = [rc("xr", [128, 1024], F32) for _ in range(2)]
    assert o <= AR.nbytes, o
    for c4 in range(4):
        DMA("pool", Wg[:, :, 512 * c4:512 * c4 + 512], w_in_v[:, :, 2304 + 512 * c4:2304 + 512 * c4 + 512], [], ["w2b:g%d" % c4])
    DMA("pool", Wzs, w_in_v[:, :, 1792:2304], [], ["w2b:zs"])
    DMA("pool", Wglu, w_glu.rearrange("(k p) e -> p k e", p=128), [], ["w2b:glu"])
    DMA("pool", Wo, w_out.rearrange("(k p) e -> p k e", p=128), [], ["w2c:o"])
    DMA("sp", fgB, fgain[0:1, :].partition_broadcast(128), [], ["w2c:fg"])
    DMA("pool", Wba, w_ba.rearrange("(k p) e -> p k e", p=128), [], ["w2d:ba"])
    DMA("pool", Wbs, w_bs.rearrange("(k p) e -> p k e", p=128), [], ["w2d:bs"])
    WG_ALL = ["w2b:g%d" % c for c in range(4)]
    ocnt = [0]

    def merge_p1(jb):
        bb = jb % 2
        hT_ = hTc[bb]
        hkeys = ["hT3:%d_%d" % (bb, i) for i in range(2)]
        tok0 = 256 * jb
        for c in range(4):
            bi = nb()
            MM([(psf(bi)[:, 0:256], Wzs[:, k, 128 * c:128 * c + 128], hT_[:, k, :], k == 0, k == 7) for k in range(8)],
               hkeys + ["w2b:zs"], ["ps%d" % bi])
            ACT(zss[:, c, :], psf(bi)[:, 0:256], AF.Silu, ["ps%d" % bi], ["zss:%d" % c])
        for c in range(4):
            bi = nb()
            MM([(psf(bi)[:, 0:256], Wglu[:, kc, 128 * c:128 * c + 128], ygT[:, kc, tok0:tok0 + 256], kc == 0, kc == 3)
                for kc in range(4)], YGT_ALL + ["w2b:glu"], ["ps%d" % bi])
            ACT(sg[:, c, :], psf(bi)[:, 0:256], AF.Sigmoid, ["ps%d" % bi, "cst:bgluT"], ["sg:%d" % c], bias=bgluT[:, c:c + 1])
        TT("pool", sg, sg, ygT[:, :, tok0:tok0 + 256], ALU.mult, ["sg:%d" % c for c in range(4)] + YGT_ALL, ["sg:all"])
        TT("pool", ssmT, sg, zss, ALU.mult, ["sg:all"] + ["zss:%d" % c for c in range(4)], ["ssmT:all"])

    def merge_p2(jb, hooks=None):
        bb = jb % 2
        hT_ = hTc[bb]
        hkeys = ["hT3:%d_%d" % (bb, i) for i in range(2)]
        tok0 = 256 * jb
        for e8 in range(8):
            for fn_ in (hooks or {}).get(e8, ()):
                fn_()
            eb = e8 % 2
            b_ga, b_gs, b_ba, b_bs = nb(), nb(), nb(), nb()
            MM([(psf(b_ga)[:, 0:256], Wg[:, k, 128 * e8:128 * e8 + 128], hT_[:, k, :], k == 0, k == 7) for k in range(8)],
               hkeys + WG_ALL, ["ps%d" % b_ga])
            ACT(sga[eb], psf(b_ga)[:, 0:256], AF.Sigmoid, ["ps%d" % b_ga, "cst:bgT"], ["sga:%d" % eb], bias=bgT[:, e8:e8 + 1])
            MM([(psf(b_gs)[:, 0:256], Wg[:, k, 1024 + 128 * e8:1024 + 128 * e8 + 128], hT_[:, k, :], k == 0, k == 7) for k in range(8)],
               hkeys + WG_ALL, ["ps%d" % b_gs])
            ACT(sgs[eb], psf(b_gs)[:, 0:256], AF.Sigmoid, ["ps%d" % b_gs, "cst:bgT"], ["sgs:%d" % eb], bias=bgT[:, 8 + e8:9 + e8])
            MM([(psf(b_ba)[:, 0:256], Wba[:, kc, 128 * e8:128 * e8 + 128], attnT[:, kc, tok0:tok0 + 256], kc == 0, kc == 3)
                for kc in range(4)], ATT_ALL + ["w2d:ba"], ["ps%d" % b_ba])
            MM([(psf(b_bs)[:, 0:256], Wbs[:, kc, 128 * e8:128 * e8 + 128], ssmT[:, kc, :], kc == 0, kc == 3)
                for kc in range(4)], ["ssmT:all", "w2d:bs"], ["ps%d" % b_bs])
            TT("dve", m1[eb], psf(b_ba)[:, 0:256], sga[eb], ALU.mult, ["ps%d" % b_ba, "sga:%d" % eb], ["m1:%d" % eb])
            TT("dve", m2[eb], psf(b_bs)[:, 0:256], sgs[eb], ALU.mult, ["ps%d" % b_bs, "sgs:%d" % eb], ["m2:%d" % eb])
            TT("dve", mT[bb][:, e8, :], m1[eb], m2[eb], ALU.add, ["m1:%d" % eb, "m2:%d" % eb], ["mT:%d_%d" % (bb, e8)])

    out_pend = []

    def merge_out(jb):
        bb = jb % 2
        tok0 = 256 * jb
        mkeys = ["mT:%d_%d" % (bb, e8) for e8 in range(8)]
        for tt in range(2):
            ob = ocnt[0] % 2
            ocnt[0] += 1
            t0 = tok0 + 128 * tt
            out_pend.append((jb, tt, ob, t0))
            DMA("sp", xr[ob], x[t0:t0 + 128, :], [], ["xr:%d" % ob])
            for half in range(2):
                bo = nb()
                MM([(psf(bo), mT[bb][:, e8, 128 * tt:128 * tt + 128], Wo[:, e8, 512 * half:512 * half + 512], e8 == 0, e8 == 7)
                    for e8 in range(8)], mkeys + ["w2c:o"], ["ps%d" % bo])
                TT("dve", xr[ob][:, 512 * half:512 * half + 512], psf(bo), xr[ob][:, 512 * half:512 * half + 512], ALU.add,
                   ["ps%d" % bo, "xr:%d" % ob], ["xr:%d" % ob])

    def merge_fin():
        while out_pend:
            (jb, tt, ob, t0) = out_pend.pop(0)
            col = 4 + ob
            ssc = ss_t[:, col:col + 1]
            ACT(junk3, xr[ob], AF.Square, ["xr:%d" % ob], ["junk3:0", "cst:ss%d" % col], accum_out=ssc)
            ACT(ssc, ssc, AF.Sqrt, ["cst:ss%d" % col, "cst:eps"], ["cst:ss%d" % col], scale=1.0 / 1024.0, bias=epst)
            S.add("dve", lambda e, ssc=ssc: e.reciprocal(out=ssc, in_=ssc), ["cst:ss%d" % col], ["cst:ss%d" % col])
            STT(xr[ob], xr[ob], ssc, fgB, ALU.mult, ALU.mult, ["xr:%d" % ob, "cst:ss%d" % col, "w2c:fg"], ["xr:%d" % ob])
            DMA("pool", out[t0:t0 + 128, :], xr[ob], ["xr:%d" % ob], ["out:%d" % (2 * jb + tt)])

    def front3_rms(jb):
        for i in range(2):
            DMA("sp", xt3[i], x[256 * jb + 128 * i: 256 * jb + 128 * i + 128, :], [], ["x3:%d" % i])
            rms_front(xt3[i], None, hn3[i], "h3:%d" % i, 2 + i, ["x3:%d" % i])

    def front3_T(jb):
        bb = jb % 2
        for i in range(2):
            pi_ = nb()
            pT_ = psb(pi_)
            TRS([(pT_[:, 128 * k:128 * k + 128], hn3[i][:, 128 * k:128 * k + 128], ident_b) for k in range(8)],
                ["h3:%d" % i, "cst:ident_b"], ["ps%d" % pi_])
            TT("dve", hTc[bb][:, :, 128 * i:128 * i + 128], pT_.rearrange("p (k n) -> p k n", k=8), gain_b, ALU.mult,
               ["ps%d" % pi_, "cst:gainT"], ["hT3:%d_%d" % (bb, i)])

    front3_rms(0)
    front3_T(0)
    front3_rms(1)
    for jb in range(16):
        if jb + 1 < 16:
            front3_T(jb + 1)
        merge_p1(jb)
        if jb >= 1:
            merge_out(jb - 1)
        hk_ = {2: [merge_fin]}
        if jb + 2 < 16:
            hk_[5] = [lambda nj=jb + 2: front3_rms(nj)]
        merge_p2(jb, hk_)
    merge_out(15)
    merge_fin()
    PHASE(0)
    S.add("sp", lambda e: e.nop(), ["out:%d" % i for i in range(32)] + ["dbgout:" + n for n in dbg_out], [])
    S.emit()
    return nc, dbg_out


def _t5_bucket_np(rel):
    half = 16
    ret = (rel > 0).astype(np.int64) * half
    n = np.abs(rel)
    max_exact = half // 2
    nf = np.maximum(n, 1).astype(np.float32)
    large = max_exact + (np.log(nf / np.float32(max_exact)) / np.float32(math.log(128 / max_exact))
                         * np.float32(half - max_exact)).astype(np.int32)
    large = np.minimum(large, half - 1)
    return ret + np.where(n < max_exact, n, large)


def _t5_bucket_table(rel):
    try:
        import jax
        import jax.numpy as jnp
        cpu = jax.devices("cpu")[0]
        with jax.default_device(cpu):
            r = jnp.asarray(rel, dtype=jnp.int32)
            half = 16
            ret = (r > 0).astype(jnp.int32) * half
            n = jnp.abs(r)
            max_exact = half // 2
            nf = jnp.maximum(n, 1).astype(jnp.float32)
            large = max_exact + (jnp.log(nf / max_exact) / math.log(128 / max_exact)
                                 * (half - max_exact)).astype(jnp.int32)
            large = jnp.minimum(large, half - 1)
            return np.asarray(ret + jnp.where(n < max_exact, n, large)).astype(np.int64)
    except Exception:
        return _t5_bucket_np(rel)


def _host_constants():
    c = {}
    c["c_ident"] = np.eye(128, dtype=np.float32)
    c["c_anti"] = np.ascontiguousarray(np.eye(128, dtype=np.float32)[::-1])
    c["c_sval"] = np.broadcast_to(np.arange(32, dtype=np.float32), (128, 32)).copy()
    sg = np.ones((128, 1), np.float32)
    sg[:64] = -1.0
    c["c_sig"] = sg
    s_idx = np.arange(128)[:, None] // 16
    t_idx = np.arange(128)[None, :] // 16
    c["c_mf"] = (t_idx >= s_idx).astype(np.float32)
    c["c_mb"] = (s_idx >= t_idx).astype(np.float32)
    oh = np.zeros((33, 512), np.float32)
    r = np.arange(511)
    rel = r - 255
    bk = _t5_bucket_table(rel)
    oh[bk, r] = 1.0
    oh[32, r] = (np.abs(rel) > 128).astype(np.float32)
    oh[32, 511] = 1.0
    c["c_oh"] = oh
    return c


_CACHE = {}


def _get_program():
    if "nc" not in _CACHE:
        _CACHE["nc"] = build_program()[0]
    return _CACHE["nc"]


def make_in_maps(inputs):
    f = lambda a: np.ascontiguousarray(np.asarray(a, dtype=np.float32))
    shared = {
        "norm_gain": f(inputs["norm_gain"]).reshape(1, 1024),
        "w_in": f(inputs["w_in"]).reshape(1024, 4352),
        "b_gate": f(inputs["b_gate"]).reshape(1, 2048),
        "attn_sink": f(inputs["attn_sink"]).reshape(1, 8),
        "rel_bias_table": f(inputs["rel_bias_table"]).reshape(32, 8),
        "ssm_a_re": f(inputs["ssm_a_re"]).reshape(2, 32, 64),
        "ssm_a_im": f(inputs["ssm_a_im"]).reshape(2, 32, 64),
        "ssm_log_dt": f(inputs["ssm_log_dt"]).reshape(2, 32),
        "ssm_b_re": f(inputs["ssm_b_re"]).reshape(2, 32, 64, 16),
        "ssm_b_im": f(inputs["ssm_b_im"]).reshape(2, 32, 64, 16),
        "ssm_c_re": f(inputs["ssm_c_re"]).reshape(2, 32, 16, 64),
        "ssm_c_im": f(inputs["ssm_c_im"]).reshape(2, 32, 16, 64),
        "ssm_d": f(inputs["ssm_d"]).reshape(1, 512),
        "w_glu": f(inputs["w_glu"]).reshape(512, 512),
        "b_glu": f(inputs["b_glu"]).reshape(1, 512),
        "w_branch_attn": f(inputs["w_branch_attn"]).reshape(512, 1024),
        "w_branch_ssm": f(inputs["w_branch_ssm"]).reshape(512, 1024),
        "w_out": f(inputs["w_out"]).reshape(1024, 1024),
        "final_norm_gain": f(inputs["final_norm_gain"]).reshape(1, 1024),
    }
    shared.update(_host_constants())
    xs = f(inputs["x"])
    maps = []
    for c in range(8):
        m = dict(shared)
        m["x"] = np.ascontiguousarray(xs[2 * c:2 * c + 2].reshape(4096, 1024))
        maps.append(m)
    return maps


def kernel(**inputs):
    nc = _get_program()
    in_maps = make_in_maps(inputs)
    res = run_bass_kernel_spmd(nc, in_maps, core_ids=list(range(8)))
    outs = [np.asarray(r["out"]).reshape(2, 2048, 1024) for r in res.results]
    return np.concatenate(outs, axis=0).astype(np.float32)
```

```python
import math
from contextlib import ExitStack
import numpy as np
import concourse.bass as bass
import concourse.mybir as mybir
from concourse.bass_utils import run_bass_kernel_spmd

F32 = mybir.dt.float32
BF16 = mybir.dt.bfloat16
I32 = mybir.dt.int32
ALU = mybir.AluOpType
AF = mybir.ActivationFunctionType
ENGS = ("pe", "act", "dve", "pool", "sp")
STRICT_SAME_ENGINE = True
EPS = 1e-6
TWO_PI = 2.0 * math.pi


def _prod(xs):
    r = 1
    for v in xs:
        r *= int(v)
    return r


class Sched:
    N_DMA_SEMS = 40

    def __init__(self, nc):
        self.nc = nc
        self.ops = []
        self.last_w = {}
        self.readers = {}
        self.guards = {}

    def guard(self, new_prefix, old_prefixes):
        s = set()
        for k, w in self.last_w.items():
            if k.split(":")[0] in old_prefixes and w is not None:
                s.add(w)
        for k, rs in self.readers.items():
            if k.split(":")[0] in old_prefixes:
                s.update(rs)
        best = {}
        out = set()
        for i in s:
            o = self.ops[i]
            if o["dma"]:
                out.add(i)
            else:
                best[o["eng"]] = max(best.get(o["eng"], -1), i)
        out.update(best.values())
        self.guards.setdefault(new_prefix, set()).update(out)

    def add(self, eng, fn, reads=(), writes=(), dma=False):
        idx = len(self.ops)
        psr = [k for k in reads if k.startswith("ps")]
        writes = list(writes) + [k for k in psr if k not in writes]
        deps = set()
        raw = set()
        for k in list(reads) + list(writes):
            g = self.guards.get(k.split(":")[0])
            if g:
                deps |= g
                raw |= g
        for k in reads:
            w = self.last_w.get(k)
            if w is not None:
                deps.add(w)
                raw.add(w)
        for k in writes:
            w = self.last_w.get(k)
            if w is not None:
                deps.add(w)
            for r in self.readers.get(k, ()):
                deps.add(r)
        deps.discard(idx)
        for k in reads:
            self.readers.setdefault(k, []).append(idx)
        for k in writes:
            self.last_w[k] = idx
            self.readers[k] = []
        fdeps = []
        for d in deps:
            p = self.ops[d]
            if (not p["dma"]) and (not dma) and p["eng"] == eng:
                if eng == "pe" or (d not in raw and not STRICT_SAME_ENGINE):
                    continue
            fdeps.append(d)
        self.ops.append(dict(eng=eng, fn=fn, deps=sorted(fdeps), dma=dma, idx=idx))
        return idx

    def emit(self):
        nc = self.nc
        ops = self.ops
        needed = set()
        for o in ops:
            needed.update(o["deps"])
        cnt = {e: 0 for e in ENGS}
        for o in ops:
            if (not o["dma"]) and o["idx"] in needed:
                cnt[o["eng"]] += 1
                o["ticket"] = cnt[o["eng"]]
        dma_ops = [o for o in ops if o["dma"]]
        pools = {"sp": (0, 28), "pool": (28, 12), "act": (40, 0)}
        nd = 40
        sem_val = [0] * nd
        qcnt = {"sp": 0, "pool": 0}
        for o in dma_ops:
            base, n = pools[o["eng"]]
            s = base + qcnt[o["eng"]] % n
            qcnt[o["eng"]] += 1
            o["dsem"] = s
            o["dprev"] = sem_val[s]
            sem_val[s] += 16
            o["dval"] = sem_val[s]
        with ExitStack() as st:
            esem = {e: st.enter_context(nc.semaphore("sem_" + e)) for e in ENGS}
            dsem = [st.enter_context(nc.semaphore("dsem%d" % i)) for i in range(nd)]
            block = st.enter_context(nc.Block())
            streams = {e: [o for o in ops if o["eng"] == e] for e in ENGS}

            def make_body(e):
                def body(eng):
                    waited = {f: 0 for f in ENGS}
                    dwaited = [0] * nd
                    for o in streams[e]:
                        for d in o["deps"]:
                            p = ops[d]
                            if p["dma"]:
                                s = p["dsem"]
                                if dwaited[s] < p["dval"]:
                                    eng.wait_ge(dsem[s], p["dval"])
                                    dwaited[s] = p["dval"]
                            else:
                                f = p["eng"]
                                if waited[f] < p["ticket"]:
                                    eng.wait_ge(esem[f], p["ticket"])
                                    waited[f] = p["ticket"]
                        if o["dma"]:
                            s = o["dsem"]
                            if o["dprev"] > 0 and dwaited[s] < o["dprev"]:
                                eng.wait_ge(dsem[s], o["dprev"])
                                dwaited[s] = o["dprev"]
                            ins = o["fn"](eng)
                            ins.then_inc(dsem[s], 16)
                        else:
                            ins = o["fn"](eng)
                            if "ticket" in o:
                                ins.then_inc(esem[e], 1)
                return body

            block.tensor(make_body("pe"))
            block.scalar(make_body("act"))
            block.vector(make_body("dve"))
            block.gpsimd(make_body("pool"))
            block.sync(make_body("sp"))


class Arena:
    def __init__(self, nc, S, nbytes):
        self.t = nc.alloc_sbuf_tensor("arena", [128, nbytes // 2], BF16)
        self.S = S
        self.nbytes = nbytes
        self.allocs = []

    def view(self, prefix, off, shape, dt):
        esz = 2 if dt == BF16 else 4
        nb = _prod(shape[1:]) * esz
        assert off % 4 == 0 and off + nb <= self.nbytes, (prefix, off, nb)
        olds = set(p for (p, s, e) in self.allocs if p != prefix and s < off + nb and off < e)
        if olds:
            self.S.guard(prefix, olds)
        self.allocs.append((prefix, off, off + nb))
        ap = self.t[:, off // 2: off // 2 + nb // 2]
        if dt != BF16:
            ap = ap.bitcast(dt)
        if len(shape) == 3:
            ap = ap.rearrange("p (a b) -> p a b", a=shape[1])
        elif len(shape) == 4:
            ap = ap.rearrange("p (a b c) -> p a b c", a=shape[1], b=shape[2])
        elif len(shape) == 5:
            ap = ap.rearrange("p (a b c d) -> p a b c d", a=shape[1], b=shape[2], c=shape[3])
        if shape[0] < 128:
            ap = ap[0:shape[0]]
        return ap


def build_program(dbg=(), phase_limit=99):
    nc = bass.Bass("TRN2", target_bir_lowering=False)
    S = Sched(nc)
    _phase = [0]
    _orig_add = S.add

    _pending = []
    _defer = [False]

    def _add(eng, fn, reads=(), writes=(), dma=False):
        if _phase[0] > phase_limit:
            return None
        if _defer[0]:
            _pending.append((eng, fn, list(reads), list(writes), dma))
            return None
        return _orig_add(eng, fn, reads, writes, dma)
    S.add = _add

    def flush(n=None):
        k = len(_pending) if n is None else min(n, len(_pending))
        for _ in range(k):
            _orig_add(*_pending.pop(0))

    def PHASE(n):
        _phase[0] = n

    def din(name, shape):
        return nc.dram_tensor(name, shape, F32, kind="ExternalInput").ap()

    x = din("x", [4096, 1024])
    norm_gain = din("norm_gain", [1, 1024])
    w_in = din("w_in", [1024, 4352])
    b_gate = din("b_gate", [1, 2048])
    attn_sink = din("attn_sink", [1, 8])
    rel_tab = din("rel_bias_table", [32, 8])
    a_re = din("ssm_a_re", [2, 32, 64])
    a_im = din("ssm_a_im", [2, 32, 64])
    log_dt = din("ssm_log_dt", [2, 32])
    b_re = din("ssm_b_re", [2, 32, 64, 16])
    b_im = din("ssm_b_im", [2, 32, 64, 16])
    c_re = din("ssm_c_re", [2, 32, 16, 64])
    c_im = din("ssm_c_im", [2, 32, 16, 64])
    ssm_d = din("ssm_d", [1, 512])
    w_glu = din("w_glu", [512, 512])
    b_glu = din("b_glu", [1, 512])
    w_ba = din("w_branch_attn", [512, 1024])
    w_bs = din("w_branch_ssm", [512, 1024])
    w_out = din("w_out", [1024, 1024])
    fgain = din("final_norm_gain", [1, 1024])
    c_ident = din("c_ident", [128, 128])
    c_anti = din("c_anti", [128, 128])
    c_sval = din("c_sval", [128, 32])
    c_sig = din("c_sig", [128, 1])
    c_mf = din("c_mf", [128, 128])
    c_mb = din("c_mb", [128, 128])
    c_oh = din("c_oh", [33, 512])
    out = nc.dram_tensor("out", [4096, 1024], F32, kind="ExternalOutput").ap()
    fd_t = nc.dram_tensor("fd_scratch", [8, 512], F32, kind="Internal")
    fd = fd_t.ap()
    lsc = nc.dram_tensor("l_scratch", [32, 128, 1024], BF16, kind="Internal").ap()
    dbg_out = {}

    AR = Arena(nc, S, 212736)
    K = 1024
    ps = [nc.alloc_psum_tensor("ps%d" % i, [128, 512], F32) for i in range(8)]

    _bank = [0]
    _bank_mod = [8]

    def nb():
        i = _bank[0] % _bank_mod[0]
        _bank[0] += 1
        return i

    def psf(i):
        return ps[i][:]

    def psb(i):
        return ps[i][:].bitcast(BF16)

    def DMA(q, o, i, reads, writes, slow=False):
        if slow:
            S.add(q, lambda e: e.dma_start(out=o, in_=i, allow_slow_non_contiguous=True), reads, writes, dma=True)
        else:
            S.add(q, lambda e: e.dma_start(out=o, in_=i), reads, writes, dma=True)

    def ACT(o, i, func, reads, writes, **kw):
        S.add("act", lambda e: e.activation(out=o, in_=i, func=func, **kw), reads, writes)

    def TT(eng, o, a, b, op, reads, writes):
        S.add(eng, lambda e: e.tensor_tensor(out=o, in0=a, in1=b, op=op), reads, writes)

    def TS(eng, o, a, s1, s2, op0, op1, reads, writes):
        if op1 is None:
            S.add(eng, lambda e: e.tensor_scalar(out=o, in0=a, scalar1=s1, scalar2=None, op0=op0), reads, writes)
        else:
            S.add(eng, lambda e: e.tensor_scalar(out=o, in0=a, scalar1=s1, scalar2=s2, op0=op0, op1=op1), reads, writes)

    def STT(o, a, sc, b, op0, op1, reads, writes):
        S.add("dve", lambda e: e.scalar_tensor_tensor(out=o, in0=a, scalar=sc, in1=b, op0=op0, op1=op1), reads, writes)

    def CP(eng, o, i, reads, writes):
        if eng == "act":
            ACT(o, i, AF.Copy, reads, writes)
        else:
            S.add(eng, lambda e: e.tensor_copy(out=o, in_=i), reads, writes)

    def MM(lst, reads, writes):
        def fn(e):
            ins = None
            for (o, l, r, st, sp) in lst:
                ins = e.matmul(o, lhsT=l, rhs=r, start=st, stop=sp)
            return ins
        S.add("pe", fn, reads, writes)

    def TRS(lst, reads, writes):
        def fn(e):
            ins = None
            for (o, i, idn) in lst:
                ins = e.transpose(o, in_=i, identity=idn)
            return ins
        S.add("pe", fn, reads, writes)

    def dump(name, ap, shape, dt, reads):
        if name not in dbg:
            return
        t = nc.dram_tensor("dbg_" + name, shape, dt, kind="ExternalOutput").ap()
        dbg_out[name] = t
        DMA("sp", t, ap, reads, ["dbgout:" + name])

    ident_f = AR.view("cst", 0, [128, 128], F32)
    ident_b = AR.view("cst", 512, [128, 128], BF16)
    mf = AR.view("cst", 768, [128, 128], F32)
    mb = AR.view("cst", 1280, [128, 128], F32)
    sval = AR.view("cst", 1792, [128, 32], F32)
    sig = AR.view("cst", 1920, [128, 1], F32)
    epst = AR.view("cst", 1924, [128, 1], F32)
    gainT = AR.view("cst", 1928, [128, 8], F32)
    esink = AR.view("cst", 1960, [128, 8], F32)
    bgT = AR.view("cst", 1992, [128, 16], F32)
    bgluT = AR.view("cst", 2056, [128, 4], F32)
    ss_t = AR.view("cst", 2072, [128, 8], F32)
    anti = AR.view("cst", 2176, [128, 128], F32)
    tab33 = AR.view("cst", 2688, [33, 8], F32)
    fsb = AR.view("cst", 2720, [8, 512], F32)
    CST_END = 5 * K

    DMA("sp", ident_f, c_ident, [], ["cst:ident_f"])
    DMA("sp", anti, c_anti, [], ["cst:anti"])
    DMA("sp", mf, c_mf, [], ["cst:mf"])
    DMA("sp", mb, c_mb, [], ["cst:mb"])
    DMA("sp", sval, c_sval, [], ["cst:sval"])
    DMA("sp", sig, c_sig, [], ["cst:sig"])
    CP("dve", ident_b, ident_f, ["cst:ident_f"], ["cst:ident_b"])
    S.add("dve", lambda e: e.memset(epst, EPS), [], ["cst:eps"])
    vecraw = AR.view("raw", 5 * K + 8 * K + 8704 + 6 * K + 32 * K + 14 * K, [32, 128], F32)
    S.add("dve", lambda e: e.memset(vecraw, 0.0), [], ["raw:vr"])
    DMA("sp", vecraw[0:8, :], norm_gain[0].rearrange("(k p) -> k p", p=128), [], ["raw:vr"])
    DMA("sp", vecraw[8:24, :], b_gate[0].rearrange("(k p) -> k p", p=128), [], ["raw:vr"])
    DMA("sp", vecraw[24:28, :], b_glu[0].rearrange("(k p) -> k p", p=128), [], ["raw:vr"])
    TRS([(ps[7][:, 0:32], vecraw, ident_f[0:32, 0:32])], ["raw:vr", "cst:ident_f"], ["ps7"])
    CP("dve", gainT, ps[7][:, 0:8], ["ps7"], ["cst:gainT"])
    CP("dve", bgT, ps[7][:, 8:24], ["ps7"], ["cst:bgT"])
    CP("dve", bgluT, ps[7][:, 24:28], ["ps7"], ["cst:bgluT"])
    DMA("sp", esink, attn_sink[0:1, :].partition_broadcast(128), [], ["cst:esink"])
    ACT(esink, esink, AF.Exp, ["cst:esink"], ["cst:esink"])

    KT_OFF = CST_END
    VALL_OFF = KT_OFF + 8 * K
    BIAS_OFF = VALL_OFF + 8704
    RA_OFF = BIAS_OFF + 6 * K
    RB_OFF = RA_OFF + 32 * K
    RC_OFF = RB_OFF + 32 * K
    RD_OFF = RC_OFF + 32 * K
    assert RD_OFF % 4 == 0
    kT_all = AR.view("kT", KT_OFF, [128, 4096], BF16)
    v_all = AR.view("vall", VALL_OFF, [128, 32, 2, 65], BF16)
    biasT = AR.view("bias", BIAS_OFF, [128, 2, 3, 4, 128], BF16)

    w_in_v = w_in.rearrange("(k p) e -> p k e", p=128)

    o = RA_OFF
    Wu = AR.view("p1w", o, [128, 8, 512], BF16); o += 8 * K
    Wk = AR.view("p1w", o, [128, 8, 128], BF16); o += 2 * K
    Wv = AR.view("p1w", o, [128, 8, 128], BF16); o += 2 * K
    xs = [AR.view("p1x", o + 4 * K * i, [128, 1024], F32) for i in range(2)]; o += 8 * K
    hn = [AR.view("p1h", o + 2 * K * i, [128, 1024], BF16) for i in range(2)]; o += 4 * K
    hTs = [AR.view("p1t", o + 2 * K * i, [128, 8, 128], BF16) for i in range(2)]; o += 4 * K
    assert o <= RA_OFF + 32 * K
    U_tm = AR.view("utm", RC_OFF, [128, 32, 32, 16], BF16)
    VT_OFF = RB_OFF + 16 * K
    vT_all = AR.view("vT", VT_OFF, [128, 4096], BF16)

    DMA("pool", Wu, w_in_v[:, :, 1280:1792], [], ["p1w:u"])
    DMA("pool", Wk, w_in_v[:, :, 512:640], [], ["p1w:k"])
    DMA("pool", Wv, w_in_v[:, :, 640:768], [], ["p1w:v"])

    x_s = x.rearrange("(n s) d -> s n d", s=32)

    def rms_front(xt, xkey, hnt, hkey, col, reads_x):
        ssc = ss_t[:, col:col + 1]
        ACT(hnt, xt, AF.Square, reads_x, [hkey, "cst:ss%d" % col], accum_out=ssc)
        ACT(ssc, ssc, AF.Sqrt, ["cst:ss%d" % col, "cst:eps"], ["cst:ss%d" % col], scale=1.0 / 1024.0, bias=epst)
        S.add("dve", lambda e: e.reciprocal(out=ssc, in_=ssc), ["cst:ss%d" % col], ["cst:ss%d" % col])
        TS("dve", hnt, xt, ssc, None, ALU.mult, None, reads_x + ["cst:ss%d" % col], [hkey])

    _defer[0] = True
    PHASE(4)
    o = RD_OFF
    def rd(prefix, shape, dt):
        nonlocal o
        esz = 2 if dt == BF16 else 4
        v = AR.view(prefix, o, shape, dt)
        o += (_prod(shape[1:]) * esz + 3) // 4 * 4
        return v

    Pr = rd("tab", [128, 32, 32], F32)
    Pi = rd("tab", [128, 32, 32], F32)
    Qr = rd("tab", [128, 32, 32], F32)
    NQi = rd("tab", [128, 32, 32], F32)
    Bbr = rd("par", [128, 32, 16], F32)
    Bbi = rd("par", [128, 32, 16], F32)
    Cr = rd("par", [128, 32, 16], F32)
    Ci = rd("par", [128, 32, 16], F32)
    dB = rd("par", [128, 512], F32)
    sm = {}
    for nm in ("are", "aim", "dt", "rho", "th", "er", "cs", "sn", "abr", "abi", "den", "cfr", "cfi", "t1", "t2",
               "rhop", "thp", "a32r", "a32i", "e32", "kf"):
        sm[nm] = rd("par", [128, 32], F32)
    ki32 = rd("par", [128, 1024], I32)
    AAf = rd("par", [128, 2, 2, 64], F32)
    Wsc = rd("scn", [128, 2, 2, 64], F32)
    Ssc = rd("scn", [128, 2, 64], F32)
    Ssc2 = rd("scn", [128, 2, 64], F32)
    Lb = [rd("lt", [128, 2, 512], BF16) for _ in range(2)]
    Rb = [rd("rt", [128, 2, 512], BF16) for _ in range(2)]
    tmpD = [rd("tmpd", [128, 512], F32) for _ in range(2)]
    tmpP = [rd("tmpp", [128, 512], F32) for _ in range(2)]
    LTb = [rd("ltt", [128, 4, 2, 128], BF16) for _ in range(2)]
    Tg = [rd("tg", [128, 4, 512], BF16) for _ in range(2)]
    Dg = [rd("dg", [128, 128], BF16) for _ in range(2)]
    bt1 = [rd("bt", [128, 128], F32) for _ in range(2)]
    bt2 = [rd("bt", [128, 128], F32) for _ in range(2)]
    g_arg = rd("gen", [128, 32, 32], F32)
    g_phi = rd("gen", [128, 32, 32], F32)
    g_sn = rd("gen", [128, 32, 32], F32)
    g_cs = rd("gen", [128, 32, 32], F32)
    assert o <= AR.nbytes, o
    GEN_OFF = o - 16 * K
    Braw_r = AR.view("raw", RB_OFF, [128, 32, 16], F32)
    Braw_i = AR.view("raw", RB_OFF + 2 * K, [128, 32, 16], F32)
    Craw_r = AR.view("raw", RB_OFF + 4 * K, [128, 4, 128], F32)
    Craw_i = AR.view("raw", RB_OFF + 6 * K, [128, 4, 128], F32)
    Araw_r = AR.view("raw", RB_OFF + 8 * K, [32, 128], F32)
    Araw_i = AR.view("raw", RB_OFF + 8 * K + 512, [32, 128], F32)

    for d in range(2):
        DMA("sp", Braw_r[64 * d:64 * d + 64], b_re[d].rearrange("g p c -> p g c"), [], ["raw:br%d" % d])
        DMA("sp", Braw_i[64 * d:64 * d + 64], b_im[d].rearrange("g p c -> p g c"), [], ["raw:bi%d" % d])
        DMA("sp", Araw_r[:, 64 * d:64 * d + 64], a_re[d], [], ["raw:ar%d" % d])
        DMA("sp", Araw_i[:, 64 * d:64 * d + 64], a_im[d], [], ["raw:ai%d" % d])
        DMA("sp", sm["dt"][64 * d:64 * d + 64, :], log_dt[d:d + 1, :].partition_broadcast(64), [], ["par:dt%d" % d])
        for t in range(4):
            DMA("sp", Craw_r[:, t, 64 * d:64 * d + 64],
                c_re[d].rearrange("g c p -> (g c) p")[128 * t:128 * t + 128, :], [], ["raw:cr%d%d" % (d, t)])
            DMA("sp", Craw_i[:, t, 64 * d:64 * d + 64],
                c_im[d].rearrange("g c p -> (g c) p")[128 * t:128 * t + 128, :], [], ["raw:ci%d%d" % (d, t)])
    DMA("sp", dB, ssm_d[0:1, :].partition_broadcast(128), [], ["par:dB"])
    p6 = psf(6)
    TRS([(p6[:, 0:32], Araw_r, ident_f[0:32, 0:32]), (p6[:, 32:64], Araw_i, ident_f[0:32, 0:32])],
        ["raw:ar0", "raw:ar1", "raw:ai0", "raw:ai1", "cst:ident_f"], ["ps6"])
    CP("dve", sm["are"], p6[:, 0:32], ["ps6"], ["par:are"])
    CP("dve", sm["aim"], p6[:, 32:64], ["ps6"], ["par:aim"])
    p7 = psf(7)
    TRS([(p7[:, 128 * t:128 * t + 128], Craw_r[:, t, :], ident_f) for t in range(4)],
        ["raw:cr%d%d" % (d, t) for d in range(2) for t in range(4)] + ["cst:ident_f"], ["ps7"])
    CP("dve", Cr.rearrange("p g c -> p (g c)"), p7, ["ps7"], ["par:Cr"])
    TRS([(p6[:, 128 * t:128 * t + 128], Craw_i[:, t, :], ident_f) for t in range(4)],
        ["raw:ci%d%d" % (d, t) for d in range(2) for t in range(4)] + ["cst:ident_f"], ["ps6"])
    CP("dve", Ci.rearrange("p g c -> p (g c)"), p6, ["ps6"], ["par:Ci"])

    PK = ["par:small"]

    def sincos(phi, sn_o, cs_o, kf, n, keyr, keyw):
        ki = ki32[:, 0:n]
        for (dst, shift) in ((sn_o, 0.0), (cs_o, math.pi / 2)):
            TS("dve", ki, phi, 1.0 / TWO_PI, shift / TWO_PI, ALU.mult, ALU.add, keyr, keyw)
            CP("dve", kf, ki, keyw, keyw)
            STT(dst, kf, -TWO_PI, phi, ALU.mult, ALU.add, keyr + keyw, keyw)
            TS("dve", dst, dst, shift, None, ALU.add, None, keyw, keyw)
            TS("dve", dst, dst, 3.14159, -3.14159, ALU.min, ALU.max, keyw, keyw)
            ACT(dst, dst, AF.Sin, keyw, keyw)

    rk = ["par:are", "par:aim", "par:dt0", "par:dt1"]
    ACT(sm["dt"], sm["dt"], AF.Exp, ["par:dt0", "par:dt1"], PK)
    TT("dve", sm["rho"], sm["are"], sm["dt"], ALU.mult, rk + PK, PK)
    TT("dve", sm["th"], sm["aim"], sm["dt"], ALU.mult, rk + PK, PK)
    sincos(sm["th"], sm["sn"], sm["cs"], sm["kf"], 32, PK, PK)
    ACT(sm["er"], sm["rho"], AF.Exp, PK, PK)
    TT("dve", sm["abr"], sm["er"], sm["cs"], ALU.mult, PK, PK)
    TT("dve", sm["abi"], sm["er"], sm["sn"], ALU.mult, PK, PK)
    TS("dve", sm["abr"], sm["abr"], -1.0, None, ALU.add, None, PK, PK)
    TT("dve", sm["t1"], sm["are"], sm["are"], ALU.mult, rk + PK, PK)
    TT("dve", sm["t2"], sm["aim"], sm["aim"], ALU.mult, rk + PK, PK)
    TT("dve", sm["den"], sm["t1"], sm["t2"], ALU.add, PK, PK)
    S.add("dve", lambda e: e.reciprocal(out=sm["den"], in_=sm["den"]), PK, PK)
    TT("dve", sm["t1"], sm["abr"], sm["are"], ALU.mult, rk + PK, PK)
    TT("dve", sm["t2"], sm["abi"], sm["aim"], ALU.mult, rk + PK, PK)
    TT("dve", sm["t1"], sm["t1"], sm["t2"], ALU.add, PK, PK)
    TT("dve", sm["cfr"], sm["t1"], sm["den"], ALU.mult, PK, PK)
    TT("dve", sm["t1"], sm["abi"], sm["are"], ALU.mult, rk + PK, PK)
    TT("dve", sm["t2"], sm["abr"], sm["aim"], ALU.mult, rk + PK, PK)
    TT("dve", sm["t1"], sm["t1"], sm["t2"], ALU.subtract, PK, PK)
    TT("dve", sm["cfi"], sm["t1"], sm["den"], ALU.mult, PK, PK)
    rawb = ["raw:br0", "raw:br1", "raw:bi0", "raw:bi1"]
    cfr_b = sm["cfr"].unsqueeze(2).broadcast_to([128, 32, 16])
    cfi_b = sm["cfi"].unsqueeze(2).broadcast_to([128, 32, 16])
    t512a = tmpD[0].rearrange("p (g c) -> p g c", g=32)
    t512b = tmpD[1].rearrange("p (g c) -> p g c", g=32)
    TT("dve", t512a, Braw_r, cfr_b, ALU.mult, rawb + PK, ["tmpd:0"])
    TT("dve", t512b, Braw_i, cfi_b, ALU.mult, rawb + PK, ["tmpd:1"])
    TT("dve", Bbr, t512a, t512b, ALU.subtract, ["tmpd:0", "tmpd:1"], ["par:Bb"])
    TT("dve", t512a, Braw_r, cfi_b, ALU.mult, rawb + PK, ["tmpd:0"])
    TT("dve", t512b, Braw_i, cfr_b, ALU.mult, rawb + PK, ["tmpd:1"])
    TT("dve", Bbi, t512a, t512b, ALU.add, ["tmpd:0", "tmpd:1"], ["par:Bb"])
    TS("dve", sm["t1"], sm["th"], 32.0, None, ALU.mult, None, PK, PK)
    sincos(sm["t1"], sm["a32i"], sm["a32r"], sm["kf"], 32, PK, PK)
    ACT(sm["e32"], sm["rho"], AF.Exp, PK, PK, scale=32.0)
    TT("dve", sm["a32r"], sm["a32r"], sm["e32"], ALU.mult, PK, PK)
    TT("dve", sm["a32i"], sm["a32i"], sm["e32"], ALU.mult, PK, PK)
    TS("dve", sm["t2"], sm["a32i"], -1.0, None, ALU.mult, None, PK, PK)
    AAv = AAf.rearrange("p o t (g q) -> p o t g q", q=2)
    for (oo, tt_, src) in ((0, 0, "a32r"), (0, 1, "t2"), (1, 0, "a32i"), (1, 1, "a32r")):
        CP("dve", AAv[:, oo, tt_, :, :], sm[src].unsqueeze(2).broadcast_to([128, 32, 2]), PK, ["par:AAf"])
    TS("dve", sm["rhop"], sm["rho"], sig, None, ALU.mult, None, PK + ["cst:sig"], PK)
    TS("dve", sm["thp"], sm["th"], sig, None, ALU.mult, None, PK + ["cst:sig"], PK)
    sval_b = sval.unsqueeze(1).broadcast_to([128, 32, 32])
    GK = ["gen:all"]
    TT("dve", g_arg, sm["rhop"].unsqueeze(2).broadcast_to([128, 32, 32]), sval_b, ALU.mult, PK + ["cst:sval"], GK)
    TT("dve", g_phi, sm["thp"].unsqueeze(2).broadcast_to([128, 32, 32]), sval_b, ALU.mult, PK + ["cst:sval"], GK)
    fl = lambda a: a.rearrange("p g s -> p (g s)")
    sincos(fl(g_phi), fl(g_sn), fl(g_cs), fl(Pr), 1024, GK, GK + ["tab:all"])
    ACT(fl(g_phi), fl(g_arg), AF.Exp, GK, GK)
    ACT(fl(g_arg), fl(g_arg), AF.Exp, GK, GK, scale=-1.0)
    TT("dve", Pr, g_phi, g_cs, ALU.mult, GK, ["tab:all"])
    TT("dve", Pi, g_phi, g_sn, ALU.mult, GK, ["tab:all"])
    TT("dve", Qr, g_arg, g_cs, ALU.mult, GK, ["tab:all"])
    TT("dve", NQi, g_arg, g_sn, ALU.mult, GK, ["tab:all"])
    dump("Pr", Pr, [128, 32, 32], F32, ["tab:all"])
    dump("Bbr", Bbr, [128, 32, 16], F32, ["par:Bb"])
    for nm in ("are", "aim", "dt", "rho", "th", "er", "cs", "sn", "abr", "abi", "den", "cfr", "cfi"):
        dump("sm_" + nm, sm[nm], [128, 32], F32, PK + ["par:are", "par:aim"])
    dump("Braw", Braw_r, [128, 32, 16], F32, rawb)
    dump("Cr", Cr, [128, 32, 16], F32, ["par:Cr"])

    PHASE(5)
    _bias_split = len(_pending)
    DMA("sp", tab33[0:32, :], rel_tab, [], ["cst:tab33a"])
    S.add("dve", lambda e: e.memset(tab33[32:33, :], -30000.0), [], ["cst:tab33b"])
    ohs = AR.view("raw", RB_OFF + 10 * K, [33, 512], F32)
    DMA("sp", ohs, c_oh, [], ["raw:oh"])
    MM([(p7[0:8, :], tab33, ohs, True, True)], ["cst:tab33a", "cst:tab33b", "raw:oh", "par:Cr"], ["ps7"])
    CP("dve", fsb, p7[0:8, :], ["ps7"], ["cst:fsb"])
    DMA("sp", fd, fsb, ["cst:fsb"], ["fd:all"])
    hk = [AR.view("raw", RB_OFF + 12 * K + 512 * i, [128, 128], F32) for i in range(4)]
    for h in range(8):
        for jk in range(3):
            i = (h * 3 + jk) % 4
            src = bass.AP(fd_t, h * 512 + 128 * jk, [[1, 128], [1, 128]])
            DMA("sp", hk[i], src, ["fd:all"], ["raw:hk%d" % i])
            pb_i = 6 + (h * 3 + jk) % 2
            MM([(psf(pb_i)[:, 0:128], hk[i], anti, True, True)], ["raw:hk%d" % i, "cst:anti"], ["ps%d" % pb_i])
            CP("act", biasT[:, h // 4, jk, h % 4, :], psf(pb_i)[:, 0:128], ["ps%d" % pb_i], ["bias:all"])
    dump("bias", biasT, [128, 2, 3, 4, 128], BF16, ["bias:all"])

    _defer[0] = False
    PHASE(1)
    gain_b = gainT.unsqueeze(2).broadcast_to([128, 8, 128])

    p1_bank = {}

    def p1_front1(s):
        b = s % 2
        DMA("sp", xs[b], x_s[s], [], ["p1x:%d" % b])
        rms_front(xs[b], "p1x:%d" % b, hn[b], "p1h:%d" % b, b, ["p1x:%d" % b])
        bi = nb()
        p1_bank[("T", s)] = bi
        pT = psb(bi)
        TRS([(pT[:, 128 * k:128 * k + 128], hn[b][:, 128 * k:128 * k + 128], ident_b) for k in range(8)],
            ["p1h:%d" % b, "cst:ident_b"], ["ps%d" % bi])

    def p1_front2(s):
        b = s % 2
        bi = p1_bank[("T", s)]
        TT("dve", hTs[b], psb(bi).rearrange("p (k n) -> p k n", k=8), gain_b, ALU.mult,
           ["ps%d" % bi, "cst:gainT"], ["p1t:%d" % b])

    def p1_mm(s):
        b = s % 2
        bu = nb()
        MM([(psf(bu), hTs[b][:, k, :], Wu[:, k, :], k == 0, k == 7) for k in range(8)],
           ["p1t:%d" % b, "p1w:u"], ["ps%d" % bu])
        bk = nb()
        pkv = psf(bk)
        MM([(pkv[:, 0:128], Wk[:, k, :], hTs[b][:, k, :], k == 0, k == 7) for k in range(8)] +
           [(pkv[:, 128:256], Wv[:, k, :], hTs[b][:, k, :], k == 0, k == 7) for k in range(8)],
           ["p1t:%d" % b, "p1w:k", "p1w:v"], ["ps%d" % bk])
        p1_bank[("U", s)] = bu
        p1_bank[("KV", s)] = bk

    def p1_copy(s):
        bu, bk = p1_bank[("U", s)], p1_bank[("KV", s)]
        pkv = psf(bk)
        CP("act", U_tm[:, :, s, :], psf(bu).rearrange("p (g c) -> p g c", g=32), ["ps%d" % bu], ["utm:%d" % s])
        CP("act", kT_all[:, s::32], pkv[:, 0:128], ["ps%d" % bk], ["kT:all"])
        CP("act", vT_all[:, s::32], pkv[:, 128:256], ["ps%d" % bk], ["vT:all"])

    _bank_mod[0] = 6
    PREP_FIRST = True
    if PREP_FIRST:
        flush(_bias_split)
    p1_front1(0)
    p1_front1(1)
    p1_front2(0)
    for s in range(32):
        if s + 2 < 32:
            p1_front1(s + 2)
        if s + 1 < 32:
            p1_front2(s + 1)
        p1_mm(s)
        if s >= 1:
            p1_copy(s - 1)
        flush(0)
    p1_copy(31)
    flush()
    _bank_mod[0] = 8
    dump("utm", U_tm, [128, 32, 32, 16], BF16, ["utm:%d" % s for s in range(32)])
    dump("kT", kT_all, [128, 4096], BF16, ["kT:all"])

    PHASE(2)
    S.add("pool", lambda e: e.memset(v_all[:, :, :, 64:65], 1.0), [], ["vall:ones"])
    for j in range(4):
        bi = nb()
        pb_ = psb(bi)
        TRS([(pb_[:, 128 * i:128 * i + 128], vT_all[:, 128 * (8 * j + i):128 * (8 * j + i) + 128], ident_b)
             for i in range(8)], ["vT:all", "cst:ident_b"], ["ps%d" % bi])
        CP("act", v_all[:, 8 * j:8 * j + 8, :, 0:64],
           pb_.rearrange("p (i h d) -> p i h d", i=8, h=2), ["ps%d" % bi], ["vall:%d" % j])
    dump("vall", v_all, [128, 32, 2, 65], BF16, ["vall:%d" % j for j in range(4)] + ["vall:ones"])

    PHASE(3)
    U_g = AR.view("ug", RA_OFF, [128, 32, 4, 128], BF16)
    for g2 in range(16):
        bi = nb()
        pb_ = psb(bi)
        lst = []
        for gi in range(2):
            g = 2 * g2 + gi
            for k in range(4):
                lst.append((pb_[:, (gi * 4 + k) * 128:(gi * 4 + k) * 128 + 128],
                            U_tm[:, g, 8 * k:8 * k + 8, :].rearrange("p s c -> p (s c)"), ident_b))
        TRS(lst, ["utm:%d" % s for s in range(32)] + ["cst:ident_b"], ["ps%d" % bi])
        CP("act" if g2 % 2 == 0 else "dve", U_g[:, 2 * g2:2 * g2 + 2, :, :],
           pb_.rearrange("p (g k n) -> p g k n", g=2, k=4), ["ps%d" % bi], ["ug:%d" % g2])
    UG_ALL = ["ug:%d" % i for i in range(16)]

    def gen_L(g, b):
        Prg = Pr[:, g, :].unsqueeze(2).broadcast_to([128, 32, 16])
        Pig = Pi[:, g, :].unsqueeze(2).broadcast_to([128, 32, 16])
        Brg = Bbr[:, g, :].unsqueeze(1).broadcast_to([128, 32, 16])
        Big = Bbi[:, g, :].unsqueeze(1).broadcast_to([128, 32, 16])
        v = lambda t: t.rearrange("p (s c) -> p s c", s=32)
        rkeys = ["tab:all", "par:Bb"]
        TT("dve", v(tmpD[0]), Prg, Brg, ALU.mult, rkeys, ["tmpd:0"])
        TT("dve", v(tmpD[1]), Pig, Big, ALU.mult, rkeys, ["tmpd:1"])
        TT("pool", v(tmpP[1]), Pig, Brg, ALU.mult, rkeys, ["tmpp:1"])
        TT("dve", Lb[b][:, 0, :], tmpD[0], tmpD[1], ALU.subtract, ["tmpd:0", "tmpd:1"], ["lt:%dr" % b])
        TT("dve", v(tmpP[0]), Prg, Big, ALU.mult, rkeys, ["tmpp:0"])
        TT("pool", Lb[b][:, 1, :], tmpP[0], tmpP[1], ALU.add, ["tmpp:0", "tmpp:1"], ["lt:%di" % b])

    def gen_R(g, b):
        Qrg = Qr[:, g, :].unsqueeze(2).broadcast_to([128, 32, 16])
        NQg = NQi[:, g, :].unsqueeze(2).broadcast_to([128, 32, 16])
        Crg = Cr[:, g, :].unsqueeze(1).broadcast_to([128, 32, 16])
        Cig = Ci[:, g, :].unsqueeze(1).broadcast_to([128, 32, 16])
        v = lambda t: t.rearrange("p (s c) -> p s c", s=32)
        rkeys = ["tab:all", "par:Cr", "par:Ci"]
        TT("dve", v(tmpD[0]), Crg, Qrg, ALU.mult, rkeys, ["tmpd:0"])
        TT("dve", v(tmpD[1]), Cig, NQg, ALU.mult, rkeys, ["tmpd:1"])
        TT("pool", v(tmpP[0]), Crg, NQg, ALU.mult, rkeys, ["tmpp:0"])
        TT("dve", Rb[b][:, 0, :], tmpD[0], tmpD[1], ALU.add, ["tmpd:0", "tmpd:1"], ["rt:%dr" % b])
        TT("pool", v(tmpP[1]), Cig, Qrg, ALU.mult, rkeys, ["tmpp:1"])
        TT("pool", Rb[b][:, 1, :], tmpP[0], tmpP[1], ALU.subtract, ["tmpp:0", "tmpp:1"], ["rt:%di" % b])

    PHASE(6)
    Z = AR.view("z", RB_OFF, [128, 2, 32, 128], F32)
    for g in range(32):
        b = g % 2
        gen_L(g, b)
        bl = nb()
        pl_ = psb(bl)
        plv = pl_.rearrange("p (k r m) -> p k r m", k=4, r=2)
        TRS([(plv[:, k, ri, :], Lb[b][:, ri, 128 * k:128 * k + 128], ident_b) for k in range(4) for ri in range(2)],
            ["lt:%dr" % b, "lt:%di" % b, "cst:ident_b"], ["ps%d" % bl])
        CP("act", LTb[b].rearrange("p k r m -> p (k r m)"), pl_, ["ps%d" % bl], ["ltt:%d" % b])
        DMA("sp", lsc[g], Lb[b].rearrange("p r m -> p (r m)"), ["lt:%dr" % b, "lt:%di" % b], ["lsc:%d" % g])
        bz = nb()
        pz = psf(bz)
        MM([(pz[:, 128 * ri:128 * ri + 128], LTb[b][:, k, ri, :], U_g[:, g, k, :], k == 0, k == 3)
            for ri in range(2) for k in range(4)], ["ltt:%d" % b] + UG_ALL, ["ps%d" % bz])
        CP("act", Z[:, :, g, :], pz[:, 0:256].rearrange("p (r n) -> p r n", r=2), ["ps%d" % bz], ["z:%d" % g])
    Z_ALL = ["z:%d" % g for g in range(32)]
    dump("Z", Z, [128, 2, 32, 128], F32, Z_ALL)

    PHASE(7)
    Xd = AR.view("xd", GEN_OFF, [128, 2, 32, 128], BF16)
    Zv = Z.rearrange("p r g (q j) -> p r (g q) j", q=2)
    Xv = Xd.rearrange("p r g (q j) -> p r (g q) j", q=2)
    S.add("dve", lambda e: e.memset(Xv[0:64, :, :, 0:1], 0.0), [], ["xd:f"])
    S.add("pool", lambda e: e.memset(Xv[64:128, :, :, 63:64], 0.0), [], ["xd:b"])
    for step in range(1, 64):
        for (eng, lo, hi, cur, prev, tag) in (("dve", 0, 64, step, step - 1, "f"), ("pool", 64, 128, 63 - step, 64 - step, "b")):
            W_ = Wsc[lo:hi]
            sp_ = step % 2
            S_ = (Ssc if sp_ == 0 else Ssc2)[lo:hi]
            tag2 = tag + str(sp_)
            Xp = Zv[lo:hi, :, :, prev].unsqueeze(1).broadcast_to([64, 2, 2, 64])
            TT(eng, W_, AAf[lo:hi], Xp, ALU.mult, Z_ALL + ["par:AAf", "z:scan" + tag], ["scn:w" + tag])
            TT(eng, S_, W_[:, :, 0, :], W_[:, :, 1, :], ALU.add, ["scn:w" + tag], ["scn:s" + tag2])
            TT(eng, Zv[lo:hi, :, :, cur], Zv[lo:hi, :, :, cur], S_, ALU.add, ["scn:s" + tag2] + Z_ALL, ["z:scan" + tag])
            CP("act", Xv[lo:hi, :, :, cur], S_, ["scn:s" + tag2], ["xd:" + tag])
    dump("Xd", Xd, [128, 2, 32, 128], BF16, ["xd:f", "xd:b"])

    PHASE(8)
    yg = AR.view("yg", RC_OFF, [128, 32, 512], BF16)
    ident3 = ident_f.rearrange("p (t c) -> p t c", t=8)
    for g in range(32):
        b = g % 2
        DMA("sp", Lb[b].rearrange("p r m -> p (r m)"), lsc[g], ["lsc:%d" % g], ["lt:%dr" % b, "lt:%di" % b])
        gen_R(g, b)
        TT("dve", Dg[b].rearrange("p (t c) -> p t c", t=8), ident3,
           dB[:, 16 * g:16 * g + 16].unsqueeze(1).broadcast_to([128, 8, 16]), ALU.mult,
           ["cst:ident_f", "par:dB"], ["dg:%d" % b])
        lk = ["lt:%dr" % b, "lt:%di" % b, "rt:%dr" % b, "rt:%di" % b]
        pf_i, pb_i = nb(), nb()
        pf, pbk = psf(pf_i), psf(pb_i)
        MM([(pf, Lb[b][0:64, ri, 0:128], Rb[b][0:64, ri, :], ri == 0, ri == 1) for ri in range(2)] +
           [(pbk, Lb[b][64:128, ri, 384:512], Rb[b][64:128, ri, :], ri == 0, ri == 1) for ri in range(2)],
           lk, ["ps%d" % pf_i, "ps%d" % pb_i])
        TB = Tg[b].rearrange("p k m -> p (k m)")[:, 0:896].rearrange("p (m q) -> p m q", m=7)
        CP("act", TB[:, 4:7, :], pf[:, 128:512].rearrange("p (m q) -> p m q", m=3), ["ps%d" % pf_i], ["tg:%d_f" % b])
        CP("act", TB[:, 0:3, :], pbk[:, 0:384].rearrange("p (m q) -> p m q", m=3), ["ps%d" % pb_i], ["tg:%d_b" % b])
        TT("dve", bt1[0], pf[:, 0:128], mf, ALU.mult, ["ps%d" % pf_i, "cst:mf"], ["bt:1_0"])
        TT("dve", bt2[0], pbk[:, 384:512], mb, ALU.mult, ["ps%d" % pb_i, "cst:mb"], ["bt:2_0"])
        TT("dve", TB[:, 3, :], bt1[0], bt2[0], ALU.add, ["bt:1_0", "bt:2_0"], ["tg:%d_d" % b])
        py_i = nb()
        py = psf(py_i)
        MM([(py, Xd[:, ri, g, :], Rb[b][:, ri, :], ri == 0, False) for ri in range(2)] +
           [(py, U_g[:, g, k, :], TB[:, 3 - k:7 - k, :].rearrange("p m q -> p (m q)"), False, False)
            for k in range(4)] +
           [(py[:, 128 * k:128 * k + 128], U_g[:, g, k, :], Dg[b], False, k == 3) for k in range(4)],
           UG_ALL + ["tg:%d_f" % b, "tg:%d_b" % b, "tg:%d_d" % b, "dg:%d" % b, "xd:f", "xd:b", "rt:%dr" % b, "rt:%di" % b],
           ["ps%d" % py_i])
        ACT(yg[:, :, 16 * g:16 * g + 16], py.rearrange("p (t c) -> p t c", t=32), AF.Gelu_apprx_tanh,
            ["ps%d" % py_i], ["yg:%d" % g])
    YG_ALL = ["yg:%d" % g for g in range(32)]
    dump("tg", Tg[1], [128, 4, 512], BF16, ["tg:1_f", "tg:1_b", "tg:1_d"])
    dump("yg", yg, [128, 32, 512], BF16, YG_ALL)

    PHASE(9)
    ygT = AR.view("ygT", RB_OFF, [128, 4, 4096], BF16)
    ygT_v = ygT.rearrange("p c (n t) -> p c n t", t=32)
    cnt = 0
    for c in range(4):
        for tb in range(4):
            pi_ = nb()
            pb_ = psb(pi_)
            TRS([(pb_[:, 128 * i:128 * i + 128], yg[:, 8 * tb + i, 128 * c:128 * c + 128], ident_b) for i in range(8)],
                YG_ALL + ["cst:ident_b"], ["ps%d" % pi_])
            CP("act" if cnt % 2 == 0 else "dve", ygT_v[:, c, :, 8 * tb:8 * tb + 8],
               pb_.rearrange("p (i n) -> p n i", i=8), ["ps%d" % pi_], ["ygT:%d" % cnt])
            cnt += 1
    YGT_ALL = ["ygT:%d" % i for i in range(16)]
    dump("ygT", ygT, [128, 4, 4096], BF16, YGT_ALL)

    PHASE(10)
    attnT = AR.view("attnT", RA_OFF, [128, 4, 4096], BF16)
    o = RC_OFF
    def rc(prefix, shape, dt):
        nonlocal o
        esz = 2 if dt == BF16 else 4
        v = AR.view(prefix, o, shape, dt)
        o += (_prod(shape[1:]) * esz + 3) // 4 * 4
        return v
    Wq = rc("w2a", [128, 8, 512], BF16)
    Wza = rc("w2a", [128, 8, 512], BF16)
    xt2 = [rc("x2", [128, 1024], F32) for _ in range(2)]
    hn2 = [AR.view("h2", RD_OFF + 76 * K + 2 * K * i_, [128, 1024], BF16) for i_ in range(4)]
    hTb = [rc("hT2", [128, 8, 512], BF16) for _ in range(2)]
    qT = rc("qT", [128, 8, 512], BF16)
    zas = rc("zas", [128, 4, 512], BF16)
    PT = [rc("pt", [128, 3, 512], BF16) for _ in range(2)]
    den = [rc("den", [128, 8], F32) for _ in range(2)]
    at = [rc("at", [128, 256], F32) for _ in range(2)]
    ag = [rc("ag", [128, 512], BF16) for _ in range(2)]
    assert o <= RD_OFF + 32 * K, o
    A2_END = o
    Wq_p = Wq.rearrange("p k (g h d) -> p k g h d", g=4, h=2)
    for h_ in range(2):
        for g_ in range(4):
            c0_ = 256 * h_ + 64 * g_
            DMA("pool", Wq_p[:, :, g_, h_, :], w_in_v[:, :, c0_:c0_ + 64], [], ["w2a:q"])
    DMA("pool", Wza, w_in_v[:, :, 768:1280], [], ["w2a:za"])
    S.add("dve", lambda e: e.memset(qT, 0.0), [], ["qT:zero"] + ["qT:%d" % h_ for h_ in range(8)])

    def front(tok0, ntile, hT_dst, hkey, xt_l, hn_l, xpre, hpre, psbase, cnt0):
        for i in range(ntile):
            bb = (cnt0 + i) % 2
            DMA("sp", xt_l[bb], x[tok0 + 128 * i: tok0 + 128 * i + 128, :], [], ["%s:%d" % (xpre, bb)])
            rms_front(xt_l[bb], None, hn_l[bb], "%s:%d" % (hpre, bb), 2 + bb, ["%s:%d" % (xpre, bb)])
            pi_ = nb()
            pT_ = psb(pi_)
            TRS([(pT_[:, 128 * k:128 * k + 128], hn_l[bb][:, 128 * k:128 * k + 128], ident_b) for k in range(8)],
                ["%s:%d" % (hpre, bb), "cst:ident_b"], ["ps%d" % pi_])
            TT("dve", hT_dst[:, :, 128 * i:128 * i + 128], pT_.rearrange("p (k n) -> p k n", k=8), gain_b, ALU.mult,
               ["ps%d" % pi_, "cst:gainT"], [hkey + "_%d" % i])

    ucount = [0]
    pend_tr = []
    unit_hooks = {}

    def attention_block(jb):
        bb = jb % 2
        hT_ = hTb[bb]
        hkeys = ["hT2:%d_%d" % (bb, i) for i in range(4)]
        for hq in range(4):
            bi = nb()
            MM([(psf(bi), Wq[:, k, 128 * hq:128 * hq + 128], hT_[:, k, :], k == 0, k == 7) for k in range(8)],
               hkeys + ["w2a:q"], ["ps%d" % bi])
            TS("dve", qT[0:64, hq, :], psf(bi)[0:64, :], 0.125, None, ALU.mult, None, ["ps%d" % bi, "qT:zero"], ["qT:%d" % hq])
            TS("dve", qT[64:128, 4 + hq, :], psf(bi)[64:128, :], 0.125, None, ALU.mult, None, ["ps%d" % bi, "qT:zero"],
               ["qT:%d" % (4 + hq)])
        for tt in range(4):
            bi = nb()
            MM([(psf(bi), hT_[:, k, 128 * tt:128 * tt + 128], Wza[:, k, :], k == 0, k == 7) for k in range(8)],
               hkeys + ["w2a:za"], ["ps%d" % bi])
            ACT(zas[:, tt, :], psf(bi), AF.Silu, ["ps%d" % bi], ["zas:%d" % tt])

        def scores(tt, kvh):
            qi = 4 * jb + tt
            bi_ = qi % 16
            jks = [jk for jk in range(3) if 0 <= bi_ + jk - 1 <= 15]
            u = ucount[0]
            ucount[0] += 1
            pb2 = u % 2
            for jk in jks:
                kt = qi + jk - 1
                sb = nb()
                lst = [(psf(sb), ident_b, biasT[:, kvh, jk, :, :].rearrange("p g q -> p (g q)"), True, False)]
                for g in range(4):
                    h = 4 * kvh + g
                    lst.append((psf(sb)[:, 128 * g:128 * g + 128], kT_all[:, 128 * kt:128 * kt + 128],
                                qT[:, h, 128 * tt:128 * tt + 128], False, g == 3))
                MM(lst, ["kT:all", "bias:all", "cst:ident_b"] + ["qT:%d" % (4 * kvh + g) for g in range(4)], ["ps%d" % sb])
                ACT(PT[pb2][:, jk, :], psf(sb), AF.Exp, ["ps%d" % sb], ["pt:%d_%d" % (pb2, jk)])
            return (tt, kvh, qi, jks, pb2)

        def bias_mults(st):
            (tt, kvh, qi, jks, pb2) = st
            for jk in jks:
                ptv = PT[pb2][:, jk, :].rearrange("p (g q) -> p g q", g=4)
                TT("dve", ptv, ptv, biasT[:, 4 * kvh:4 * kvh + 4, jk, :], ALU.mult,
                   ["pt:%d_%d" % (pb2, jk), "bias:all"], ["pt:%d_%d" % (pb2, jk)])

        def pv_post(st):
            (tt, kvh, qi, jks, pb2) = st
            ab = qi % 2
            pvb = nb()
            pv = psf(pvb)[:, 0:260].rearrange("p (g d) -> p g d", g=4)
            lst = []
            for g in range(4):
                for jk in jks:
                    kt = qi + jk - 1
                    lst.append((pv[:, g, :], PT[pb2][:, jk, 128 * g:128 * g + 128], v_all[:, kt, kvh, :],
                                jk == jks[0], jk == jks[-1]))
            MM(lst, ["pt:%d_%d" % (pb2, jk) for jk in jks] + ["vall:ones"] + ["vall:%d" % j for j in range(4)], ["ps%d" % pvb])
            dn = den[pb2]
            TT("dve", dn[:, 0:4], pv[:, :, 64], esink[:, 4 * kvh:4 * kvh + 4], ALU.add, ["ps%d" % pvb, "cst:esink"], ["den:%d" % pb2])
            S.add("dve", lambda e, dn=dn: e.reciprocal(out=dn[:, 4:8], in_=dn[:, 0:4]), ["den:%d" % pb2], ["den:%d" % pb2])
            atv = at[pb2].rearrange("p (g d) -> p g d", g=4)
            TT("dve", atv, pv[:, :, 0:64], dn[:, 4:8].unsqueeze(2).broadcast_to([128, 4, 64]), ALU.mult,
               ["ps%d" % pvb, "den:%d" % pb2], ["at:%d" % pb2])
            TT("dve", ag[ab][:, 256 * kvh:256 * kvh + 256], at[pb2], zas[:, tt, 256 * kvh:256 * kvh + 256], ALU.mult,
               ["at:%d" % pb2, "zas:%d" % tt], ["ag:%d_%d" % (ab, kvh)])
            if kvh == 1:
                pend_tr.append((ab, qi))

        def do_tr(n):
            for _ in range(n):
                (ab, qi) = pend_tr.pop(0)
                pt_i = nb()
                TRS([(psb(pt_i)[:, 128 * c:128 * c + 128], ag[ab][:, 128 * c:128 * c + 128], ident_b) for c in range(4)],
                    ["ag:%d_0" % ab, "ag:%d_1" % ab, "cst:ident_b"], ["ps%d" % pt_i])
                CP("dve", attnT[:, :, 128 * qi:128 * qi + 128], psb(pt_i)[:, 0:512].rearrange("p (c n) -> p c n", c=4),
                   ["ps%d" % pt_i], ["attnT:%d" % qi])

        units = [(tt, kvh) for tt in range(4) for kvh in range(2)]
        pend = [scores(*units[0])]
        for i in range(len(units)):
            if i + 1 < len(units):
                pend.append(scores(*units[i + 1]))
            had = list(pend_tr)
            pv_post(pend.pop(0))
            if had:
                do_tr(len(had))
            for fn_ in unit_hooks.get(i, ()):
                fn_()

    def f2_rms(jb):
        for i in range(4):
            DMA("sp", xt2[i % 2], x[512 * jb + 128 * i: 512 * jb + 128 * i + 128, :], [], ["x2:%d" % (i % 2)])
            rms_front(xt2[i % 2], None, hn2[i], "h2:%d" % i, 2 + i % 2, ["x2:%d" % (i % 2)])

    def f2_T(jb):
        for i in range(4):
            pi_ = nb()
            pT_ = psb(pi_)
            TRS([(pT_[:, 128 * k:128 * k + 128], hn2[i][:, 128 * k:128 * k + 128], ident_b) for k in range(8)],
                ["h2:%d" % i, "cst:ident_b"], ["ps%d" % pi_])
            TT("dve", hTb[jb % 2][:, :, 128 * i:128 * i + 128], pT_.rearrange("p (k n) -> p k n", k=8), gain_b, ALU.mult,
               ["ps%d" % pi_, "cst:gainT"], ["hT2:%d_%d" % (jb % 2, i)])

    f2_rms(0)
    f2_T(0)
    f2_rms(1)
    for jb in range(8):
        unit_hooks.clear()
        if jb + 1 < 8:
            f2_T(jb + 1)
        if jb + 2 < 8:
            unit_hooks[3] = [lambda nj=jb + 2: f2_rms(nj)]
        attention_block(jb)
    if pend_tr:
        _jb_last = 7
        (ab, qi) = pend_tr.pop(0)
        pt_i = nb()
        TRS([(psb(pt_i)[:, 128 * c:128 * c + 128], ag[ab][:, 128 * c:128 * c + 128], ident_b) for c in range(4)],
            ["ag:%d_0" % ab, "ag:%d_1" % ab, "cst:ident_b"], ["ps%d" % pt_i])
        CP("dve", attnT[:, :, 128 * qi:128 * qi + 128], psb(pt_i)[:, 0:512].rearrange("p (c n) -> p c n", c=4),
           ["ps%d" % pt_i], ["attnT:%d" % qi])
    ATT_ALL = ["attnT:%d" % i for i in range(32)]
    dump("attnT", attnT, [128, 4, 4096], BF16, ATT_ALL)

    PHASE(11)
    o = RD_OFF + 32 * K
    Wg = rc("w2b", [128, 8, 2048], BF16)
    Wzs = rc("w2b", [128, 8, 512], BF16)
    Wglu = rc("w2b", [128, 4, 512], BF16)
    assert o <= AR.nbytes, o
    o = CST_END
    Wo = rc("w2c", [128, 8, 1024], BF16)
    fgB = rc("w2c", [128, 1024], F32)
    assert o <= RA_OFF
    o = RC_OFF
    Wba = rc("w2d", [128, 4, 1024], BF16)
    Wbs = rc("w2d", [128, 4, 1024], BF16)
    xt3 = [rc("x3", [128, 1024], F32) for _ in range(2)]
    hn3 = [rc("h3", [128, 1024], BF16) for _ in range(2)]
    hTc = [rc("hT3", [128, 8, 256], BF16) for _ in range(2)]
    zss = rc("zss", [128, 4, 256], F32)
    sg = rc("sg", [128, 4, 256], F32)
    ssmT = rc("ssmT", [128, 4, 256], BF16)
    sga = [rc("sga", [128, 256], F32) for _ in range(2)]
    sgs = [rc("sgs", [128, 256], F32) for _ in range(2)]
    m1 = [rc("m1", [128, 256], F32) for _ in range(2)]
    m2 = [rc("m2", [128, 256], F32) for _ in range(2)]
    mT = [rc("mT", [128, 8, 256], BF16) for _ in range(2)]
    junk3 = rc("junk3", [128, 1024], BF16)
    assert o <= RD_OFF + 32 * K, o
    o = RD_OFF + 32 * K + 44 * K
    xr = [rc("xr", [128, 1024], F32) for _ in range(2)]
    assert o <= AR.nbytes, o
    for c4 in range(4):
        DMA("pool", Wg[:, :, 512 * c4:512 * c4 + 512], w_in_v[:, :, 2304 + 512 * c4:2304 + 512 * c4 + 512], [], ["w2b:g%d" % c4])
    DMA("pool", Wzs, w_in_v[:, :, 1792:2304], [], ["w2b:zs"])
    DMA("pool", Wglu, w_glu.rearrange("(k p) e -> p k e", p=128), [], ["w2b:glu"])
    DMA("pool", Wo, w_out.rearrange("(k p) e -> p k e", p=128), [], ["w2c:o"])
    DMA("sp", fgB, fgain[0:1, :].partition_broadcast(128), [], ["w2c:fg"])
    DMA("pool", Wba, w_ba.rearrange("(k p) e -> p k e", p=128), [], ["w2d:ba"])
    DMA("pool", Wbs, w_bs.rearrange("(k p) e -> p k e", p=128), [], ["w2d:bs"])
    WG_ALL = ["w2b:g%d" % c for c in range(4)]
    ocnt = [0]

    def merge_p1(jb):
        bb = jb % 2
        hT_ = hTc[bb]
        hkeys = ["hT3:%d_%d" % (bb, i) for i in range(2)]
        tok0 = 256 * jb
        for c in range(4):
            bi = nb()
            MM([(psf(bi)[:, 0:256], Wzs[:, k, 128 * c:128 * c + 128], hT_[:, k, :], k == 0, k == 7) for k in range(8)],
               hkeys + ["w2b:zs"], ["ps%d" % bi])
            ACT(zss[:, c, :], psf(bi)[:, 0:256], AF.Silu, ["ps%d" % bi], ["zss:%d" % c])
        for c in range(4):
            bi = nb()
            MM([(psf(bi)[:, 0:256], Wglu[:, kc, 128 * c:128 * c + 128], ygT[:, kc, tok0:tok0 + 256], kc == 0, kc == 3)
                for kc in range(4)], YGT_ALL + ["w2b:glu"], ["ps%d" % bi])
            ACT(sg[:, c, :], psf(bi)[:, 0:256], AF.Sigmoid, ["ps%d" % bi, "cst:bgluT"], ["sg:%d" % c], bias=bgluT[:, c:c + 1])
        TT("pool", sg, sg, ygT[:, :, tok0:tok0 + 256], ALU.mult, ["sg:%d" % c for c in range(4)] + YGT_ALL, ["sg:all"])
        TT("pool", ssmT, sg, zss, ALU.mult, ["sg:all"] + ["zss:%d" % c for c in range(4)], ["ssmT:all"])

    def merge_p2(jb, hooks=None):
        bb = jb % 2
        hT_ = hTc[bb]
        hkeys = ["hT3:%d_%d" % (bb, i) for i in range(2)]
        tok0 = 256 * jb
        for e8 in range(8):
            for fn_ in (hooks or {}).get(e8, ()):
                fn_()
            eb = e8 % 2
            b_ga, b_gs, b_ba, b_bs = nb(), nb(), nb(), nb()
            MM([(psf(b_ga)[:, 0:256], Wg[:, k, 128 * e8:128 * e8 + 128], hT_[:, k, :], k == 0, k == 7) for k in range(8)],
               hkeys + WG_ALL, ["ps%d" % b_ga])
            ACT(sga[eb], psf(b_ga)[:, 0:256], AF.Sigmoid, ["ps%d" % b_ga, "cst:bgT"], ["sga:%d" % eb], bias=bgT[:, e8:e8 + 1])
            MM([(psf(b_gs)[:, 0:256], Wg[:, k, 1024 + 128 * e8:1024 + 128 * e8 + 128], hT_[:, k, :], k == 0, k == 7) for k in range(8)],
               hkeys + WG_ALL, ["ps%d" % b_gs])
            ACT(sgs[eb], psf(b_gs)[:, 0:256], AF.Sigmoid, ["ps%d" % b_gs, "cst:bgT"], ["sgs:%d" % eb], bias=bgT[:, 8 + e8:9 + e8])
            MM([(psf(b_ba)[:, 0:256], Wba[:, kc, 128 * e8:128 * e8 + 128], attnT[:, kc, tok0:tok0 + 256], kc == 0, kc == 3)
                for kc in range(4)], ATT_ALL + ["w2d:ba"], ["ps%d" % b_ba])
            MM([(psf(b_bs)[:, 0:256], Wbs[:, kc, 128 * e8:128 * e8 + 128], ssmT[:, kc, :], kc == 0, kc == 3)
                for kc in range(4)], ["ssmT:all", "w2d:bs"], ["ps%d" % b_bs])
            TT("dve", m1[eb], psf(b_ba)[:, 0:256], sga[eb], ALU.mult, ["ps%d" % b_ba, "sga:%d" % eb], ["m1:%d" % eb])
            TT("dve", m2[eb], psf(b_bs)[:, 0:256], sgs[eb], ALU.mult, ["ps%d" % b_bs, "sgs:%d" % eb], ["m2:%d" % eb])
            TT("dve", mT[bb][:, e8, :], m1[eb], m2[eb], ALU.add, ["m1:%d" % eb, "m2:%d" % eb], ["mT:%d_%d" % (bb, e8)])

    out_pend = []

    def merge_out(jb):
        bb = jb % 2
        tok0 = 256 * jb
        mkeys = ["mT:%d_%d" % (bb, e8) for e8 in range(8)]
        for tt in range(2):
            ob = ocnt[0] % 2
            ocnt[0] += 1
            t0 = tok0 + 128 * tt
            out_pend.append((jb, tt, ob, t0))
            DMA("sp", xr[ob], x[t0:t0 + 128, :], [], ["xr:%d" % ob])
            for half in range(2):
                bo = nb()
                MM([(psf(bo), mT[bb][:, e8, 128 * tt:128 * tt + 128], Wo[:, e8, 512 * half:512 * half + 512], e8 == 0, e8 == 7)
                    for e8 in range(8)], mkeys + ["w2c:o"], ["ps%d" % bo])
                TT("dve", xr[ob][:, 512 * half:512 * half + 512], psf(bo), xr[ob][:, 512 * half:512 * half + 512], ALU.add,
                   ["ps%d" % bo, "xr:%d" % ob], ["xr:%d" % ob])

    def merge_fin():
        while out_pend:
            (jb, tt, ob, t0) = out_pend.pop(0)
            col = 4 + ob
            ssc = ss_t[:, col:col + 1]
            ACT(junk3, xr[ob], AF.Square, ["xr:%d" % ob], ["junk3:0", "cst:ss%d" % col], accum_out=ssc)
            ACT(ssc, ssc, AF.Sqrt, ["cst:ss%d" % col, "cst:eps"], ["cst:ss%d" % col], scale=1.0 / 1024.0, bias=epst)
            S.add("dve", lambda e, ssc=ssc: e.reciprocal(out=ssc, in_=ssc), ["cst:ss%d" % col], ["cst:ss%d" % col])
            STT(xr[ob], xr[ob], ssc, fgB, ALU.mult, ALU.mult, ["xr:%d" % ob, "cst:ss%d" % col, "w2c:fg"], ["xr:%d" % ob])
            DMA("pool", out[t0:t0 + 128, :], xr[ob], ["xr:%d" % ob], ["out:%d" % (2 * jb + tt)])

    def front3_rms(jb):
        for i in range(2):
            DMA("sp", xt3[i], x[256 * jb + 128 * i: 256 * jb + 128 * i + 128, :], [], ["x3:%d" % i])
            rms_front(xt3[i], None, hn3[i], "h3:%d" % i, 2 + i, ["x3:%d" % i])

    def front3_T(jb):
        bb = jb % 2
        for i in range(2):
            pi_ = nb()
            pT_ = psb(pi_)
            TRS([(pT_[:, 128 * k:128 * k + 128], hn3[i][:, 128 * k:128 * k + 128], ident_b) for k in range(8)],
                ["h3:%d" % i, "cst:ident_b"], ["ps%d" % pi_])
            TT("dve", hTc[bb][:, :, 128 * i:128 * i + 128], pT_.rearrange("p (k n) -> p k n", k=8), gain_b, ALU.mult,
               ["ps%d" % pi_, "cst:gainT"], ["hT3:%d_%d" % (bb, i)])

    front3_rms(0)
    front3_T(0)
    front3_rms(1)
    for jb in range(16):
        if jb + 1 < 16:
            front3_T(jb + 1)
        merge_p1(jb)
        if jb >= 1:
            merge_out(jb - 1)
        hk_ = {2: [merge_fin]}
        if jb + 2 < 16:
            hk_[5] = [lambda nj=jb + 2: front3_rms(nj)]
        merge_p2(jb, hk_)
    merge_out(15)
    merge_fin()
    PHASE(0)
    S.add("sp", lambda e: e.nop(), ["out:%d" % i for i in range(32)] + ["dbgout:" + n for n in dbg_out], [])
    S.emit()
    return nc, dbg_out


def _t5_bucket_np(rel):
    half = 16
    ret = (rel > 0).astype(np.int64) * half
    n = np.abs(rel)
    max_exact = half // 2
    nf = np.maximum(n, 1).astype(np.float32)
    large = max_exact + (np.log(nf / np.float32(max_exact)) / np.float32(math.log(128 / max_exact))
                         * np.float32(half - max_exact)).astype(np.int32)
    large = np.minimum(large, half - 1)
    return ret + np.where(n < max_exact, n, large)


def _t5_bucket_table(rel):
    try:
        import jax
        import jax.numpy as jnp
        cpu = jax.devices("cpu")[0]
        with jax.default_device(cpu):
            r = jnp.asarray(rel, dtype=jnp.int32)
            half = 16
            ret = (r > 0).astype(jnp.int32) * half
            n = jnp.abs(r)
            max_exact = half // 2
            nf = jnp.maximum(n, 1).astype(jnp.float32)
            large = max_exact + (jnp.log(nf / max_exact) / math.log(128 / max_exact)
                                 * (half - max_exact)).astype(jnp.int32)
            large = jnp.minimum(large, half - 1)
            return np.asarray(ret + jnp.where(n < max_exact, n, large)).astype(np.int64)
    except Exception:
        return _t5_bucket_np(rel)


def _host_constants():
    c = {}
    c["c_ident"] = np.eye(128, dtype=np.float32)
    c["c_anti"] = np.ascontiguousarray(np.eye(128, dtype=np.float32)[::-1])
    c["c_sval"] = np.broadcast_to(np.arange(32, dtype=np.float32), (128, 32)).copy()
    sg = np.ones((128, 1), np.float32)
    sg[:64] = -1.0
    c["c_sig"] = sg
    s_idx = np.arange(128)[:, None] // 16
    t_idx = np.arange(128)[None, :] // 16
    c["c_mf"] = (t_idx >= s_idx).astype(np.float32)
    c["c_mb"] = (s_idx >= t_idx).astype(np.float32)
    oh = np.zeros((33, 512), np.float32)
    r = np.arange(511)
    rel = r - 255
    bk = _t5_bucket_table(rel)
    oh[bk, r] = 1.0
    oh[32, r] = (np.abs(rel) > 128).astype(np.float32)
    oh[32, 511] = 1.0
    c["c_oh"] = oh
    return c


_CACHE = {}


def _get_program():
    if "nc" not in _CACHE:
        _CACHE["nc"] = build_program()[0]
    return _CACHE["nc"]


def make_in_maps(inputs):
    f = lambda a: np.ascontiguousarray(np.asarray(a, dtype=np.float32))
    shared = {
        "norm_gain": f(inputs["norm_gain"]).reshape(1, 1024),
        "w_in": f(inputs["w_in"]).reshape(1024, 4352),
        "b_gate": f(inputs["b_gate"]).reshape(1, 2048),
        "attn_sink": f(inputs["attn_sink"]).reshape(1, 8),
        "rel_bias_table": f(inputs["rel_bias_table"]).reshape(32, 8),
        "ssm_a_re": f(inputs["ssm_a_re"]).reshape(2, 32, 64),
        "ssm_a_im": f(inputs["ssm_a_im"]).reshape(2, 32, 64),
        "ssm_log_dt": f(inputs["ssm_log_dt"]).reshape(2, 32),
        "ssm_b_re": f(inputs["ssm_b_re"]).reshape(2, 32, 64, 16),
        "ssm_b_im": f(inputs["ssm_b_im"]).reshape(2, 32, 64, 16),
        "ssm_c_re": f(inputs["ssm_c_re"]).reshape(2, 32, 16, 64),
        "ssm_c_im": f(inputs["ssm_c_im"]).reshape(2, 32, 16, 64),
        "ssm_d": f(inputs["ssm_d"]).reshape(1, 512),
        "w_glu": f(inputs["w_glu"]).reshape(512, 512),
        "b_glu": f(inputs["b_glu"]).reshape(1, 512),
        "w_branch_attn": f(inputs["w_branch_attn"]).reshape(512, 1024),
        "w_branch_ssm": f(inputs["w_branch_ssm"]).reshape(512, 1024),
        "w_out": f(inputs["w_out"]).reshape(1024, 1024),
        "final_norm_gain": f(inputs["final_norm_gain"]).reshape(1, 1024),
    }
    shared.update(_host_constants())
    xs = f(inputs["x"])
    maps = []
    for c in range(8):
        m = dict(shared)
        m["x"] = np.ascontiguousarray(xs[2 * c:2 * c + 2].reshape(4096, 1024))
        maps.append(m)
    return maps


def kernel(**inputs):
    nc = _get_program()
    in_maps = make_in_maps(inputs)
    res = run_bass_kernel_spmd(nc, in_maps, core_ids=list(range(8)))
    outs = [np.asarray(r["out"]).reshape(2, 2048, 1024) for r in res.results]
    return np.concatenate(outs, axis=0).astype(np.float32)
```

```python
import math
from contextlib import ExitStack
import numpy as np
import concourse.bass as bass
import concourse.mybir as mybir
from concourse.bass_utils import run_bass_kernel_spmd

F32 = mybir.dt.float32
BF16 = mybir.dt.bfloat16
I32 = mybir.dt.int32
ALU = mybir.AluOpType
AF = mybir.ActivationFunctionType
ENGS = ("pe", "act", "dve", "pool", "sp")
STRICT_SAME_ENGINE = True
EPS = 1e-6
TWO_PI = 2.0 * math.pi


def _prod(xs):
    r = 1
    for v in xs:
        r *= int(v)
    return r


class Sched:
    N_DMA_SEMS = 40

    def __init__(self, nc):
        self.nc = nc
        self.ops = []
        self.last_w = {}
        self.readers = {}
        self.guards = {}

    def guard(self, new_prefix, old_prefixes):
        s = set()
        for k, w in self.last_w.items():
            if k.split(":")[0] in old_prefixes and w is not None:
                s.add(w)
        for k, rs in self.readers.items():
            if k.split(":")[0] in old_prefixes:
                s.update(rs)
        best = {}
        out = set()
        for i in s:
            o = self.ops[i]
            if o["dma"]:
                out.add(i)
            else:
                best[o["eng"]] = max(best.get(o["eng"], -1), i)
        out.update(best.values())
        self.guards.setdefault(new_prefix, set()).update(out)

    def add(self, eng, fn, reads=(), writes=(), dma=False):
        idx = len(self.ops)
        psr = [k for k in reads if k.startswith("ps")]
        writes = list(writes) + [k for k in psr if k not in writes]
        deps = set()
        raw = set()
        for k in list(reads) + list(writes):
            g = self.guards.get(k.split(":")[0])
            if g:
                deps |= g
                raw |= g
        for k in reads:
            w = self.last_w.get(k)
            if w is not None:
                deps.add(w)
                raw.add(w)
        for k in writes:
            w = self.last_w.get(k)
            if w is not None:
                deps.add(w)
            for r in self.readers.get(k, ()):
                deps.add(r)
        deps.discard(idx)
        for k in reads:
            self.readers.setdefault(k, []).append(idx)
        for k in writes:
            self.last_w[k] = idx
            self.readers[k] = []
        fdeps = []
        for d in deps:
            p = self.ops[d]
            if (not p["dma"]) and (not dma) and p["eng"] == eng:
                if eng == "pe" or (d not in raw and not STRICT_SAME_ENGINE):
                    continue
            fdeps.append(d)
        self.ops.append(dict(eng=eng, fn=fn, deps=sorted(fdeps), dma=dma, idx=idx))
        return idx

    def emit(self):
        nc = self.nc
        ops = self.ops
        needed = set()
        for o in ops:
            needed.update(o["deps"])
        cnt = {e: 0 for e in ENGS}
        for o in ops:
            if (not o["dma"]) and o["idx"] in needed:
                cnt[o["eng"]] += 1
                o["ticket"] = cnt[o["eng"]]
        dma_ops = [o for o in ops if o["dma"]]
        pools = {"sp": (0, 28), "pool": (28, 12), "act": (40, 0)}
        nd = 40
        sem_val = [0] * nd
        qcnt = {"sp": 0, "pool": 0}
        for o in dma_ops:
            base, n = pools[o["eng"]]
            s = base + qcnt[o["eng"]] % n
            qcnt[o["eng"]] += 1
            o["dsem"] = s
            o["dprev"] = sem_val[s]
            sem_val[s] += 16
            o["dval"] = sem_val[s]
        with ExitStack() as st:
            esem = {e: st.enter_context(nc.semaphore("sem_" + e)) for e in ENGS}
            dsem = [st.enter_context(nc.semaphore("dsem%d" % i)) for i in range(nd)]
            block = st.enter_context(nc.Block())
            streams = {e: [o for o in ops if o["eng"] == e] for e in ENGS}

            def make_body(e):
                def body(eng):
                    waited = {f: 0 for f in ENGS}
                    dwaited = [0] * nd
                    for o in streams[e]:
                        for d in o["deps"]:
                            p = ops[d]
                            if p["dma"]:
                                s = p["dsem"]
                                if dwaited[s] < p["dval"]:
                                    eng.wait_ge(dsem[s], p["dval"])
                                    dwaited[s] = p["dval"]
                            else:
                                f = p["eng"]
                                if waited[f] < p["ticket"]:
                                    eng.wait_ge(esem[f], p["ticket"])
                                    waited[f] = p["ticket"]
                        if o["dma"]:
                            s = o["dsem"]
                            if o["dprev"] > 0 and dwaited[s] < o["dprev"]:
                                eng.wait_ge(dsem[s], o["dprev"])
                                dwaited[s] = o["dprev"]
                            ins = o["fn"](eng)
                            ins.then_inc(dsem[s], 16)
                        else:
                            ins = o["fn"](eng)
                            if "ticket" in o:
                                ins.then_inc(esem[e], 1)
                return body

            block.tensor(make_body("pe"))
            block.scalar(make_body("act"))
            block.vector(make_body("dve"))
            block.gpsimd(make_body("pool"))
            block.sync(make_body("sp"))


class Arena:
    def __init__(self, nc, S, nbytes):
        self.t = nc.alloc_sbuf_tensor("arena", [128, nbytes // 2], BF16)
        self.S = S
        self.nbytes = nbytes
        self.allocs = []

    def view(self, prefix, off, shape, dt):
        esz = 2 if dt == BF16 else 4
        nb = _prod(shape[1:]) * esz
        assert off % 4 == 0 and off + nb <= self.nbytes, (prefix, off, nb)
        olds = set(p for (p, s, e) in self.allocs if p != prefix and s < off + nb and off < e)
        if olds:
            self.S.guard(prefix, olds)
        self.allocs.append((prefix, off, off + nb))
        ap = self.t[:, off // 2: off // 2 + nb // 2]
        if dt != BF16:
            ap = ap.bitcast(dt)
        if len(shape) == 3:
            ap = ap.rearrange("p (a b) -> p a b", a=shape[1])
        elif len(shape) == 4:
            ap = ap.rearrange("p (a b c) -> p a b c", a=shape[1], b=shape[2])
        elif len(shape) == 5:
            ap = ap.rearrange("p (a b c d) -> p a b c d", a=shape[1], b=shape[2], c=shape[3])
        if shape[0] < 128:
            ap = ap[0:shape[0]]
        return ap


def build_program(dbg=(), phase_limit=99):
    nc = bass.Bass("TRN2", target_bir_lowering=False)
    S = Sched(nc)
    _phase = [0]
    _orig_add = S.add

    _pending = []
    _defer = [False]

    def _add(eng, fn, reads=(), writes=(), dma=False):
        if _phase[0] > phase_limit:
            return None
        if _defer[0]:
            _pending.append((eng, fn, list(reads), list(writes), dma))
            return None
        return _orig_add(eng, fn, reads, writes, dma)
    S.add = _add

    def flush(n=None):
        k = len(_pending) if n is None else min(n, len(_pending))
        for _ in range(k):
            _orig_add(*_pending.pop(0))

    def PHASE(n):
        _phase[0] = n

    def din(name, shape):
        return nc.dram_tensor(name, shape, F32, kind="ExternalInput").ap()

    x = din("x", [4096, 1024])
    norm_gain = din("norm_gain", [1, 1024])
    w_in = din("w_in", [1024, 4352])
    b_gate = din("b_gate", [1, 2048])
    attn_sink = din("attn_sink", [1, 8])
    rel_tab = din("rel_bias_table", [32, 8])
    a_re = din("ssm_a_re", [2, 32, 64])
    a_im = din("ssm_a_im", [2, 32, 64])
    log_dt = din("ssm_log_dt", [2, 32])
    b_re = din("ssm_b_re", [2, 32, 64, 16])
    b_im = din("ssm_b_im", [2, 32, 64, 16])
    c_re = din("ssm_c_re", [2, 32, 16, 64])
    c_im = din("ssm_c_im", [2, 32, 16, 64])
    ssm_d = din("ssm_d", [1, 512])
    w_glu = din("w_glu", [512, 512])
    b_glu = din("b_glu", [1, 512])
    w_ba = din("w_branch_attn", [512, 1024])
    w_bs = din("w_branch_ssm", [512, 1024])
    w_out = din("w_out", [1024, 1024])
    fgain = din("final_norm_gain", [1, 1024])
    c_ident = din("c_ident", [128, 128])
    c_anti = din("c_anti", [128, 128])
    c_sval = din("c_sval", [128, 32])
    c_sig = din("c_sig", [128, 1])
    c_mf = din("c_mf", [128, 128])
    c_mb = din("c_mb", [128, 128])
    c_oh = din("c_oh", [33, 512])
    out = nc.dram_tensor("out", [4096, 1024], F32, kind="ExternalOutput").ap()
    fd_t = nc.dram_tensor("fd_scratch", [8, 512], F32, kind="Internal")
    fd = fd_t.ap()
    lsc = nc.dram_tensor("l_scratch", [32, 128, 1024], BF16, kind="Internal").ap()
    dbg_out = {}

    AR = Arena(nc, S, 212736)
    K = 1024
    ps = [nc.alloc_psum_tensor("ps%d" % i, [128, 512], F32) for i in range(8)]

    _bank = [0]
    _bank_mod = [8]

    def nb():
        i = _bank[0] % _bank_mod[0]
        _bank[0] += 1
        return i

    def psf(i):
        return ps[i][:]

    def psb(i):
        return ps[i][:].bitcast(BF16)

    def DMA(q, o, i, reads, writes, slow=False):
        if slow:
            S.add(q, lambda e: e.dma_start(out=o, in_=i, allow_slow_non_contiguous=True), reads, writes, dma=True)
        else:
            S.add(q, lambda e: e.dma_start(out=o, in_=i), reads, writes, dma=True)

    def ACT(o, i, func, reads, writes, **kw):
        S.add("act", lambda e: e.activation(out=o, in_=i, func=func, **kw), reads, writes)

    def TT(eng, o, a, b, op, reads, writes):
        S.add(eng, lambda e: e.tensor_tensor(out=o, in0=a, in1=b, op=op), reads, writes)

    def TS(eng, o, a, s1, s2, op0, op1, reads, writes):
        if op1 is None:
            S.add(eng, lambda e: e.tensor_scalar(out=o, in0=a, scalar1=s1, scalar2=None, op0=op0), reads, writes)
        else:
            S.add(eng, lambda e: e.tensor_scalar(out=o, in0=a, scalar1=s1, scalar2=s2, op0=op0, op1=op1), reads, writes)

    def STT(o, a, sc, b, op0, op1, reads, writes):
        S.add("dve", lambda e: e.scalar_tensor_tensor(out=o, in0=a, scalar=sc, in1=b, op0=op0, op1=op1), reads, writes)

    def CP(eng, o, i, reads, writes):
        if eng == "act":
            ACT(o, i, AF.Copy, reads, writes)
        else:
            S.add(eng, lambda e: e.tensor_copy(out=o, in_=i), reads, writes)

    def MM(lst, reads, writes):
        def fn(e):
            ins = None
            for (o, l, r, st, sp) in lst:
                ins = e.matmul(o, lhsT=l, rhs=r, start=st, stop=sp)
            return ins
        S.add("pe", fn, reads, writes)

    def TRS(lst, reads, writes):
        def fn(e):
            ins = None
            for (o, i, idn) in lst:
                ins = e.transpose(o, in_=i, identity=idn)
            return ins
        S.add("pe", fn, reads, writes)

    def dump(name, ap, shape, dt, reads):
        if name not in dbg:
            return
        t = nc.dram_tensor("dbg_" + name, shape, dt, kind="ExternalOutput").ap()
        dbg_out[name] = t
        DMA("sp", t, ap, reads, ["dbgout:" + name])

    ident_f = AR.view("cst", 0, [128, 128], F32)
    ident_b = AR.view("cst", 512, [128, 128], BF16)
    mf = AR.view("cst", 768, [128, 128], F32)
    mb = AR.view("cst", 1280, [128, 128], F32)
    sval = AR.view("cst", 1792, [128, 32], F32)
    sig = AR.view("cst", 1920, [128, 1], F32)
    epst = AR.view("cst", 1924, [128, 1], F32)
    gainT = AR.view("cst", 1928, [128, 8], F32)
    esink = AR.view("cst", 1960, [128, 8], F32)
    bgT = AR.view("cst", 1992, [128, 16], F32)
    bgluT = AR.view("cst", 2056, [128, 4], F32)
    ss_t = AR.view("cst", 2072, [128, 8], F32)
    anti = AR.view("cst", 2176, [128, 128], F32)
    tab33 = AR.view("cst", 2688, [33, 8], F32)
    fsb = AR.view("cst", 2720, [8, 512], F32)
    CST_END = 5 * K

    DMA("sp", ident_f, c_ident, [], ["cst:ident_f"])
    DMA("sp", anti, c_anti, [], ["cst:anti"])
    DMA("sp", mf, c_mf, [], ["cst:mf"])
    DMA("sp", mb, c_mb, [], ["cst:mb"])
    DMA("sp", sval, c_sval, [], ["cst:sval"])
    DMA("sp", sig, c_sig, [], ["cst:sig"])
    CP("dve", ident_b, ident_f, ["cst:ident_f"], ["cst:ident_b"])
    S.add("dve", lambda e: e.memset(epst, EPS), [], ["cst:eps"])
    vecraw = AR.view("raw", 5 * K + 8 * K + 8704 + 6 * K + 32 * K + 14 * K, [32, 128], F32)
    S.add("dve", lambda e: e.memset(vecraw, 0.0), [], ["raw:vr"])
    DMA("sp", vecraw[0:8, :], norm_gain[0].rearrange("(k p) -> k p", p=128), [], ["raw:vr"])
    DMA("sp", vecraw[8:24, :], b_gate[0].rearrange("(k p) -> k p", p=128), [], ["raw:vr"])
    DMA("sp", vecraw[24:28, :], b_glu[0].rearrange("(k p) -> k p", p=128), [], ["raw:vr"])
    TRS([(ps[7][:, 0:32], vecraw, ident_f[0:32, 0:32])], ["raw:vr", "cst:ident_f"], ["ps7"])
    CP("dve", gainT, ps[7][:, 0:8], ["ps7"], ["cst:gainT"])
    CP("dve", bgT, ps[7][:, 8:24], ["ps7"], ["cst:bgT"])
    CP("dve", bgluT, ps[7][:, 24:28], ["ps7"], ["cst:bgluT"])
    DMA("sp", esink, attn_sink[0:1, :].partition_broadcast(128), [], ["cst:esink"])
    ACT(esink, esink, AF.Exp, ["cst:esink"], ["cst:esink"])

    KT_OFF = CST_END
    VALL_OFF = KT_OFF + 8 * K
    BIAS_OFF = VALL_OFF + 8704
    RA_OFF = BIAS_OFF + 6 * K
    RB_OFF = RA_OFF + 32 * K
    RC_OFF = RB_OFF + 32 * K
    RD_OFF = RC_OFF + 32 * K
    assert RD_OFF % 4 == 0
    kT_all = AR.view("kT", KT_OFF, [128, 4096], BF16)
    v_all = AR.view("vall", VALL_OFF, [128, 32, 2, 65], BF16)
    biasT = AR.view("bias", BIAS_OFF, [128, 2, 3, 4, 128], BF16)

    w_in_v = w_in.rearrange("(k p) e -> p k e", p=128)

    o = RA_OFF
    Wu = AR.view("p1w", o, [128, 8, 512], BF16); o += 8 * K
    Wk = AR.view("p1w", o, [128, 8, 128], BF16); o += 2 * K
    Wv = AR.view("p1w", o, [128, 8, 128], BF16); o += 2 * K
    xs = [AR.view("p1x", o + 4 * K * i, [128, 1024], F32) for i in range(2)]; o += 8 * K
    hn = [AR.view("p1h", o + 2 * K * i, [128, 1024], BF16) for i in range(2)]; o += 4 * K
    hTs = [AR.view("p1t", o + 2 * K * i, [128, 8, 128], BF16) for i in range(2)]; o += 4 * K
    assert o <= RA_OFF + 32 * K
    U_tm = AR.view("utm", RC_OFF, [128, 32, 32, 16], BF16)
    VT_OFF = RB_OFF + 16 * K
    vT_all = AR.view("vT", VT_OFF, [128, 4096], BF16)

    DMA("pool", Wu, w_in_v[:, :, 1280:1792], [], ["p1w:u"])
    DMA("pool", Wk, w_in_v[:, :, 512:640], [], ["p1w:k"])
    DMA("pool", Wv, w_in_v[:, :, 640:768], [], ["p1w:v"])

    x_s = x.rearrange("(n s) d -> s n d", s=32)

    def rms_front(xt, xkey, hnt, hkey, col, reads_x):
        ssc = ss_t[:, col:col + 1]
        ACT(hnt, xt, AF.Square, reads_x, [hkey, "cst:ss%d" % col], accum_out=ssc)
        ACT(ssc, ssc, AF.Sqrt, ["cst:ss%d" % col, "cst:eps"], ["cst:ss%d" % col], scale=1.0 / 1024.0, bias=epst)
        S.add("dve", lambda e: e.reciprocal(out=ssc, in_=ssc), ["cst:ss%d" % col], ["cst:ss%d" % col])
        TS("dve", hnt, xt, ssc, None, ALU.mult, None, reads_x + ["cst:ss%d" % col], [hkey])

    _defer[0] = True
    PHASE(4)
    o = RD_OFF
    def rd(prefix, shape, dt):
        nonlocal o
        esz = 2 if dt == BF16 else 4
        v = AR.view(prefix, o, shape, dt)
        o += (_prod(shape[1:]) * esz + 3) // 4 * 4
        return v

    Pr = rd("tab", [128, 32, 32], F32)
    Pi = rd("tab", [128, 32, 32], F32)
    Qr = rd("tab", [128, 32, 32], F32)
    NQi = rd("tab", [128, 32, 32], F32)
    Bbr = rd("par", [128, 32, 16], F32)
    Bbi = rd("par", [128, 32, 16], F32)
    Cr = rd("par", [128, 32, 16], F32)
    Ci = rd("par", [128, 32, 16], F32)
    dB = rd("par", [128, 512], F32)
    sm = {}
    for nm in ("are", "aim", "dt", "rho", "th", "er", "cs", "sn", "abr", "abi", "den", "cfr", "cfi", "t1", "t2",
               "rhop", "thp", "a32r", "a32i", "e32", "kf"):
        sm[nm] = rd("par", [128, 32], F32)
    ki32 = rd("par", [128, 1024], I32)
    AAf = rd("par", [128, 2, 2, 64], F32)
    Wsc = rd("scn", [128, 2, 2, 64], F32)
    Ssc = rd("scn", [128, 2, 64], F32)
    Ssc2 = rd("scn", [128, 2, 64], F32)
    Lb = [rd("lt", [128, 2, 512], BF16) for _ in range(2)]
    Rb = [rd("rt", [128, 2, 512], BF16) for _ in range(2)]
    tmpD = [rd("tmpd", [128, 512], F32) for _ in range(2)]
    tmpP = [rd("tmpp", [128, 512], F32) for _ in range(2)]
    LTb = [rd("ltt", [128, 4, 2, 128], BF16) for _ in range(2)]
    Tg = [rd("tg", [128, 4, 512], BF16) for _ in range(2)]
    Dg = [rd("dg", [128, 128], BF16) for _ in range(2)]
    bt1 = [rd("bt", [128, 128], F32) for _ in range(2)]
    bt2 = [rd("bt", [128, 128], F32) for _ in range(2)]
    g_arg = rd("gen", [128, 32, 32], F32)
    g_phi = rd("gen", [128, 32, 32], F32)
    g_sn = rd("gen", [128, 32, 32], F32)
    g_cs = rd("gen", [128, 32, 32], F32)
    assert o <= AR.nbytes, o
    GEN_OFF = o - 16 * K
    Braw_r = AR.view("raw", RB_OFF, [128, 32, 16], F32)
    Braw_i = AR.view("raw", RB_OFF + 2 * K, [128, 32, 16], F32)
    Craw_r = AR.view("raw", RB_OFF + 4 * K, [128, 4, 128], F32)
    Craw_i = AR.view("raw", RB_OFF + 6 * K, [128, 4, 128], F32)
    Araw_r = AR.view("raw", RB_OFF + 8 * K, [32, 128], F32)
    Araw_i = AR.view("raw", RB_OFF + 8 * K + 512, [32, 128], F32)

    for d in range(2):
        DMA("sp", Braw_r[64 * d:64 * d + 64], b_re[d].rearrange("g p c -> p g c"), [], ["raw:br%d" % d])
        DMA("sp", Braw_i[64 * d:64 * d + 64], b_im[d].rearrange("g p c -> p g c"), [], ["raw:bi%d" % d])
        DMA("sp", Araw_r[:, 64 * d:64 * d + 64], a_re[d], [], ["raw:ar%d" % d])
        DMA("sp", Araw_i[:, 64 * d:64 * d + 64], a_im[d], [], ["raw:ai%d" % d])
        DMA("sp", sm["dt"][64 * d:64 * d + 64, :], log_dt[d:d + 1, :].partition_broadcast(64), [], ["par:dt%d" % d])
        for t in range(4):
            DMA("sp", Craw_r[:, t, 64 * d:64 * d + 64],
                c_re[d].rearrange("g c p -> (g c) p")[128 * t:128 * t + 128, :], [], ["raw:cr%d%d" % (d, t)])
            DMA("sp", Craw_i[:, t, 64 * d:64 * d + 64],
                c_im[d].rearrange("g c p -> (g c) p")[128 * t:128 * t + 128, :], [], ["raw:ci%d%d" % (d, t)])
    DMA("sp", dB, ssm_d[0:1, :].partition_broadcast(128), [], ["par:dB"])
    p6 = psf(6)
    TRS([(p6[:, 0:32], Araw_r, ident_f[0:32, 0:32]), (p6[:, 32:64], Araw_i, ident_f[0:32, 0:32])],
        ["raw:ar0", "raw:ar1", "raw:ai0", "raw:ai1", "cst:ident_f"], ["ps6"])
    CP("dve", sm["are"], p6[:, 0:32], ["ps6"], ["par:are"])
    CP("dve", sm["aim"], p6[:, 32:64], ["ps6"], ["par:aim"])
    p7 = psf(7)
    TRS([(p7[:, 128 * t:128 * t + 128], Craw_r[:, t, :], ident_f) for t in range(4)],
        ["raw:cr%d%d" % (d, t) for d in range(2) for t in range(4)] + ["cst:ident_f"], ["ps7"])
    CP("dve", Cr.rearrange("p g c -> p (g c)"), p7, ["ps7"], ["par:Cr"])
    TRS([(p6[:, 128 * t:128 * t + 128], Craw_i[:, t, :], ident_f) for t in range(4)],
        ["raw:ci%d%d" % (d, t) for d in range(2) for t in range(4)] + ["cst:ident_f"], ["ps6"])
    CP("dve", Ci.rearrange("p g c -> p (g c)"), p6, ["ps6"], ["par:Ci"])

    PK = ["par:small"]

    def sincos(phi, sn_o, cs_o, kf, n, keyr, keyw):
        ki = ki32[:, 0:n]
        for (dst, shift) in ((sn_o, 0.0), (cs_o, math.pi / 2)):
            TS("dve", ki, phi, 1.0 / TWO_PI, shift / TWO_PI, ALU.mult, ALU.add, keyr, keyw)
            CP("dve", kf, ki, keyw, keyw)
            STT(dst, kf, -TWO_PI, phi, ALU.mult, ALU.add, keyr + keyw, keyw)
            TS("dve", dst, dst, shift, None, ALU.add, None, keyw, keyw)
            TS("dve", dst, dst, 3.14159, -3.14159, ALU.min, ALU.max, keyw, keyw)
            ACT(dst, dst, AF.Sin, keyw, keyw)

    rk = ["par:are", "par:aim", "par:dt0", "par:dt1"]
    ACT(sm["dt"], sm["dt"], AF.Exp, ["par:dt0", "par:dt1"], PK)
    TT("dve", sm["rho"], sm["are"], sm["dt"], ALU.mult, rk + PK, PK)
    TT("dve", sm["th"], sm["aim"], sm["dt"], ALU.mult, rk + PK, PK)
    sincos(sm["th"], sm["sn"], sm["cs"], sm["kf"], 32, PK, PK)
    ACT(sm["er"], sm["rho"], AF.Exp, PK, PK)
    TT("dve", sm["abr"], sm["er"], sm["cs"], ALU.mult, PK, PK)
    TT("dve", sm["abi"], sm["er"], sm["sn"], ALU.mult, PK, PK)
    TS("dve", sm["abr"], sm["abr"], -1.0, None, ALU.add, None, PK, PK)
    TT("dve", sm["t1"], sm["are"], sm["are"], ALU.mult, rk + PK, PK)
    TT("dve", sm["t2"], sm["aim"], sm["aim"], ALU.mult, rk + PK, PK)
    TT("dve", sm["den"], sm["t1"], sm["t2"], ALU.add, PK, PK)
    S.add("dve", lambda e: e.reciprocal(out=sm["den"], in_=sm["den"]), PK, PK)
    TT("dve", sm["t1"], sm["abr"], sm["are"], ALU.mult, rk + PK, PK)
    TT("dve", sm["t2"], sm["abi"], sm["aim"], ALU.mult, rk + PK, PK)
    TT("dve", sm["t1"], sm["t1"], sm["t2"], ALU.add, PK, PK)
    TT("dve", sm["cfr"], sm["t1"], sm["den"], ALU.mult, PK, PK)
    TT("dve", sm["t1"], sm["abi"], sm["are"], ALU.mult, rk + PK, PK)
    TT("dve", sm["t2"], sm["abr"], sm["aim"], ALU.mult, rk + PK, PK)
    TT("dve", sm["t1"], sm["t1"], sm["t2"], ALU.subtract, PK, PK)
    TT("dve", sm["cfi"], sm["t1"], sm["den"], ALU.mult, PK, PK)
    rawb = ["raw:br0", "raw:br1", "raw:bi0", "raw:bi1"]
    cfr_b = sm["cfr"].unsqueeze(2).broadcast_to([128, 32, 16])
    cfi_b = sm["cfi"].unsqueeze(2).broadcast_to([128, 32, 16])
    t512a = tmpD[0].rearrange("p (g c) -> p g c", g=32)
    t512b = tmpD[1].rearrange("p (g c) -> p g c", g=32)
    TT("dve", t512a, Braw_r, cfr_b, ALU.mult, rawb + PK, ["tmpd:0"])
    TT("dve", t512b, Braw_i, cfi_b, ALU.mult, rawb + PK, ["tmpd:1"])
    TT("dve", Bbr, t512a, t512b, ALU.subtract, ["tmpd:0", "tmpd:1"], ["par:Bb"])
    TT("dve", t512a, Braw_r, cfi_b, ALU.mult, rawb + PK, ["tmpd:0"])
    TT("dve", t512b, Braw_i, cfr_b, ALU.mult, rawb + PK, ["tmpd:1"])
    TT("dve", Bbi, t512a, t512b, ALU.add, ["tmpd:0", "tmpd:1"], ["par:Bb"])
    TS("dve", sm["t1"], sm["th"], 32.0, None, ALU.mult, None, PK, PK)
    sincos(sm["t1"], sm["a32i"], sm["a32r"], sm["kf"], 32, PK, PK)
    ACT(sm["e32"], sm["rho"], AF.Exp, PK, PK, scale=32.0)
    TT("dve", sm["a32r"], sm["a32r"], sm["e32"], ALU.mult, PK, PK)
    TT("dve", sm["a32i"], sm["a32i"], sm["e32"], ALU.mult, PK, PK)
    TS("dve", sm["t2"], sm["a32i"], -1.0, None, ALU.mult, None, PK, PK)
    AAv = AAf.rearrange("p o t (g q) -> p o t g q", q=2)
    for (oo, tt_, src) in ((0, 0, "a32r"), (0, 1, "t2"), (1, 0, "a32i"), (1, 1, "a32r")):
        CP("dve", AAv[:, oo, tt_, :, :], sm[src].unsqueeze(2).broadcast_to([128, 32, 2]), PK, ["par:AAf"])
    TS("dve", sm["rhop"], sm["rho"], sig, None, ALU.mult, None, PK + ["cst:sig"], PK)
    TS("dve", sm["thp"], sm["th"], sig, None, ALU.mult, None, PK + ["cst:sig"], PK)
    sval_b = sval.unsqueeze(1).broadcast_to([128, 32, 32])
    GK = ["gen:all"]
    TT("dve", g_arg, sm["rhop"].unsqueeze(2).broadcast_to([128, 32, 32]), sval_b, ALU.mult, PK + ["cst:sval"], GK)
    TT("dve", g_phi, sm["thp"].unsqueeze(2).broadcast_to([128, 32, 32]), sval_b, ALU.mult, PK + ["cst:sval"], GK)
    fl = lambda a: a.rearrange("p g s -> p (g s)")
    sincos(fl(g_phi), fl(g_sn), fl(g_cs), fl(Pr), 1024, GK, GK + ["tab:all"])
    ACT(fl(g_phi), fl(g_arg), AF.Exp, GK, GK)
    ACT(fl(g_arg), fl(g_arg), AF.Exp, GK, GK, scale=-1.0)
    TT("dve", Pr, g_phi, g_cs, ALU.mult, GK, ["tab:all"])
    TT("dve", Pi, g_phi, g_sn, ALU.mult, GK, ["tab:all"])
    TT("dve", Qr, g_arg, g_cs, ALU.mult, GK, ["tab:all"])
    TT("dve", NQi, g_arg, g_sn, ALU.mult, GK, ["tab:all"])
    dump("Pr", Pr, [128, 32, 32], F32, ["tab:all"])
    dump("Bbr", Bbr, [128, 32, 16], F32, ["par:Bb"])
    for nm in ("are", "aim", "dt", "rho", "th", "er", "cs", "sn", "abr", "abi", "den", "cfr", "cfi"):
        dump("sm_" + nm, sm[nm], [128, 32], F32, PK + ["par:are", "par:aim"])
    dump("Braw", Braw_r, [128, 32, 16], F32, rawb)
    dump("Cr", Cr, [128, 32, 16], F32, ["par:Cr"])

    PHASE(5)
    _bias_split = len(_pending)
    DMA("sp", tab33[0:32, :], rel_tab, [], ["cst:tab33a"])
    S.add("dve", lambda e: e.memset(tab33[32:33, :], -30000.0), [], ["cst:tab33b"])
    ohs = AR.view("raw", RB_OFF + 10 * K, [33, 512], F32)
    DMA("sp", ohs, c_oh, [], ["raw:oh"])
    MM([(p7[0:8, :], tab33, ohs, True, True)], ["cst:tab33a", "cst:tab33b", "raw:oh", "par:Cr"], ["ps7"])
    CP("dve", fsb, p7[0:8, :], ["ps7"], ["cst:fsb"])
    DMA("sp", fd, fsb, ["cst:fsb"], ["fd:all"])
    hk = [AR.view("raw", RB_OFF + 12 * K + 512 * i, [128, 128], F32) for i in range(4)]
    for h in range(8):
        for jk in range(3):
            i = (h * 3 + jk) % 4
            src = bass.AP(fd_t, h * 512 + 128 * jk, [[1, 128], [1, 128]])
            DMA("sp", hk[i], src, ["fd:all"], ["raw:hk%d" % i])
            pb_i = 6 + (h * 3 + jk) % 2
            MM([(psf(pb_i)[:, 0:128], hk[i], anti, True, True)], ["raw:hk%d" % i, "cst:anti"], ["ps%d" % pb_i])
            CP("act", biasT[:, h // 4, jk, h % 4, :], psf(pb_i)[:, 0:128], ["ps%d" % pb_i], ["bias:all"])
    dump("bias", biasT, [128, 2, 3, 4, 128], BF16, ["bias:all"])

    _defer[0] = False
    PHASE(1)
    gain_b = gainT.unsqueeze(2).broadcast_to([128, 8, 128])

    p1_bank = {}

    def p1_front1(s):
        b = s % 2
        DMA("sp", xs[b], x_s[s], [], ["p1x:%d" % b])
        rms_front(xs[b], "p1x:%d" % b, hn[b], "p1h:%d" % b, b, ["p1x:%d" % b])
        bi = nb()
        p1_bank[("T", s)] = bi
        pT = psb(bi)
        TRS([(pT[:, 128 * k:128 * k + 128], hn[b][:, 128 * k:128 * k + 128], ident_b) for k in range(8)],
            ["p1h:%d" % b, "cst:ident_b"], ["ps%d" % bi])

    def p1_front2(s):
        b = s % 2
        bi = p1_bank[("T", s)]
        TT("dve", hTs[b], psb(bi).rearrange("p (k n) -> p k n", k=8), gain_b, ALU.mult,
           ["ps%d" % bi, "cst:gainT"], ["p1t:%d" % b])

    def p1_mm(s):
        b = s % 2
        bu = nb()
        MM([(psf(bu), hTs[b][:, k, :], Wu[:, k, :], k == 0, k == 7) for k in range(8)],
           ["p1t:%d" % b, "p1w:u"], ["ps%d" % bu])
        bk = nb()
        pkv = psf(bk)
        MM([(pkv[:, 0:128], Wk[:, k, :], hTs[b][:, k, :], k == 0, k == 7) for k in range(8)] +
           [(pkv[:, 128:256], Wv[:, k, :], hTs[b][:, k, :], k == 0, k == 7) for k in range(8)],
           ["p1t:%d" % b, "p1w:k", "p1w:v"], ["ps%d" % bk])
        p1_bank[("U", s)] = bu
        p1_bank[("KV", s)] = bk

    def p1_copy(s):
        bu, bk = p1_bank[("U", s)], p1_bank[("KV", s)]
        pkv = psf(bk)
        CP("act", U_tm[:, :, s, :], psf(bu).rearrange("p (g c) -> p g c", g=32), ["ps%d" % bu], ["utm:%d" % s])
        CP("act", kT_all[:, s::32], pkv[:, 0:128], ["ps%d" % bk], ["kT:all"])
        CP("act", vT_all[:, s::32], pkv[:, 128:256], ["ps%d" % bk], ["vT:all"])

    _bank_mod[0] = 8
    PREP_FIRST = True
    if PREP_FIRST:
        flush(_bias_split)
    p1_front1(0)
    p1_front1(1)
    p1_front2(0)
    for s in range(32):
        if s + 2 < 32:
            p1_front1(s + 2)
        if s + 1 < 32:
            p1_front2(s + 1)
        p1_mm(s)
        if s >= 1:
            p1_copy(s - 1)
        flush(0)
    p1_copy(31)
    flush()
    _bank_mod[0] = 8
    dump("utm", U_tm, [128, 32, 32, 16], BF16, ["utm:%d" % s for s in range(32)])
    dump("kT", kT_all, [128, 4096], BF16, ["kT:all"])

    PHASE(2)
    S.add("pool", lambda e: e.memset(v_all[:, :, :, 64:65], 1.0), [], ["vall:ones"])
    for j in range(4):
        bi = nb()
        pb_ = psb(bi)
        TRS([(pb_[:, 128 * i:128 * i + 128], vT_all[:, 128 * (8 * j + i):128 * (8 * j + i) + 128], ident_b)
             for i in range(8)], ["vT:all", "cst:ident_b"], ["ps%d" % bi])
        CP("act", v_all[:, 8 * j:8 * j + 8, :, 0:64],
           pb_.rearrange("p (i h d) -> p i h d", i=8, h=2), ["ps%d" % bi], ["vall:%d" % j])
    dump("vall", v_all, [128, 32, 2, 65], BF16, ["vall:%d" % j for j in range(4)] + ["vall:ones"])

    PHASE(3)
    U_g = AR.view("ug", RA_OFF, [128, 32, 4, 128], BF16)
    for g2 in range(16):
        bi = nb()
        pb_ = psb(bi)
        lst = []
        for gi in range(2):
            g = 2 * g2 + gi
            for k in range(4):
                lst.append((pb_[:, (gi * 4 + k) * 128:(gi * 4 + k) * 128 + 128],
                            U_tm[:, g, 8 * k:8 * k + 8, :].rearrange("p s c -> p (s c)"), ident_b))
        TRS(lst, ["utm:%d" % s for s in range(32)] + ["cst:ident_b"], ["ps%d" % bi])
        CP("act" if g2 % 2 == 0 else "dve", U_g[:, 2 * g2:2 * g2 + 2, :, :],
           pb_.rearrange("p (g k n) -> p g k n", g=2, k=4), ["ps%d" % bi], ["ug:%d" % g2])
    UG_ALL = ["ug:%d" % i for i in range(16)]

    def gen_L(g, b):
        Prg = Pr[:, g, :].unsqueeze(2).broadcast_to([128, 32, 16])
        Pig = Pi[:, g, :].unsqueeze(2).broadcast_to([128, 32, 16])
        Brg = Bbr[:, g, :].unsqueeze(1).broadcast_to([128, 32, 16])
        Big = Bbi[:, g, :].unsqueeze(1).broadcast_to([128, 32, 16])
        v = lambda t: t.rearrange("p (s c) -> p s c", s=32)
        rkeys = ["tab:all", "par:Bb"]
        TT("dve", v(tmpD[0]), Prg, Brg, ALU.mult, rkeys, ["tmpd:0"])
        TT("dve", v(tmpD[1]), Pig, Big, ALU.mult, rkeys, ["tmpd:1"])
        TT("pool", v(tmpP[1]), Pig, Brg, ALU.mult, rkeys, ["tmpp:1"])
        TT("dve", Lb[b][:, 0, :], tmpD[0], tmpD[1], ALU.subtract, ["tmpd:0", "tmpd:1"], ["lt:%dr" % b])
        TT("dve", v(tmpP[0]), Prg, Big, ALU.mult, rkeys, ["tmpp:0"])
        TT("pool", Lb[b][:, 1, :], tmpP[0], tmpP[1], ALU.add, ["tmpp:0", "tmpp:1"], ["lt:%di" % b])

    def gen_R(g, b):
        Qrg = Qr[:, g, :].unsqueeze(2).broadcast_to([128, 32, 16])
        NQg = NQi[:, g, :].unsqueeze(2).broadcast_to([128, 32, 16])
        Crg = Cr[:, g, :].unsqueeze(1).broadcast_to([128, 32, 16])
        Cig = Ci[:, g, :].unsqueeze(1).broadcast_to([128, 32, 16])
        v = lambda t: t.rearrange("p (s c) -> p s c", s=32)
        rkeys = ["tab:all", "par:Cr", "par:Ci"]
        TT("dve", v(tmpD[0]), Crg, Qrg, ALU.mult, rkeys, ["tmpd:0"])
        TT("dve", v(tmpD[1]), Cig, NQg, ALU.mult, rkeys, ["tmpd:1"])
        TT("pool", v(tmpP[0]), Crg, NQg, ALU.mult, rkeys, ["tmpp:0"])
        TT("dve", Rb[b][:, 0, :], tmpD[0], tmpD[1], ALU.add, ["tmpd:0", "tmpd:1"], ["rt:%dr" % b])
        TT("pool", v(tmpP[1]), Cig, Qrg, ALU.mult, rkeys, ["tmpp:1"])
        TT("pool", Rb[b][:, 1, :], tmpP[0], tmpP[1], ALU.subtract, ["tmpp:0", "tmpp:1"], ["rt:%di" % b])

    PHASE(6)
    Z = AR.view("z", RB_OFF, [128, 2, 32, 128], F32)
    for g in range(32):
        b = g % 2
        gen_L(g, b)
        bl = nb()
        pl_ = psb(bl)
        plv = pl_.rearrange("p (k r m) -> p k r m", k=4, r=2)
        TRS([(plv[:, k, ri, :], Lb[b][:, ri, 128 * k:128 * k + 128], ident_b) for k in range(4) for ri in range(2)],
            ["lt:%dr" % b, "lt:%di" % b, "cst:ident_b"], ["ps%d" % bl])
        CP("act", LTb[b].rearrange("p k r m -> p (k r m)"), pl_, ["ps%d" % bl], ["ltt:%d" % b])
        DMA("sp", lsc[g], Lb[b].rearrange("p r m -> p (r m)"), ["lt:%dr" % b, "lt:%di" % b], ["lsc:%d" % g])
        bz = nb()
        pz = psf(bz)
        MM([(pz[:, 128 * ri:128 * ri + 128], LTb[b][:, k, ri, :], U_g[:, g, k, :], k == 0, k == 3)
            for ri in range(2) for k in range(4)], ["ltt:%d" % b] + UG_ALL, ["ps%d" % bz])
        CP("act", Z[:, :, g, :], pz[:, 0:256].rearrange("p (r n) -> p r n", r=2), ["ps%d" % bz], ["z:%d" % g])
    Z_ALL = ["z:%d" % g for g in range(32)]
    dump("Z", Z, [128, 2, 32, 128], F32, Z_ALL)

    PHASE(7)
    Xd = AR.view("xd", GEN_OFF, [128, 2, 32, 128], BF16)
    Zv = Z.rearrange("p r g (q j) -> p r (g q) j", q=2)
    Xv = Xd.rearrange("p r g (q j) -> p r (g q) j", q=2)
    S.add("dve", lambda e: e.memset(Xv[0:64, :, :, 0:1], 0.0), [], ["xd:f"])
    S.add("pool", lambda e: e.memset(Xv[64:128, :, :, 63:64], 0.0), [], ["xd:b"])
    for step in range(1, 64):
        for (eng, lo, hi, cur, prev, tag) in (("dve", 0, 64, step, step - 1, "f"), ("pool", 64, 128, 63 - step, 64 - step, "b")):
            W_ = Wsc[lo:hi]
            sp_ = step % 2
            S_ = (Ssc if sp_ == 0 else Ssc2)[lo:hi]
            tag2 = tag + str(sp_)
            Xp = Zv[lo:hi, :, :, prev].unsqueeze(1).broadcast_to([64, 2, 2, 64])
            TT(eng, W_, AAf[lo:hi], Xp, ALU.mult, Z_ALL + ["par:AAf", "z:scan" + tag], ["scn:w" + tag])
            TT(eng, S_, W_[:, :, 0, :], W_[:, :, 1, :], ALU.add, ["scn:w" + tag], ["scn:s" + tag2])
            TT(eng, Zv[lo:hi, :, :, cur], Zv[lo:hi, :, :, cur], S_, ALU.add, ["scn:s" + tag2] + Z_ALL, ["z:scan" + tag])
            CP("act", Xv[lo:hi, :, :, cur], S_, ["scn:s" + tag2], ["xd:" + tag])
    dump("Xd", Xd, [128, 2, 32, 128], BF16, ["xd:f", "xd:b"])

    PHASE(8)
    yg = AR.view("yg", RC_OFF, [128, 32, 512], BF16)
    ident3 = ident_f.rearrange("p (t c) -> p t c", t=8)
    for g in range(32):
        b = g % 2
        DMA("sp", Lb[b].rearrange("p r m -> p (r m)"), lsc[g], ["lsc:%d" % g], ["lt:%dr" % b, "lt:%di" % b])
        gen_R(g, b)
        TT("dve", Dg[b].rearrange("p (t c) -> p t c", t=8), ident3,
           dB[:, 16 * g:16 * g + 16].unsqueeze(1).broadcast_to([128, 8, 16]), ALU.mult,
           ["cst:ident_f", "par:dB"], ["dg:%d" % b])
        lk = ["lt:%dr" % b, "lt:%di" % b, "rt:%dr" % b, "rt:%di" % b]
        pf_i, pb_i = nb(), nb()
        pf, pbk = psf(pf_i), psf(pb_i)
        MM([(pf, Lb[b][0:64, ri, 0:128], Rb[b][0:64, ri, :], ri == 0, ri == 1) for ri in range(2)] +
           [(pbk, Lb[b][64:128, ri, 384:512], Rb[b][64:128, ri, :], ri == 0, ri == 1) for ri in range(2)],
           lk, ["ps%d" % pf_i, "ps%d" % pb_i])
        TB = Tg[b].rearrange("p k m -> p (k m)")[:, 0:896].rearrange("p (m q) -> p m q", m=7)
        CP("act", TB[:, 4:7, :], pf[:, 128:512].rearrange("p (m q) -> p m q", m=3), ["ps%d" % pf_i], ["tg:%d_f" % b])
        CP("act", TB[:, 0:3, :], pbk[:, 0:384].rearrange("p (m q) -> p m q", m=3), ["ps%d" % pb_i], ["tg:%d_b" % b])
        TT("dve", bt1[0], pf[:, 0:128], mf, ALU.mult, ["ps%d" % pf_i, "cst:mf"], ["bt:1_0"])
        TT("dve", bt2[0], pbk[:, 384:512], mb, ALU.mult, ["ps%d" % pb_i, "cst:mb"], ["bt:2_0"])
        TT("dve", TB[:, 3, :], bt1[0], bt2[0], ALU.add, ["bt:1_0", "bt:2_0"], ["tg:%d_d" % b])
        py_i = nb()
        py = psf(py_i)
        MM([(py, Xd[:, ri, g, :], Rb[b][:, ri, :], ri == 0, False) for ri in range(2)] +
           [(py[:, 128 * j:128 * j + 128], U_g[:, g, k, :], TB[:, j - k + 3, :], False, False)
            for k in range(4) for j in range(4)] +
           [(py[:, 128 * k:128 * k + 128], U_g[:, g, k, :], Dg[b], False, k == 3) for k in range(4)],
           UG_ALL + ["tg:%d_f" % b, "tg:%d_b" % b, "tg:%d_d" % b, "dg:%d" % b, "xd:f", "xd:b", "rt:%dr" % b, "rt:%di" % b],
           ["ps%d" % py_i])
        ACT(yg[:, :, 16 * g:16 * g + 16], py.rearrange("p (t c) -> p t c", t=32), AF.Gelu_apprx_tanh,
            ["ps%d" % py_i], ["yg:%d" % g])
    YG_ALL = ["yg:%d" % g for g in range(32)]
    dump("tg", Tg[1], [128, 4, 512], BF16, ["tg:1_f", "tg:1_b", "tg:1_d"])
    dump("yg", yg, [128, 32, 512], BF16, YG_ALL)

    PHASE(9)
    ygT = AR.view("ygT", RB_OFF, [128, 4, 4096], BF16)
    ygT_v = ygT.rearrange("p c (n t) -> p c n t", t=32)
    cnt = 0
    for c in range(4):
        for tb in range(4):
            pi_ = nb()
            pb_ = psb(pi_)
            TRS([(pb_[:, 128 * i:128 * i + 128], yg[:, 8 * tb + i, 128 * c:128 * c + 128], ident_b) for i in range(8)],
                YG_ALL + ["cst:ident_b"], ["ps%d" % pi_])
            CP("act" if cnt % 2 == 0 else "dve", ygT_v[:, c, :, 8 * tb:8 * tb + 8],
               pb_.rearrange("p (i n) -> p n i", i=8), ["ps%d" % pi_], ["ygT:%d" % cnt])
            cnt += 1
    YGT_ALL = ["ygT:%d" % i for i in range(16)]
    dump("ygT", ygT, [128, 4, 4096], BF16, YGT_ALL)

    PHASE(10)
    attnT = AR.view("attnT", RA_OFF, [128, 4, 4096], BF16)
    o = RC_OFF
    def rc(prefix, shape, dt):
        nonlocal o
        esz = 2 if dt == BF16 else 4
        v = AR.view(prefix, o, shape, dt)
        o += (_prod(shape[1:]) * esz + 3) // 4 * 4
        return v
    Wq = rc("w2a", [128, 8, 512], BF16)
    Wza = rc("w2a", [128, 8, 512], BF16)
    xt2 = [rc("x2", [128, 1024], F32) for _ in range(2)]
    hn2 = [AR.view("h2", RD_OFF + 76 * K + 2 * K * i_, [128, 1024], BF16) for i_ in range(4)]
    hTb = [rc("hT2", [128, 8, 512], BF16) for _ in range(2)]
    qT = rc("qT", [128, 8, 512], BF16)
    zas = rc("zas", [128, 4, 512], BF16)
    PT = [rc("pt", [128, 3, 512], BF16) for _ in range(2)]
    den = [rc("den", [128, 8], F32) for _ in range(2)]
    at = [rc("at", [128, 256], F32) for _ in range(2)]
    ag = [rc("ag", [128, 512], BF16) for _ in range(2)]
    assert o <= RD_OFF + 32 * K, o
    A2_END = o
    Wq_p = Wq.rearrange("p k (g h d) -> p k g h d", g=4, h=2)
    for h_ in range(2):
        for g_ in range(4):
            c0_ = 256 * h_ + 64 * g_
            DMA("pool", Wq_p[:, :, g_, h_, :], w_in_v[:, :, c0_:c0_ + 64], [], ["w2a:q"])
    DMA("pool", Wza, w_in_v[:, :, 768:1280], [], ["w2a:za"])
    S.add("dve", lambda e: e.memset(qT, 0.0), [], ["qT:zero"] + ["qT:%d" % h_ for h_ in range(8)])

    def front(tok0, ntile, hT_dst, hkey, xt_l, hn_l, xpre, hpre, psbase, cnt0):
        for i in range(ntile):
            bb = (cnt0 + i) % 2
            DMA("sp", xt_l[bb], x[tok0 + 128 * i: tok0 + 128 * i + 128, :], [], ["%s:%d" % (xpre, bb)])
            rms_front(xt_l[bb], None, hn_l[bb], "%s:%d" % (hpre, bb), 2 + bb, ["%s:%d" % (xpre, bb)])
            pi_ = nb()
            pT_ = psb(pi_)
            TRS([(pT_[:, 128 * k:128 * k + 128], hn_l[bb][:, 128 * k:128 * k + 128], ident_b) for k in range(8)],
                ["%s:%d" % (hpre, bb), "cst:ident_b"], ["ps%d" % pi_])
            TT("dve", hT_dst[:, :, 128 * i:128 * i + 128], pT_.rearrange("p (k n) -> p k n", k=8), gain_b, ALU.mult,
               ["ps%d" % pi_, "cst:gainT"], [hkey + "_%d" % i])

    ucount = [0]
    pend_tr = []
    unit_hooks = {}

    def attention_block(jb):
        bb = jb % 2
        hT_ = hTb[bb]
        hkeys = ["hT2:%d_%d" % (bb, i) for i in range(4)]
        for hq in range(4):
            bi = nb()
            MM([(psf(bi), Wq[:, k, 128 * hq:128 * hq + 128], hT_[:, k, :], k == 0, k == 7) for k in range(8)],
               hkeys + ["w2a:q"], ["ps%d" % bi])
            TS("dve", qT[0:64, hq, :], psf(bi)[0:64, :], 0.125, None, ALU.mult, None, ["ps%d" % bi, "qT:zero"], ["qT:%d" % hq])
            TS("dve", qT[64:128, 4 + hq, :], psf(bi)[64:128, :], 0.125, None, ALU.mult, None, ["ps%d" % bi, "qT:zero"],
               ["qT:%d" % (4 + hq)])
        for tt in range(4):
            bi = nb()
            MM([(psf(bi), hT_[:, k, 128 * tt:128 * tt + 128], Wza[:, k, :], k == 0, k == 7) for k in range(8)],
               hkeys + ["w2a:za"], ["ps%d" % bi])
            ACT(zas[:, tt, :], psf(bi), AF.Silu, ["ps%d" % bi], ["zas:%d" % tt])

        def scores(tt, kvh):
            qi = 4 * jb + tt
            bi_ = qi % 16
            jks = [jk for jk in range(3) if 0 <= bi_ + jk - 1 <= 15]
            u = ucount[0]
            ucount[0] += 1
            pb2 = u % 2
            for jk in jks:
                kt = qi + jk - 1
                sb = nb()
                lst = [(psf(sb), ident_b, biasT[:, kvh, jk, :, :].rearrange("p g q -> p (g q)"), True, False)]
                for g in range(4):
                    h = 4 * kvh + g
                    lst.append((psf(sb)[:, 128 * g:128 * g + 128], kT_all[:, 128 * kt:128 * kt + 128],
                                qT[:, h, 128 * tt:128 * tt + 128], False, g == 3))
                MM(lst, ["kT:all", "bias:all", "cst:ident_b"] + ["qT:%d" % (4 * kvh + g) for g in range(4)], ["ps%d" % sb])
                ACT(PT[pb2][:, jk, :], psf(sb), AF.Exp, ["ps%d" % sb], ["pt:%d_%d" % (pb2, jk)])
            return (tt, kvh, qi, jks, pb2)

        def bias_mults(st):
            (tt, kvh, qi, jks, pb2) = st
            for jk in jks:
                ptv = PT[pb2][:, jk, :].rearrange("p (g q) -> p g q", g=4)
                TT("dve", ptv, ptv, biasT[:, 4 * kvh:4 * kvh + 4, jk, :], ALU.mult,
                   ["pt:%d_%d" % (pb2, jk), "bias:all"], ["pt:%d_%d" % (pb2, jk)])

        def pv_post(st):
            (tt, kvh, qi, jks, pb2) = st
            ab = qi % 2
            pvb = nb()
            pv = psf(pvb)[:, 0:260].rearrange("p (g d) -> p g d", g=4)
            lst = []
            for g in range(4):
                for jk in jks:
                    kt = qi + jk - 1
                    lst.append((pv[:, g, :], PT[pb2][:, jk, 128 * g:128 * g + 128], v_all[:, kt, kvh, :],
                                jk == jks[0], jk == jks[-1]))
            MM(lst, ["pt:%d_%d" % (pb2, jk) for jk in jks] + ["vall:ones"] + ["vall:%d" % j for j in range(4)], ["ps%d" % pvb])
            dn = den[pb2]
            TT("dve", dn[:, 0:4], pv[:, :, 64], esink[:, 4 * kvh:4 * kvh + 4], ALU.add, ["ps%d" % pvb, "cst:esink"], ["den:%d" % pb2])
            S.add("dve", lambda e, dn=dn: e.reciprocal(out=dn[:, 4:8], in_=dn[:, 0:4]), ["den:%d" % pb2], ["den:%d" % pb2])
            atv = at[pb2].rearrange("p (g d) -> p g d", g=4)
            TT("dve", atv, pv[:, :, 0:64], dn[:, 4:8].unsqueeze(2).broadcast_to([128, 4, 64]), ALU.mult,
               ["ps%d" % pvb, "den:%d" % pb2], ["at:%d" % pb2])
            TT("dve", ag[ab][:, 256 * kvh:256 * kvh + 256], at[pb2], zas[:, tt, 256 * kvh:256 * kvh + 256], ALU.mult,
               ["at:%d" % pb2, "zas:%d" % tt], ["ag:%d_%d" % (ab, kvh)])
            if kvh == 1:
                pend_tr.append((ab, qi))

        def do_tr(n):
            for _ in range(n):
                (ab, qi) = pend_tr.pop(0)
                pt_i = nb()
                TRS([(psb(pt_i)[:, 128 * c:128 * c + 128], ag[ab][:, 128 * c:128 * c + 128], ident_b) for c in range(4)],
                    ["ag:%d_0" % ab, "ag:%d_1" % ab, "cst:ident_b"], ["ps%d" % pt_i])
                CP("dve", attnT[:, :, 128 * qi:128 * qi + 128], psb(pt_i)[:, 0:512].rearrange("p (c n) -> p c n", c=4),
                   ["ps%d" % pt_i], ["attnT:%d" % qi])

        units = [(tt, kvh) for tt in range(4) for kvh in range(2)]
        pend = [scores(*units[0])]
        for i in range(len(units)):
            if i + 1 < len(units):
                pend.append(scores(*units[i + 1]))
            had = list(pend_tr)
            pv_post(pend.pop(0))
            if had:
                do_tr(len(had))
            for fn_ in unit_hooks.get(i, ()):
                fn_()

    def f2_rms(jb):
        for i in range(4):
            DMA("sp", xt2[i % 2], x[512 * jb + 128 * i: 512 * jb + 128 * i + 128, :], [], ["x2:%d" % (i % 2)])
            rms_front(xt2[i % 2], None, hn2[i], "h2:%d" % i, 2 + i % 2, ["x2:%d" % (i % 2)])

    def f2_T(jb):
        for i in range(4):
            pi_ = nb()
            pT_ = psb(pi_)
            TRS([(pT_[:, 128 * k:128 * k + 128], hn2[i][:, 128 * k:128 * k + 128], ident_b) for k in range(8)],
                ["h2:%d" % i, "cst:ident_b"], ["ps%d" % pi_])
            TT("dve", hTb[jb % 2][:, :, 128 * i:128 * i + 128], pT_.rearrange("p (k n) -> p k n", k=8), gain_b, ALU.mult,
               ["ps%d" % pi_, "cst:gainT"], ["hT2:%d_%d" % (jb % 2, i)])

    f2_rms(0)
    f2_T(0)
    f2_rms(1)
    for jb in range(8):
        unit_hooks.clear()
        if jb + 1 < 8:
            f2_T(jb + 1)
        if jb + 2 < 8:
            unit_hooks[3] = [lambda nj=jb + 2: f2_rms(nj)]
        attention_block(jb)
    if pend_tr:
        _jb_last = 7
        (ab, qi) = pend_tr.pop(0)
        pt_i = nb()
        TRS([(psb(pt_i)[:, 128 * c:128 * c + 128], ag[ab][:, 128 * c:128 * c + 128], ident_b) for c in range(4)],
            ["ag:%d_0" % ab, "ag:%d_1" % ab, "cst:ident_b"], ["ps%d" % pt_i])
        CP("dve", attnT[:, :, 128 * qi:128 * qi + 128], psb(pt_i)[:, 0:512].rearrange("p (c n) -> p c n", c=4),
           ["ps%d" % pt_i], ["attnT:%d" % qi])
    ATT_ALL = ["attnT:%d" % i for i in range(32)]
    dump("attnT", attnT, [128, 4, 4096], BF16, ATT_ALL)

    PHASE(11)
    o = RD_OFF + 32 * K
    Wg = rc("w2b", [128, 8, 2048], BF16)
    Wzs = rc("w2b", [128, 8, 512], BF16)
    Wglu = rc("w2b", [128, 4, 512], BF16)
    assert o <= AR.nbytes, o
    o = CST_END
    Wo = rc("w2c", [128, 8, 1024], BF16)
    fgB = rc("w2c", [128, 1024], F32)
    assert o <= RA_OFF
    o = RC_OFF
    Wba = rc("w2d", [128, 4, 1024], BF16)
    Wbs = rc("w2d", [128, 4, 1024], BF16)
    xt3 = [rc("x3", [128, 1024], F32) for _ in range(2)]
    hn3 = [rc("h3", [128, 1024], BF16) for _ in range(2)]
    hTc = [rc("hT3", [128, 8, 256], BF16) for _ in range(2)]
    zss = rc("zss", [128, 4, 256], F32)
    sg = rc("sg", [128, 4, 256], F32)
    ssmT = rc("ssmT", [128, 4, 256], BF16)
    sga = [rc("sga", [128, 256], F32) for _ in range(2)]
    sgs = [rc("sgs", [128, 256], F32) for _ in range(2)]
    m1 = [rc("m1", [128, 256], F32) for _ in range(2)]
    m2 = [rc("m2", [128, 256], F32) for _ in range(2)]
    mT = [rc("mT", [128, 8, 256], BF16) for _ in range(2)]
    junk3 = rc("junk3", [128, 1024], BF16)
    assert o <= RD_OFF + 32 * K, o
    o = RD_OFF + 32 * K + 44 * K
    xr = [rc("xr", [128, 1024], F32) for _ in range(2)]
    assert o <= AR.nbytes, o
    for c4 in range(4):
        DMA("pool", Wg[:, :, 512 * c4:512 * c4 + 512], w_in_v[:, :, 2304 + 512 * c4:2304 + 512 * c4 + 512], [], ["w2b:g%d" % c4])
    DMA("pool", Wzs, w_in_v[:, :, 1792:2304], [], ["w2b:zs"])
    DMA("pool", Wglu, w_glu.rearrange("(k p) e -> p k e", p=128), [], ["w2b:glu"])
    DMA("pool", Wo, w_out.rearrange("(k p) e -> p k e", p=128), [], ["w2c:o"])
    DMA("sp", fgB, fgain[0:1, :].partition_broadcast(128), [], ["w2c:fg"])
    DMA("pool", Wba, w_ba.rearrange("(k p) e -> p k e", p=128), [], ["w2d:ba"])
    DMA("pool", Wbs, w_bs.rearrange("(k p) e -> p k e", p=128), [], ["w2d:bs"])
    WG_ALL = ["w2b:g%d" % c for c in range(4)]
    ocnt = [0]

    def merge_p1(jb):
        bb = jb % 2
        hT_ = hTc[bb]
        hkeys = ["hT3:%d_%d" % (bb, i) for i in range(2)]
        tok0 = 256 * jb
        for c in range(4):
            bi = nb()
            MM([(psf(bi)[:, 0:256], Wzs[:, k, 128 * c:128 * c + 128], hT_[:, k, :], k == 0, k == 7) for k in range(8)],
               hkeys + ["w2b:zs"], ["ps%d" % bi])
            ACT(zss[:, c, :], psf(bi)[:, 0:256], AF.Silu, ["ps%d" % bi], ["zss:%d" % c])
        for c in range(4):
            bi = nb()
            MM([(psf(bi)[:, 0:256], Wglu[:, kc, 128 * c:128 * c + 128], ygT[:, kc, tok0:tok0 + 256], kc == 0, kc == 3)
                for kc in range(4)], YGT_ALL + ["w2b:glu"], ["ps%d" % bi])
            ACT(sg[:, c, :], psf(bi)[:, 0:256], AF.Sigmoid, ["ps%d" % bi, "cst:bgluT"], ["sg:%d" % c], bias=bgluT[:, c:c + 1])
        TT("pool", sg, sg, ygT[:, :, tok0:tok0 + 256], ALU.mult, ["sg:%d" % c for c in range(4)] + YGT_ALL, ["sg:all"])
        TT("pool", ssmT, sg, zss, ALU.mult, ["sg:all"] + ["zss:%d" % c for c in range(4)], ["ssmT:all"])

    def merge_p2(jb, hooks=None):
        bb = jb % 2
        hT_ = hTc[bb]
        hkeys = ["hT3:%d_%d" % (bb, i) for i in range(2)]
        tok0 = 256 * jb
        for e8 in range(8):
            for fn_ in (hooks or {}).get(e8, ()):
                fn_()
            eb = e8 % 2
            b_ga, b_gs, b_ba, b_bs = nb(), nb(), nb(), nb()
            MM([(psf(b_ga)[:, 0:256], Wg[:, k, 128 * e8:128 * e8 + 128], hT_[:, k, :], k == 0, k == 7) for k in range(8)],
               hkeys + WG_ALL, ["ps%d" % b_ga])
            ACT(sga[eb], psf(b_ga)[:, 0:256], AF.Sigmoid, ["ps%d" % b_ga, "cst:bgT"], ["sga:%d" % eb], bias=bgT[:, e8:e8 + 1])
            MM([(psf(b_gs)[:, 0:256], Wg[:, k, 1024 + 128 * e8:1024 + 128 * e8 + 128], hT_[:, k, :], k == 0, k == 7) for k in range(8)],
               hkeys + WG_ALL, ["ps%d" % b_gs])
            ACT(sgs[eb], psf(b_gs)[:, 0:256], AF.Sigmoid, ["ps%d" % b_gs, "cst:bgT"], ["sgs:%d" % eb], bias=bgT[:, 8 + e8:9 + e8])
            MM([(psf(b_ba)[:, 0:256], Wba[:, kc, 128 * e8:128 * e8 + 128], attnT[:, kc, tok0:tok0 + 256], kc == 0, kc == 3)
                for kc in range(4)], ATT_ALL + ["w2d:ba"], ["ps%d" % b_ba])
            MM([(psf(b_bs)[:, 0:256], Wbs[:, kc, 128 * e8:128 * e8 + 128], ssmT[:, kc, :], kc == 0, kc == 3)
                for kc in range(4)], ["ssmT:all", "w2d:bs"], ["ps%d" % b_bs])
            TT("dve", m1[eb], psf(b_ba)[:, 0:256], sga[eb], ALU.mult, ["ps%d" % b_ba, "sga:%d" % eb], ["m1:%d" % eb])
            TT("dve", m2[eb], psf(b_bs)[:, 0:256], sgs[eb], ALU.mult, ["ps%d" % b_bs, "sgs:%d" % eb], ["m2:%d" % eb])
            TT("dve", mT[bb][:, e8, :], m1[eb], m2[eb], ALU.add, ["m1:%d" % eb, "m2:%d" % eb], ["mT:%d_%d" % (bb, e8)])

    out_pend = []

    def merge_out(jb):
        bb = jb % 2
        tok0 = 256 * jb
        mkeys = ["mT:%d_%d" % (bb, e8) for e8 in range(8)]
        for tt in range(2):
            ob = ocnt[0] % 2
            ocnt[0] += 1
            t0 = tok0 + 128 * tt
            out_pend.append((jb, tt, ob, t0))
            DMA("sp", xr[ob], x[t0:t0 + 128, :], [], ["xr:%d" % ob])
            for half in range(2):
                bo = nb()
                MM([(psf(bo), mT[bb][:, e8, 128 * tt:128 * tt + 128], Wo[:, e8, 512 * half:512 * half + 512], e8 == 0, e8 == 7)
                    for e8 in range(8)], mkeys + ["w2c:o"], ["ps%d" % bo])
                TT("dve", xr[ob][:, 512 * half:512 * half + 512], psf(bo), xr[ob][:, 512 * half:512 * half + 512], ALU.add,
                   ["ps%d" % bo, "xr:%d" % ob], ["xr:%d" % ob])

    def merge_fin():
        while out_pend:
            (jb, tt, ob, t0) = out_pend.pop(0)
            col = 4 + ob
            ssc = ss_t[:, col:col + 1]
            ACT(junk3, xr[ob], AF.Square, ["xr:%d" % ob], ["junk3:0", "cst:ss%d" % col], accum_out=ssc)
            ACT(ssc, ssc, AF.Sqrt, ["cst:ss%d" % col, "cst:eps"], ["cst:ss%d" % col], scale=1.0 / 1024.0, bias=epst)
            S.add("dve", lambda e, ssc=ssc: e.reciprocal(out=ssc, in_=ssc), ["cst:ss%d" % col], ["cst:ss%d" % col])
            STT(xr[ob], xr[ob], ssc, fgB, ALU.mult, ALU.mult, ["xr:%d" % ob, "cst:ss%d" % col, "w2c:fg"], ["xr:%d" % ob])
            DMA("pool", out[t0:t0 + 128, :], xr[ob], ["xr:%d" % ob], ["out:%d" % (2 * jb + tt)])

    def front3_rms(jb):
        for i in range(2):
            DMA("sp", xt3[i], x[256 * jb + 128 * i: 256 * jb + 128 * i + 128, :], [], ["x3:%d" % i])
            rms_front(xt3[i], None, hn3[i], "h3:%d" % i, 2 + i, ["x3:%d" % i])

    def front3_T(jb):
        bb = jb % 2
        for i in range(2):
            pi_ = nb()
            pT_ = psb(pi_)
            TRS([(pT_[:, 128 * k:128 * k + 128], hn3[i][:, 128 * k:128 * k + 128], ident_b) for k in range(8)],
                ["h3:%d" % i, "cst:ident_b"], ["ps%d" % pi_])
            TT("dve", hTc[bb][:, :, 128 * i:128 * i + 128], pT_.rearrange("p (k n) -> p k n", k=8), gain_b, ALU.mult,
               ["ps%d" % pi_, "cst:gainT"], ["hT3:%d_%d" % (bb, i)])

    front3_rms(0)
    front3_T(0)
    front3_rms(1)
    for jb in range(16):
        if jb + 1 < 16:
            front3_T(jb + 1)
        merge_p1(jb)
        if jb >= 1:
            merge_out(jb - 1)
        hk_ = {2: [merge_fin]}
        if jb + 2 < 16:
            hk_[5] = [lambda nj=jb + 2: front3_rms(nj)]
        merge_p2(jb, hk_)
    merge_out(15)
    merge_fin()
    PHASE(0)
    S.add("sp", lambda e: e.nop(), ["out:%d" % i for i in range(32)] + ["dbgout:" + n for n in dbg_out], [])
    S.emit()
    return nc, dbg_out


def _t5_bucket_np(rel):
    half = 16
    ret = (rel > 0).astype(np.int64) * half
    n = np.abs(rel)
    max_exact = half // 2
    nf = np.maximum(n, 1).astype(np.float32)
    large = max_exact + (np.log(nf / np.float32(max_exact)) / np.float32(math.log(128 / max_exact))
                         * np.float32(half - max_exact)).astype(np.int32)
    large = np.minimum(large, half - 1)
    return ret + np.where(n < max_exact, n, large)


def _t5_bucket_table(rel):
    try:
        import jax
        import jax.numpy as jnp
        cpu = jax.devices("cpu")[0]
        with jax.default_device(cpu):
            r = jnp.asarray(rel, dtype=jnp.int32)
            half = 16
            ret = (r > 0).astype(jnp.int32) * half
            n = jnp.abs(r)
            max_exact = half // 2
            nf = jnp.maximum(n, 1).astype(jnp.float32)
            large = max_exact + (jnp.log(nf / max_exact) / math.log(128 / max_exact)
                                 * (half - max_exact)).astype(jnp.int32)
            large = jnp.minimum(large, half - 1)
            return np.asarray(ret + jnp.where(n < max_exact, n, large)).astype(np.int64)
    except Exception:
        return _t5_bucket_np(rel)


def _host_constants():
    c = {}
    c["c_ident"] = np.eye(128, dtype=np.float32)
    c["c_anti"] = np.ascontiguousarray(np.eye(128, dtype=np.float32)[::-1])
    c["c_sval"] = np.broadcast_to(np.arange(32, dtype=np.float32), (128, 32)).copy()
    sg = np.ones((128, 1), np.float32)
    sg[:64] = -1.0
    c["c_sig"] = sg
    s_idx = np.arange(128)[:, None] // 16
    t_idx = np.arange(128)[None, :] // 16
    c["c_mf"] = (t_idx >= s_idx).astype(np.float32)
    c["c_mb"] = (s_idx >= t_idx).astype(np.float32)
    oh = np.zeros((33, 512), np.float32)
    r = np.arange(511)
    rel = r - 255
    bk = _t5_bucket_table(rel)
    oh[bk, r] = 1.0
    oh[32, r] = (np.abs(rel) > 128).astype(np.float32)
    oh[32, 511] = 1.0
    c["c_oh"] = oh
    return c


_CACHE = {}


def _get_program():
    if "nc" not in _CACHE:
        _CACHE["nc"] = build_program()[0]
    return _CACHE["nc"]


def make_in_maps(inputs):
    f = lambda a: np.ascontiguousarray(np.asarray(a, dtype=np.float32))
    shared = {
        "norm_gain": f(inputs["norm_gain"]).reshape(1, 1024),
        "w_in": f(inputs["w_in"]).reshape(1024, 4352),
        "b_gate": f(inputs["b_gate"]).reshape(1, 2048),
        "attn_sink": f(inputs["attn_sink"]).reshape(1, 8),
        "rel_bias_table": f(inputs["rel_bias_table"]).reshape(32, 8),
        "ssm_a_re": f(inputs["ssm_a_re"]).reshape(2, 32, 64),
        "ssm_a_im": f(inputs["ssm_a_im"]).reshape(2, 32, 64),
        "ssm_log_dt": f(inputs["ssm_log_dt"]).reshape(2, 32),
        "ssm_b_re": f(inputs["ssm_b_re"]).reshape(2, 32, 64, 16),
        "ssm_b_im": f(inputs["ssm_b_im"]).reshape(2, 32, 64, 16),
        "ssm_c_re": f(inputs["ssm_c_re"]).reshape(2, 32, 16, 64),
        "ssm_c_im": f(inputs["ssm_c_im"]).reshape(2, 32, 16, 64),
        "ssm_d": f(inputs["ssm_d"]).reshape(1, 512),
        "w_glu": f(inputs["w_glu"]).reshape(512, 512),
        "b_glu": f(inputs["b_glu"]).reshape(1, 512),
        "w_branch_attn": f(inputs["w_branch_attn"]).reshape(512, 1024),
        "w_branch_ssm": f(inputs["w_branch_ssm"]).reshape(512, 1024),
        "w_out": f(inputs["w_out"]).reshape(1024, 1024),
        "final_norm_gain": f(inputs["final_norm_gain"]).reshape(1, 1024),
    }
    shared.update(_host_constants())
    xs = f(inputs["x"])
    maps = []
    for c in range(8):
        m = dict(shared)
        m["x"] = np.ascontiguousarray(xs[2 * c:2 * c + 2].reshape(4096, 1024))
        maps.append(m)
    return maps


def kernel(**inputs):
    nc = _get_program()
    in_maps = make_in_maps(inputs)
    res = run_bass_kernel_spmd(nc, in_maps, core_ids=list(range(8)))
    outs = [np.asarray(r["out"]).reshape(2, 2048, 1024) for r in res.results]
    return np.concatenate(outs, axis=0).astype(np.float32)
```
